# Optimizing a Trainium2 kernel written in Bass

```python
import math
import jax, jax.numpy as jnp
from jax import lax
import numpy as np

D_MODEL = 1024
BATCH = 8
SEQ = 2048
DEPTH = 2
DEC_BATCH = 128
DEC_SEQ = 1
PAST_LEN = 2048
PAGE_SIZE = 128

DIFF_HEADS = 4
DIFF_DK = D_MODEL // 16
DIFF_DV = 2 * DIFF_DK
GLA_HEADS = 4
GLA_DV = (D_MODEL - DIFF_HEADS * DIFF_DV) // GLA_HEADS
GLA_DK = GLA_DV // 2
GLA_GATE_RANK = 16
GLA_GATE_NORM = 16.0
GLA_CHUNK = 64
Q_BLOCK = 128
ROPE_THETA = 10000.0
D_FF = -(-8 * D_MODEL // (3 * 256)) * 256
EPS = 1e-6
COL_SIZES = (DIFF_HEADS * 2 * DIFF_DK, DIFF_HEADS * 2 * DIFF_DK, DIFF_HEADS * DIFF_DV,
             GLA_HEADS * GLA_DK, GLA_HEADS * GLA_DK, GLA_HEADS * GLA_DV, GLA_HEADS * GLA_DV,
             GLA_GATE_RANK)
D_IN = sum(COL_SIZES)
MIX_WIDTH = DIFF_HEADS * DIFF_DV + GLA_HEADS * GLA_DV

kernel_name = 'hybrid_diffattn_gla_decode_step'


def rmsnorm(x, g):
    xf = x.astype(jnp.float32)
    y = xf * lax.rsqrt(jnp.mean(xf * xf, axis=-1, keepdims=True) + EPS)
    return (y * g.astype(jnp.float32)).astype(x.dtype)


def rope(x, pos):
    half = x.shape[-1] // 2
    freqs = ROPE_THETA ** (-jnp.arange(half, dtype=jnp.float32) / half)
    ang = pos.astype(jnp.float32)[:, None] * freqs[None, :]
    shape = (1, pos.shape[0]) + (1,) * (x.ndim - 3) + (half,)
    cos = jnp.cos(ang).reshape(shape)
    sin = jnp.sin(ang).reshape(shape)
    xf = x.astype(jnp.float32)
    x1, x2 = xf[..., :half], xf[..., half:]
    return jnp.concatenate([x1 * cos - x2 * sin, x2 * cos + x1 * sin], axis=-1).astype(x.dtype)


def project(n, w_in_l, q_norm_l, k_norm_l, w_a2_l, b_a_l, pos):
    B, T, _ = n.shape
    z = n @ w_in_l
    cuts = np.cumsum(COL_SIZES)[:-1].tolist()
    dq, dk, dv, gq, gk, gv, gg, gr = jnp.split(z, cuts, axis=-1)
    q = rope(rmsnorm(dq.reshape(B, T, DIFF_HEADS, 2, DIFF_DK), q_norm_l), pos)
    k = rope(rmsnorm(dk.reshape(B, T, DIFF_HEADS, 2, DIFF_DK), k_norm_l), pos)
    v = dv.reshape(B, T, DIFF_HEADS, DIFF_DV)
    gq = gq.reshape(B, T, GLA_HEADS, GLA_DK) * (GLA_DK ** -0.5)
    gk = gk.reshape(B, T, GLA_HEADS, GLA_DK)
    gv = gv.reshape(B, T, GLA_HEADS, GLA_DV)
    glog = jax.nn.log_sigmoid((gr @ w_a2_l + b_a_l).astype(jnp.float32)) / GLA_GATE_NORM
    glog = glog.reshape(B, T, GLA_HEADS, GLA_DK)
    return q, k, v, gq, gk, gv, glog, gg


def diff_lambda(lqk_l, lam_init):
    lf = lqk_l.astype(jnp.float32)
    return jnp.exp(jnp.sum(lf[0] * lf[1])) - jnp.exp(jnp.sum(lf[2] * lf[3])) + lam_init


def diff_attn_prompt(q, k, v, lam):
    B, S = q.shape[:2]
    nb = S // Q_BLOCK
    scale = DIFF_DK ** -0.5
    qb = q.reshape(B, nb, Q_BLOCK, DIFF_HEADS, 2, DIFF_DK).transpose(1, 0, 2, 3, 4, 5)
    kpos = jnp.arange(S)

    def one_block(args):
        qi, i = args
        s = jnp.einsum('bqhmd,bkhmd->bhmqk', qi, k).astype(jnp.float32) * scale
        qpos = i * Q_BLOCK + jnp.arange(Q_BLOCK)
        mask = kpos[None, :] <= qpos[:, None]
        p = jax.nn.softmax(jnp.where(mask, s, -jnp.inf), axis=-1)
        w = p[:, :, 0] - lam * p[:, :, 1]
        return jnp.einsum('bhqk,bkhv->bqhv', w.astype(v.dtype), v)

    o = lax.map(one_block, (qb, jnp.arange(nb)))
    return o.transpose(1, 0, 2, 3, 4).reshape(B, S, DIFF_HEADS, DIFF_DV)


def diff_attn_sample(q, k_new, v_new, k_past, v_past, lam):
    T = q.shape[1]
    P = k_past.shape[1]
    scale = DIFF_DK ** -0.5
    s_past = jnp.einsum('bqhmd,bkhmd->bhmqk', q, k_past).astype(jnp.float32) * scale
    s_new = jnp.einsum('bqhmd,bkhmd->bhmqk', q, k_new).astype(jnp.float32) * scale
    causal = jnp.tril(jnp.ones((T, T), dtype=bool))
    s_new = jnp.where(causal, s_new, -jnp.inf)
    p = jax.nn.softmax(jnp.concatenate([s_past, s_new], axis=-1), axis=-1)
    w = (p[:, :, 0] - lam * p[:, :, 1]).astype(v_new.dtype)
    return (jnp.einsum('bhqk,bkhv->bqhv', w[..., :P], v_past)
            + jnp.einsum('bhqk,bkhv->bqhv', w[..., P:], v_new))


def gla_chunk(S0, q, k, v, g):
    L = q.shape[1]
    qf, kf, vf = q.astype(jnp.float32), k.astype(jnp.float32), v.astype(jnp.float32)
    b = jnp.cumsum(g.astype(jnp.float32), axis=1)
    o_inter = jnp.einsum('blhk,bhkv->blhv', qf * jnp.exp(b), S0)
    mask = jnp.tril(jnp.ones((L, L), dtype=bool))[None, :, :, None, None]
    decay = jnp.exp(jnp.where(mask, b[:, :, None] - b[:, None, :], -jnp.inf))
    A = jnp.einsum('bthk,bshk,btshk->bths', qf, kf, decay)
    o = o_inter + jnp.einsum('bths,bshv->bthv', A, vf)
    b_last = b[:, -1]
    S_new = (jnp.exp(b_last)[..., None] * S0
             + jnp.einsum('bshk,bshv->bhkv', kf * jnp.exp(b_last[:, None] - b), vf))
    return o, S_new


def gla_prompt(q, k, v, g):
    B, S = q.shape[:2]
    nc = S // GLA_CHUNK

    def to_chunks(a):
        return a.reshape((B, nc, GLA_CHUNK) + a.shape[2:]).transpose(1, 0, 2, 3, 4)

    def step(state, xs):
        o, state = gla_chunk(state, *xs)
        return state, o

    S0 = jnp.zeros((B, GLA_HEADS, GLA_DK, GLA_DV), jnp.float32)
    S_fin, o = lax.scan(step, S0, (to_chunks(q), to_chunks(k), to_chunks(v), to_chunks(g)))
    o = o.transpose(1, 0, 2, 3, 4).reshape(B, S, GLA_HEADS, GLA_DV)
    return o, S_fin


def merge(o_diff, o_gla, gate, subln_l, gla_norm_l, lam_init, w_out_l):
    B, T = o_diff.shape[:2]
    od = rmsnorm(o_diff, subln_l) * (1.0 - lam_init)
    og = rmsnorm(o_gla, gla_norm_l).reshape(B, T, GLA_HEADS * GLA_DV) * jax.nn.silu(gate)
    o = jnp.concatenate([od.reshape(B, T, DIFF_HEADS * DIFF_DV), og], axis=-1)
    return o @ w_out_l


def ffn(h, norm2_l, w_gu_l, w_down_l):
    n = rmsnorm(h, norm2_l)
    a, b = jnp.split(n @ w_gu_l, 2, axis=-1)
    return (jax.nn.silu(a) * b) @ w_down_l


def setup_inputs(seed: int = 0) -> dict:
    key = jax.random.key(seed)
    ks = jax.random.split(key, 20)
    n_pages = PAST_LEN // PAGE_SIZE
    n_used = DEC_BATCH * n_pages
    n_phys = n_used + max(1, n_used // 4)
    page_table = jax.random.permutation(ks[0], n_phys)[:n_used].reshape(DEC_BATCH, n_pages).astype(jnp.int32)
    f32 = jnp.float32
    nrm = lambda k, s: jax.random.normal(k, s, f32)
    gain = lambda k, s: 1.0 + 0.02 * nrm(k, s)
    return {
        'x_prompt': nrm(ks[1], (BATCH, SEQ, D_MODEL)),
        'x_sample': nrm(ks[2], (DEC_BATCH, DEC_SEQ, D_MODEL)),
        'cache_k': nrm(ks[3], (DEPTH, n_phys, PAGE_SIZE, DIFF_HEADS, 2, DIFF_DK)),
        'cache_v': nrm(ks[4], (DEPTH, n_phys, PAGE_SIZE, DIFF_HEADS, DIFF_DV)),
        'state_gla': nrm(ks[5], (DEPTH, DEC_BATCH, GLA_HEADS, GLA_DK, GLA_DV)),
        'page_table': page_table,
        'norm1': gain(ks[6], (DEPTH, D_MODEL)),
        'w_in': nrm(ks[7], (DEPTH, D_MODEL, D_IN)) * D_MODEL ** -0.5,
        'q_norm': gain(ks[8], (DEPTH, DIFF_DK)),
        'k_norm': gain(ks[9], (DEPTH, DIFF_DK)),
        'lambda_qk': 0.1 * nrm(ks[10], (DEPTH, 4, DIFF_DK)),
        'subln': gain(ks[11], (DEPTH, DIFF_DV)),
        'w_a2': nrm(ks[12], (DEPTH, GLA_GATE_RANK, GLA_HEADS * GLA_DK)) * GLA_GATE_RANK ** -0.5,
        'b_a': 0.1 * nrm(ks[13], (DEPTH, GLA_HEADS * GLA_DK)),
        'gla_norm': gain(ks[14], (DEPTH, GLA_DV)),
        'w_out': nrm(ks[15], (DEPTH, MIX_WIDTH, D_MODEL)) * MIX_WIDTH ** -0.5,
        'norm2': gain(ks[16], (DEPTH, D_MODEL)),
        'w_gu': nrm(ks[17], (DEPTH, D_MODEL, 2 * D_FF)) * D_MODEL ** -0.5,
        'w_down': nrm(ks[18], (DEPTH, D_FF, D_MODEL)) * D_FF ** -0.5,
    }


def reference(x_prompt, x_sample, cache_k, cache_v, state_gla, page_table, norm1, w_in, q_norm, k_norm,
              lambda_qk, subln, w_a2, b_a, gla_norm, w_out, norm2, w_gu, w_down):
    yp, ys = x_prompt, x_sample
    S, T = yp.shape[1], ys.shape[1]
    DB = ys.shape[0]
    pos_p = jnp.arange(S)
    pos_s = PAST_LEN + jnp.arange(T)
    kp, vp, sp, k_s, v_s, s_s = [], [], [], [], [], []
    for l in range(DEPTH):
        lam_init = 0.8 - 0.6 * math.exp(-0.3 * l)
        lam = diff_lambda(lambda_qk[l], lam_init)
        n = rmsnorm(yp, norm1[l])
        q, k, v, gq, gk, gv, glog, gg = project(n, w_in[l], q_norm[l], k_norm[l], w_a2[l], b_a[l], pos_p)
        od = diff_attn_prompt(q, k, v, lam)
        og, s_fin = gla_prompt(gq, gk, gv, glog)
        yp = yp + merge(od, og.astype(yp.dtype), gg, subln[l], gla_norm[l], lam_init, w_out[l])
        yp = yp + ffn(yp, norm2[l], w_gu[l], w_down[l])
        kp.append(k)
        vp.append(v)
        sp.append(s_fin.astype(state_gla.dtype))
        n = rmsnorm(ys, norm1[l])
        q, k, v, gq, gk, gv, glog, gg = project(n, w_in[l], q_norm[l], k_norm[l], w_a2[l], b_a[l], pos_s)
        k_past = cache_k[l, page_table].reshape(DB, -1, DIFF_HEADS, 2, DIFF_DK)
        v_past = cache_v[l, page_table].reshape(DB, -1, DIFF_HEADS, DIFF_DV)
        od = diff_attn_sample(q, k, v, k_past, v_past, lam)
        og, s_new = gla_chunk(state_gla[l].astype(jnp.float32), gq, gk, gv, glog)
        ys = ys + merge(od, og.astype(ys.dtype), gg, subln[l], gla_norm[l], lam_init, w_out[l])
        ys = ys + ffn(ys, norm2[l], w_gu[l], w_down[l])
        k_s.append(k)
        v_s.append(v)
        s_s.append(s_new.astype(state_gla.dtype))
    return (yp, ys, jnp.stack(kp), jnp.stack(vp), jnp.stack(sp), jnp.stack(k_s), jnp.stack(v_s), jnp.stack(s_s))
```

```python
import contextlib
import math
import numpy as np
import concourse.bass as bass
import concourse.mybir as mb
from concourse.bass_utils import run_bass_kernel_spmd

F32 = mb.dt.float32
BF = mb.dt.bfloat16
I32 = mb.dt.int32
AF = mb.ActivationFunctionType
ALU = mb.AluOpType
AX = mb.AxisListType

D = 1024
DIN = 3088
DFF = 2816
NPHYS = 2560
EPS = 1e-6
ENGS = ("pe", "act", "dve", "pool", "sp")


class Op:
    __slots__ = ("eng", "fn", "dma", "deps", "seq", "signal", "sigidx", "dsem", "dval",
                 "waits", "gidx", "slotwait", "cc")

    def __init__(self, eng, fn, dma):
        self.eng = eng
        self.fn = fn
        self.dma = dma
        self.deps = set()
        self.signal = False
        self.sigidx = 0
        self.dsem = None
        self.dval = 0
        self.waits = []
        self.slotwait = None
        self.cc = False


class Prog:
    def __init__(self, nc):
        self.nc = nc
        self.ops = []
        self.eops = {e: [] for e in ENGS}
        self.last_writer = {}
        self.readers = {}
        self.ndma = {"sp": 8, "pool": 8, "act": 4}
        self.out_dmas = []

    def op(self, eng, fn, reads=(), writes=(), dma=False, is_output=False):
        if len(self.ops) >= _OPLIMIT and not is_output:
            return None
        o = Op(eng, fn, dma)
        deps = set()
        pr = [r for r in reads if r.startswith("PS") or r.startswith("PB")]
        if pr:
            writes = list(writes) + [r for r in pr if r not in writes]
        for r in reads:
            w = self.last_writer.get(r)
            if w is not None:
                deps.add(w)
        for r in writes:
            w = self.last_writer.get(r)
            if w is not None:
                deps.add(w)
            for rd in self.readers.get(r, ()):
                deps.add(rd)
        for r in reads:
            self.readers.setdefault(r, []).append(o)
        for r in writes:
            self.last_writer[r] = o
            self.readers[r] = []
        deps.discard(o)
        o.deps = deps
        o.gidx = len(self.ops)
        o.seq = len(self.eops[eng])
        self.ops.append(o)
        self.eops[eng].append(o)
        if is_output:
            self.out_dmas.append(o)
        return o

    def pe(self, fn, reads=(), writes=()):
        return self.op("pe", fn, reads, writes)

    def act(self, fn, reads=(), writes=()):
        return self.op("act", fn, reads, writes)

    def dve(self, fn, reads=(), writes=()):
        return self.op("dve", fn, reads, writes)

    def pool(self, fn, reads=(), writes=()):
        return self.op("pool", fn, reads, writes)

    def dma(self, q, fn, reads=(), writes=(), is_output=False):
        return self.op(q, fn, reads, writes, dma=True, is_output=is_output)

    def ccop(self, fn, reads=(), writes=()):
        o = self.op("pool", fn, reads, writes, dma=True)
        if o is not None:
            o.cc = True
        return o

    def finalize(self, sems):
        fin = Op("sp", None, False)
        fin.deps = set(self.out_dmas)
        fin.gidx = len(self.ops)
        fin.seq = len(self.eops["sp"])
        self.ops.append(fin)
        self.eops["sp"].append(fin)
        cnt = {}
        lastval = {}
        ncc = 0
        for o in self.ops:
            if o.dma and o.cc:
                o.dsem = "cc%d" % ncc
                ncc += 1
                o.dval = 1
                continue
            if o.dma:
                q = o.eng
                i = cnt.get(q, 0)
                cnt[q] = i + 1
                key = "d_%s_%d" % (q, i % self.ndma[q])
                o.dsem = key
                prev = lastval.get(key, 0)
                o.dval = prev + 16
                lastval[key] = o.dval
                if prev > 0:
                    o.slotwait = (key, prev)
        known = {e: {p: -1 for p in ENGS} for e in ENGS}
        known_dma = {e: {} for e in ENGS}
        for o in self.ops:
            e = o.eng
            best = {}
            for d in o.deps:
                if d.dma:
                    k = known_dma[e].get(d.dsem, 0)
                    if d.dval > k:
                        cur = best.get(("dma", d.dsem))
                        if cur is None or d.dval > cur.dval:
                            best[("dma", d.dsem)] = d
                else:
                    if d.seq > known[e][d.eng]:
                        cur = best.get(("eng", d.eng))
                        if cur is None or d.seq > cur.seq:
                            best[("eng", d.eng)] = d
            if o.slotwait is not None:
                key, prev = o.slotwait
                if known_dma[e].get(key, 0) >= prev:
                    o.slotwait = None
                else:
                    known_dma[e][key] = prev
            for k, d in best.items():
                if k[0] == "dma":
                    known_dma[e][d.dsem] = d.dval
                else:
                    known[e][d.eng] = d.seq
                    d.signal = True
            o.waits = [best[k] for k in sorted(best.keys())]
        for e in ENGS:
            n = 0
            for o in self.eops[e]:
                if o.signal and not o.dma:
                    n += 1
                    o.sigidx = n
        self.sems = sems

    def emit(self, eng, e):
        sems = self.sems
        for o in self.eops[eng]:
            if o.slotwait is not None:
                e.wait_ge(sems[o.slotwait[0]], o.slotwait[1])
            for d in o.waits:
                if d.dma:
                    e.wait_ge(sems[d.dsem], d.dval)
                else:
                    e.wait_ge(sems[d.eng], d.sigidx)
            if o.fn is None:
                continue
            ins = o.fn(e)
            if o.dma and o.cc:
                ins.then_inc(sems[o.dsem])
            elif o.dma:
                ins.then_inc(sems[o.dsem], 16)
            elif o.signal:
                ins.then_inc(sems[o.eng], 1)


class T:
    def __init__(self, h, F):
        self.h = h
        self.F = F

    def v(self, col=0, dims=None, p0=0, np_=128):
        return bass.AP(self.h, p0 * self.F + col, [[self.F, np_]] + [list(d) for d in dims])

    def c(self, a, b, p0=0, np_=128):
        return self.v(a, [[1, b - a]], p0, np_)


def build(nc, S, use_cc=True, NPHYS=2560):
    NTP = S // 128
    NT = NTP + 1
    ST = NTP
    es = contextlib.ExitStack()
    P = Prog(nc)

    def din(name, shape, dt=F32):
        return nc.dram_tensor(name, list(shape), dt, kind="ExternalInput")

    def dout(name, shape, dt=F32):
        return nc.dram_tensor(name, list(shape), dt, kind="ExternalOutput")

    xp = din("xp", [S, D])
    xs = din("xs", [128, D])
    w_in = din("w_in", [2, D, DIN])
    w_out = din("w_out", [2, D, D])
    w_gu = din("w_gu", [2, D, 2 * DFF])
    w_down = din("w_down", [2, DFF, D])
    norm1 = din("norm1", [2, D])
    norm2 = din("norm2", [2, D])
    qn = din("qn", [2, 64])
    kn = din("kn", [2, 64])
    lqk = din("lqk", [2, 256])
    subln = din("subln", [2, 128])
    wa2 = din("wa2", [2, 16, 256])
    ba = din("ba", [2, 256])
    gnorm = din("gnorm", [2, 128])
    kc = din("kc", [8 * NPHYS * 8, 2048])
    vc = din("vc", [8 * NPHYS * 8, 2048])
    sg = din("sg", [2, 16, 4, 64, 128])
    ptrep = din("ptrep", [128, 16], I32)
    rgcol = din("rgcol", [128, 1])
    cosP = din("cosP", [S, 32])
    sinP = din("sinP", [S, 32])
    cosS = din("cosS", [128, 32])
    sinS = din("sinS", [128, 32])
    identd = din("identd", [128, 128])
    trid = din("trid", [128, 128])
    i16d = din("i16d", [128, 256])

    yp = dout("yp", [S, D])
    ys = dout("ys", [16, D])
    kp = dout("kp", [2, S, 512])
    vp = dout("vp", [2, S, 512])
    gp = dout("gp", [2, 4, 64, 128])
    ks = dout("ks", [2, 16, 512])
    vs = dout("vs", [2, 16, 512])
    gs = dout("gs", [2, 16, 4, 64, 128])

    ag1_in = nc.dram_tensor("ag1_in", [16, 1536], F32)
    ag2_in = nc.dram_tensor("ag2_in", [64, 128], F32)
    vscr = nc.dram_tensor("vscr", [16, 512], F32)

    def sb(name, F, dt):
        return T(es.enter_context(nc.sbuf_tensor(name, [128, F], dt)), F)

    def psb(name, F, dt):
        return T(es.enter_context(nc.psum_tensor(name, [128, F], dt)), F)

    X = sb("X", NT * D, F32)
    ARC = 49664
    AR = sb("AR", ARC, BF)
    CH = 2048

    def arres(off, n):
        return ["A%d" % i for i in range(off // CH, (off + n - 1) // CH + 1)]

    WIN_O, WOUT_O, KT_O, VA_O = 0, 24704, 32896, 41088
    WIN_R = arres(WIN_O, 24704)
    WOUT_R = arres(WOUT_O, 8192)
    GT = 640
    WD_O, HT_O, WGU_O, N2T_O = 0, 22528, 36864, 43008
    ARF = AR.h.bitcast(F32)
    ARFt = T(ARF, ARC // 2)

    def arfres(offf, n):
        return arres(offf * 2, n * 2)

    ident_bf = sb("ident_bf", 128, BF)
    identf = sb("identf", 128, F32)
    triN = sb("triN", 128, F32)
    onesN = sb("onesN", 128, F32)
    onesf = sb("onesf", 128, F32)
    tri4 = sb("tri4", 512, BF)
    i16 = sb("i16", 256, F32)
    g1T = sb("g1T", 16, F32)
    g2T = sb("g2T", 16, F32)
    Gq = sb("Gq", 128, F32)
    Gk = sb("Gk", 128, F32)
    Gsub = sb("Gsub", 256, F32)
    Ggn = sb("Ggn", 256, F32)
    lamc = sb("lamc", 8, F32)
    wa2a = sb("wa2a", 512, BF)
    CS = sb("CS", 64, F32)
    CSS = sb("CSS", 64, F32)
    grT = sb("grT", 128, BF)
    stat = sb("stat", 64, F32)
    F0 = sb("F0", 512, F32)
    F1 = sb("F1", 512, F32)
    F2 = sb("F2", 512, F32)
    F5 = sb("F5", 512, F32)
    LQ = F5
    F6 = sb("F6", 512, F32)
    F7 = sb("F7", 512, F32)
    B0 = sb("B0", 1024, BF)
    B1 = sb("B1", 1024, BF)
    B2 = sb("B2", 512, BF)
    B3 = sb("B3", 512, BF)
    B5 = sb("B5", 1024, BF)
    B7 = sb("B7", 256, BF)
    B8 = sb("B8", 512, BF)
    B9 = sb("B9", 1024, BF)
    B10 = sb("B10", 512, BF)
    B11 = sb("B11", 512, BF)
    PT = sb("PT", 1024, BF)
    Sst = sb("Sst", 512, F32)
    Sbf = sb("Sbf", 512, BF)
    sc32 = sb("sc32", 32, F32)
    pbuf = sb("pbuf", 32, F32)
    idxK = sb("idxK", 128, I32)
    idxKf = sb("idxKf", 128, F32)
    ptf = sb("ptf", 16, F32)
    pti = sb("pti", 16, I32)
    PS = [psb("ps%d" % i, 512, F32) for i in range(6)]
    PB = [psb("pb%d" % i, 1024, BF) for i in range(2)]

    names = ["pe", "act", "dve", "pool"] + ["d_sp_%d" % i for i in range(8)] + \
        ["d_pool_%d" % i for i in range(8)] + ["d_act_%d" % i for i in range(4)] + ["cc%d" % i for i in range(4)]
    sems = {n: es.enter_context(nc.semaphore(n)) for n in names}

    def dmain(q, out, in_, writes, reads=()):
        P.dma(q, lambda e: e.dma_start(out=out, in_=in_, allow_slow_non_contiguous=True), reads=reads, writes=writes)

    def dmaout(q, out, in_, reads):
        P.dma(q, lambda e: e.dma_start(out=out, in_=in_, allow_slow_non_contiguous=True), reads=reads, is_output=True)

    def tt(eng, out, in0, in1, op, reads, writes):
        P.op(eng, lambda e: e.tensor_tensor(out=out, in0=in0, in1=in1, op=op), reads, writes)

    def ts(eng, out, in0, s1, s2, op0, op1, reads, writes):
        if op1 is None:
            P.op(eng, lambda e: e.tensor_scalar(out=out, in0=in0, scalar1=s1, scalar2=None, op0=op0), reads, writes)
        else:
            P.op(eng, lambda e: e.tensor_scalar(out=out, in0=in0, scalar1=s1, scalar2=s2, op0=op0, op1=op1), reads, writes)

    def stt(eng, out, in0, sc, in1, op0, op1, reads, writes):
        P.op(eng, lambda e: e.scalar_tensor_tensor(out=out, in0=in0, scalar=sc, in1=in1, op0=op0, op1=op1), reads, writes)

    def red(eng, out, in_, reads, writes):
        P.op(eng, lambda e: e.tensor_reduce(out=out, in_=in_, axis=AX.X, op=ALU.add), reads, writes)

    def actf(out, in_, func, reads, writes, scale=1.0, bias=0.0, accum=None):
        if accum is None:
            P.act(lambda e: e.activation(out=out, in_=in_, func=func, bias=bias, scale=scale), reads, writes)
        else:
            P.act(lambda e: e.activation(out=out, in_=in_, func=func, bias=bias, scale=scale, accum_out=accum), reads, writes)

    def cp(eng, out, in_, reads, writes):
        if eng == "act":
            P.act(lambda e: e.copy(out=out, in_=in_), reads, writes)
        else:
            P.op(eng, lambda e: e.tensor_copy(out=out, in_=in_), reads, writes)

    def mms(lst, reads, writes):
        def fn(e):
            ins = None
            for it in lst:
                (o, l, r, st, sp_) = it[:5]
                if len(it) > 5:
                    ins = e.matmul(o, l, r, start=st, stop=sp_, skip_group_check=True)
                else:
                    ins = e.matmul(o, l, r, start=st, stop=sp_)
            return ins
        P.pe(fn, reads, writes)

    def trs(lst, reads, writes):
        def fn(e):
            ins = None
            for (o, i, idn) in lst:
                ins = e.transpose(o, i, idn)
            return ins
        P.pe(fn, reads, writes)

    def rstd_from_ss(ssap, n, cols, name):
        actf(ssap, ssap, AF.Ln, [name], [name], scale=1.0 / n, bias=EPS)
        actf(ssap, ssap, AF.Exp, [name], [name], scale=-0.5)

    dmain("sp", identf.c(0, 128), identd.ap(), ["identf"])
    dmain("pool", ident_bf.c(0, 128), identd.ap(), ["ident_bf"])
    dmain("sp", triN.c(0, 128), trid.ap(), ["triN"])
    for r in range(4):
        dmain("pool", tri4.c(r * 128, r * 128 + 128), trid.ap(), ["tri4"])
    dmain("sp", i16.c(0, 256), i16d.ap(), ["i16"])
    P.pool(lambda e: e.memset(onesN.c(0, 128), -1.0 / 16), writes=["onesN"])
    P.pool(lambda e: e.memset(onesf.c(0, 128), 1.0), writes=["onesf"])
    P.pool(lambda e: e.memset(grT.c(0, 128), 1.0), writes=["grT"])
    P.pool(lambda e: e.memset(B5.c(0, 1024), 0.0), writes=["B5"])
    ts("dve", triN.c(0, 128), triN.c(0, 128), -1.0 / 16, None, ALU.mult, None, ["triN"], ["triN"])
    dmain("sp", g1T.v(0, [[8, 2], [1, 8]]), norm1.ap().rearrange("l (k p) -> p l k", p=128), ["g1T"])
    dmain("sp", g2T.v(0, [[8, 2], [1, 8]]), norm2.ap().rearrange("l (k p) -> p l k", p=128), ["g2T"])

    def bc(dr, n):
        return bass.AP(dr, 0, [[0, 128], [1, 2 * n]])
    dmain("sp", Gq.c(0, 128), bc(qn, 64), ["Gq"])
    dmain("sp", Gk.c(0, 128), bc(kn, 64), ["Gk"])
    dmain("sp", Gsub.c(0, 256), bc(subln, 128), ["Gsub"])
    dmain("sp", Ggn.c(0, 256), bc(gnorm, 128), ["Ggn"])
    dmain("sp", LQ.c(0, 512), bc(lqk, 256), ["F5a", "F5b"])
    dmain("sp", CSS.c(0, 32), cosS.ap(), ["CSS"])
    dmain("sp", CSS.c(32, 64), sinS.ap(), ["CSS"])
    dmain("sp", pti.c(0, 16), ptrep.ap(), ["pti"])
    dmain("sp", stat.c(60, 61), rgcol.ap(), ["rg"])
    dmain("pool", wa2a.v(0, [[256, 2], [1, 256]], 0, 16), wa2.ap().rearrange("l r c -> r l c"), ["wa2a"])
    dmain("pool", wa2a.v(0, [[1, 512]], 16, 1), bass.AP(ba, 0, [[0, 1], [1, 512]]), ["wa2a"])
    for l in range(2):
        li = 0.8 - 0.6 * math.exp(-0.3 * l)
        ts("dve", Gsub.c(l * 128, l * 128 + 128), Gsub.c(l * 128, l * 128 + 128), 1.0 - li, None, ALU.mult, None, ["Gsub"], ["Gsub"])
        b0 = l * 256
        tt("dve", F6.c(0, 64), LQ.c(b0, b0 + 64), LQ.c(b0 + 64, b0 + 128), ALU.mult, ["F5a", "F5b"], ["F6"])
        tt("dve", F6.c(64, 128), LQ.c(b0 + 128, b0 + 192), LQ.c(b0 + 192, b0 + 256), ALU.mult, ["F5a", "F5b", "F6"], ["F6"])
        red("dve", stat.c(40, 42), F6.v(0, [[64, 2], [1, 64]]), ["F6"], ["lamtmp"])
        actf(stat.c(40, 42), stat.c(40, 42), AF.Exp, ["lamtmp"], ["lamtmp"])
        tt("dve", stat.c(42, 43), stat.c(41, 42), stat.c(40, 41), ALU.subtract, ["lamtmp"], ["lamtmp2"])
        ts("dve", lamc.c(l, l + 1), stat.c(42, 43), -li, None, ALU.add, None, ["lamtmp2"], ["lamc"])
    cp("dve", ptf.c(0, 16), pti.c(0, 16), ["pti"], ["ptf"])
    ts("dve", ptf.c(0, 16), ptf.c(0, 16), 8.0, stat.c(60, 61), ALU.mult, ALU.add, ["ptf", "rg"], ["ptf"])
    for l_ in range(2):
        for h_ in range(4):
            ts("dve", idxKf.v(l_ * 64 + h_, [[4, 16]]), ptf.c(0, 16), float((l_ * 4 + h_) * NPHYS * 8), None, ALU.add, None, ["ptf", "idxKf"], ["idxKf"])
    cp("dve", idxK.c(0, 128), idxKf.c(0, 128), ["idxKf"], ["idxK"])

    for t in range(NTP):
        dmain("sp", X.c(t * D, t * D + D), xp.ap()[t * 128:(t + 1) * 128, :], ["X%d" % t])
    dmain("sp", X.c(ST * D, ST * D + D), xs.ap(), ["X%d" % ST])

    def bcast_mid(Tn, col, n_g, n_d):
        return Tn.v(col, [[0, n_g], [1, n_d]])

    for l in range(_NL if _STOP >= 1 else 0):
        for kc_ in range(8):
            dmain("pool", AR.c(WIN_O + kc_ * DIN, WIN_O + (kc_ + 1) * DIN), w_in.ap()[l, kc_ * 128:(kc_ + 1) * 128, :], arres(WIN_O + kc_ * DIN, DIN))
        dmain("pool", AR.v(WOUT_O, [[D, 8], [1, D]]), w_out.ap()[l].rearrange("(k p) c -> p k c", p=128), WOUT_R)
        P.pool(lambda e: e.memset(AR.c(VA_O, VA_O + NTP * 528), 1.0), writes=arres(VA_O, NTP * 528))
        P.pool(lambda e: e.memset(Sst.c(0, 512), 0.0), writes=["Sst"])
        P.pool(lambda e: e.memset(Sbf.c(0, 512), 0.0), writes=["Sbf"])

        def win(kc_, c0, c1):
            return AR.c(WIN_O + kc_ * DIN + c0, WIN_O + kc_ * DIN + c1)

        def tile_norm_T(t, gT, dstT_ap_fn, dst_res):
            xr = "X%d" % t
            xa = X.c(t * D, t * D + D)
            actf(B0.c(0, 1024), xa, AF.Square, [xr], ["B0", "ssn"], accum=stat.c(0, 1))
            rstd_from_ss(stat.c(0, 1), D, 1, "ssn")
            ts("dve", B0.c(0, 1024), xa, stat.c(0, 1), None, ALU.mult, None, [xr, "ssn"], ["B0"])
            trs([(PB[0].c(k * 128, k * 128 + 128), B0.c(k * 128, k * 128 + 128), ident_bf.c(0, 128)) for k in range(8)],
                ["B0", "ident_bf"], ["PB0"])
            tt("dve", dstT_ap_fn(), PB[0].v(0, [[128, 8], [1, 128]]), gT.v(l * 8, [[1, 8], [0, 128]]), ALU.mult,
               ["PB0", "g1T", "g2T"], dst_res)

        def qk_norm_rope(Fx, fx, Gx, cs, csn):
            tt("pool", F6.c(0, 512), Fx.c(0, 512), Fx.c(0, 512), ALU.mult, [fx], ["F6"])
            red("dve", stat.c(8, 16), F6.v(0, [[64, 8], [1, 64]]), ["F6"], ["ss8"])
            rstd_from_ss(stat.c(8, 16), 64, 8, "ss8")
            tt("dve", Fx.v(0, [[64, 8], [1, 64]]), Fx.v(0, [[64, 8], [1, 64]]), stat.v(8, [[1, 8], [0, 64]]), ALU.mult, [fx, "ss8"], [fx])
            tt("dve", Fx.v(0, [[64, 8], [1, 64]]), Fx.v(0, [[64, 8], [1, 64]]), bcast_mid(Gx, l * 64, 8, 64), ALU.mult, [fx, "Gq", "Gk"], [fx])
            tt("pool", F6.v(0, [[32, 16], [1, 32]]), Fx.v(0, [[32, 16], [1, 32]]), bcast_mid(cs, 0, 16, 32), ALU.mult, [fx, csn], ["F6"])
            tt("dve", F7.v(0, [[64, 8], [1, 32]]), Fx.v(32, [[64, 8], [1, 32]]), bcast_mid(cs, 32, 8, 32), ALU.mult, [fx, csn], ["F7a"])
            tt("dve", F7.v(32, [[64, 8], [1, 32]]), Fx.v(0, [[64, 8], [1, 32]]), bcast_mid(cs, 32, 8, 32), ALU.mult, [fx, csn], ["F7b"])
            tt("dve", Fx.v(0, [[64, 8], [1, 32]]), F6.v(0, [[64, 8], [1, 32]]), F7.v(0, [[64, 8], [1, 32]]), ALU.subtract, ["F6", "F7a", "F7b"], [fx])
            tt("dve", Fx.v(32, [[64, 8], [1, 32]]), F6.v(32, [[64, 8], [1, 32]]), F7.v(32, [[64, 8], [1, 32]]), ALU.add, ["F6", "F7a", "F7b", fx], [fx])

        def proj_block(c0, n, ps):
            mms([(ps.c(0, n), B1.c(k * 128, k * 128 + 128), win(k, c0, c0 + n), k == 0, k == 7) for k in range(8)],
                ["B1"] + WIN_R, [ps_name(ps)])

        def ps_name(ps):
            for i, p_ in enumerate(PS):
                if p_ is ps:
                    return "PS%d" % i
            return "PB"

        def tile_A(t, cs, csn, is_sample, emit_out=True):
            tile_norm_T(t, g1T, lambda: B1.v(0, [[128, 8], [1, 128]]), ["B1"])
            proj_block(0, 512, PS[0])
            cp("act", F0.c(0, 512), PS[0].c(0, 512), ["PS0"], ["F0"])
            proj_block(512, 512, PS[1])
            cp("act", F1.c(0, 512), PS[1].c(0, 512), ["PS1"], ["F1"])
            proj_block(1024, 512, PS[0])
            cp("act", F2.c(0, 512), PS[0].c(0, 512), ["PS0"], ["F2"])
            if not is_sample:
                cp("dve", AR.v(VA_O + t * 528, [[132, 4], [1, 128]]), PS[0].v(0, [[128, 4], [1, 128]]), ["PS0"], arres(VA_O + t * 528, 528))
                dmaout("sp", vp.ap()[l, t * 128:(t + 1) * 128, :], F2.c(0, 512), ["F2"])
            elif emit_out:
                dmaout("sp", vs.ap()[l], F2.c(0, 512, 0, 16), ["F2"])
            proj_block(1536, 512, PS[1])
            cp("act", B8.c(0, 512), PS[1].c(0, 512), ["PS1"], ["B8"])
            proj_block(2048, 512, PS[0])
            cp("act", B2.c(0, 512), PS[0].c(0, 512), ["PS0"], ["B2"])
            if is_sample:
                cp("dve", F7.c(0, 512, 0, 16), PS[0].c(0, 512, 0, 16), ["PS0"], ["F7a", "F7b"])
                dmain("sp", bass.AP(vscr, 0, [[512, 16], [1, 512]]), F7.c(0, 512, 0, 16), ["vscr"], reads=["F7a", "F7b"])
            proj_block(2560, 512, PS[1])
            actf(B11.c(0, 512), PS[1].c(0, 512), AF.Silu, ["PS1"], ["B11"])
            mms([(PS[0].c(0, 128, 0, 16), win(k, 3072, 3088), B1.c(k * 128, k * 128 + 128), k == 0, k == 7) for k in range(8)],
                ["B1"] + WIN_R, ["PS0"])
            cp("act", grT.c(0, 128, 0, 16), PS[0].c(0, 128, 0, 16), ["PS0"], ["grT"])
            mms([(PS[1].c(0, 256), grT.c(0, 128, 0, 17), wa2a.c(l * 256, l * 256 + 256, 0, 17), True, True)], ["grT", "wa2a"], ["PS1"])
            actf(F5.c(0, 256), PS[1].c(0, 256), AF.Exp, ["PS1"], ["F5a"], scale=-1.0)
            actf(F5.c(0, 256), F5.c(0, 256), AF.Ln, ["F5a"], ["F5a"], bias=1.0)
            qk_norm_rope(F0, "F0", Gq, cs, csn)
            qk_norm_rope(F1, "F1", Gk, cs, csn)
            if not is_sample:
                dmaout("sp", kp.ap()[l, t * 128:(t + 1) * 128, :], F1.c(0, 512), ["F1"])
                cp("pool", B3.c(0, 512), F1.c(0, 512), ["F1"], ["B3"])
                trs([(PB[1].c(h * 128, h * 128 + 128), B3.c(h * 128, h * 128 + 128), ident_bf.c(0, 128)) for h in range(4)],
                    ["B3", "ident_bf"], ["PB1"])
                cp("act", AR.v(KT_O + t * 128, [[S, 4], [1, 128]]), PB[1].v(0, [[128, 4], [1, 128]]), ["PB1"], arres(KT_O, 4 * S))
                cp("pool", B3.c(0, 512), F0.c(0, 512), ["F0"], ["B3"])
                trs([(PB[1].c(h * 128, h * 128 + 128), B3.c(h * 128, h * 128 + 128), ident_bf.c(0, 128)) for h in range(4)],
                    ["B3", "ident_bf"], ["PB1"])
                cp("act", B5.v(0, [[256, 4], [1, 128]], 0, 64), PB[1].v(0, [[128, 4], [1, 128]], 0, 64), ["PB1"], ["B5"])
                cp("act", B5.v(128, [[256, 4], [1, 128]], 64, 64), PB[1].v(0, [[128, 4], [1, 128]], 64, 64), ["PB1"], ["B5"])
            elif emit_out:
                dmaout("sp", ks.ap()[l], F1.c(0, 512, 0, 16), ["F1"])

        def merge_head_norm(src_ps, Gx, goff, dst_col, extra_mul, nrows=128, srcres=("PS3",)):
            r = nrows
            cp("act", F6.c(0, 512, 0, r), src_ps, list(srcres) + ["F6"], ["F6"])
            tt("pool", F7.c(0, 512, 0, r), F6.c(0, 512, 0, r), F6.c(0, 512, 0, r), ALU.mult, ["F6"], ["F7a", "F7b"])
            red("dve", stat.c(16, 20, 0, r), F7.v(0, [[128, 4], [1, 128]], 0, r), ["F7a", "F7b"], ["ss4"])
            rstd_from_ss(stat.c(16, 20, 0, r), 128, 4, "ss4")
            tt("dve", F6.v(0, [[128, 4], [1, 128]], 0, r), F6.v(0, [[128, 4], [1, 128]], 0, r), stat.v(16, [[1, 4], [0, 128]], 0, r), ALU.mult, ["F6", "ss4"], ["F6"])
            if extra_mul is None:
                tt("dve", B0.v(dst_col, [[128, 4], [1, 128]], 0, r), F6.v(0, [[128, 4], [1, 128]], 0, r), Gx.v(goff, [[0, 4], [1, 128]], 0, r), ALU.mult, ["F6", "Gsub", "Ggn"], ["B0"])
            else:
                tt("dve", F6.v(0, [[128, 4], [1, 128]], 0, r), F6.v(0, [[128, 4], [1, 128]], 0, r), Gx.v(goff, [[0, 4], [1, 128]], 0, r), ALU.mult, ["F6", "Gsub", "Ggn"], ["F6"])
                tt("dve", B0.c(dst_col, dst_col + 512, 0, r), F6.c(0, 512, 0, r), extra_mul, ALU.mult, ["F6", "B11"], ["B0"])

        def tile_C(t):
            trs([(PB[0].c(k * 128, k * 128 + 128), B0.c(k * 128, k * 128 + 128), ident_bf.c(0, 128)) for k in range(8)],
                ["B0", "ident_bf"], ["PB0"])
            cp("act", B1.c(0, 1024), PB[0].c(0, 1024), ["PB0"], ["B1"])
            for cb in range(2):
                ps = PS[cb]
                mms([(ps.c(0, 512), B1.c(k * 128, k * 128 + 128), AR.c(WOUT_O + k * D + cb * 512, WOUT_O + k * D + cb * 512 + 512), k == 0, k == 7) for k in range(8)],
                    ["B1"] + WOUT_R, ["PS%d" % cb])
                xa = X.c(t * D + cb * 512, t * D + cb * 512 + 512)
                tt("dve", xa, xa, ps.c(0, 512), ALU.add, ["X%d" % t, "PS%d" % cb], ["X%d" % t])

        tile_A(ST, CSS, "CSS", True)
        for (src, o) in ((F0, 0), (F1, 128), (F2, 256)):
            P.dma("sp", (lambda s_, o_: (lambda e: e.dma_start(out=bass.AP(ag1_in, o_, [[1536, 16], [384, 4], [1, 128]]),
                                                                in_=s_.v(0, [[128, 4], [1, 128]], 0, 16))))(src, o),
                  reads=["F0", "F1", "F2"], writes=["ag1_in"])
        SR = arfres(0, 12288)
        S0 = ARFt

        for t in range(NTP if _STOP >= 2 else 0):
            dmain("sp", CS.c(0, 32), cosP.ap()[t * 128:(t + 1) * 128, :], ["CS"])
            dmain("sp", CS.c(32, 64), sinP.ap()[t * 128:(t + 1) * 128, :], ["CS"])
            tile_A(t, CS, "CS", False)
            KTR = arres(KT_O, 4 * S)
            accs = []
            for g in range(8):
                accs.append((PS[3 + g // 3], (g % 3) * 129))
            for j in range(t + 1):
                for half in range(2):
                    psi = 2 if half == 0 else 1
                    ps = PS[psi]
                    lst = []
                    for hh in range(2):
                        h = half * 2 + hh
                        lst.append((ps.c(hh * 256, hh * 256 + 256),
                                    AR.c(KT_O + h * S + j * 128, KT_O + h * S + j * 128 + 128),
                                    B5.c(h * 256, h * 256 + 256), True, True))
                    mms(lst[:_DBG_N], KTR + ["B5"], ["PS%d" % psi])
                    pt = PT.c(half * 512, half * 512 + 512)
                    actf(pt, ps.c(0, 512), AF.Exp, ["PS%d" % psi], ["PT%d" % half], scale=0.125)
                    if j == t:
                        tt("pool", pt, pt, tri4.c(0, 512), ALU.mult, ["PT%d" % half, "tri4"], ["PT%d" % half])
                    lst = []
                    for hh in range(2):
                        h = half * 2 + hh
                        for m in range(2):
                            bank, col = accs[h * 2 + m]
                            lst.append((bank.c(col, col + 129),
                                        PT.c(half * 512 + (hh * 2 + m) * 128, half * 512 + (hh * 2 + m) * 128 + 128),
                                        AR.c(VA_O + j * 528 + h * 132, VA_O + j * 528 + h * 132 + 129), (j == 0 and (h * 2 + m) in (0, 3, 6)), j == t, 1))
                    mms(lst, ["PT%d" % half] + arres(VA_O + j * 528, 528), ["PS3", "PS4", "PS5"])
            for g in range(8):
                bank, col = accs[g]
                P.dve(lambda e, b_=bank, c_=col, g_=g: e.reciprocal(out=stat.c(24 + g_, 25 + g_), in_=b_.c(c_ + 128, c_ + 129)),
                      ["PS3", "PS4", "PS5"], ["rden"])
            for h in range(4):
                ts("dve", stat.c(24 + 2 * h + 1, 24 + 2 * h + 2), stat.c(24 + 2 * h + 1, 24 + 2 * h + 2), lamc.c(l, l + 1), None, ALU.mult, None, ["rden", "lamc"], ["rden"])
            for h in range(4):
                b1, c1 = accs[2 * h]
                b2, c2 = accs[2 * h + 1]
                ts("dve", F0.c(h * 128, h * 128 + 128), b1.c(c1, c1 + 128), stat.c(24 + 2 * h, 25 + 2 * h), None, ALU.mult, None, ["PS3", "PS4", "PS5", "rden"], ["F0"])
                stt("dve", F0.c(h * 128, h * 128 + 128), b2.c(c2, c2 + 128), stat.c(25 + 2 * h, 26 + 2 * h), F0.c(h * 128, h * 128 + 128), ALU.mult, ALU.add, ["PS3", "PS4", "PS5", "rden", "F0"], ["F0"])
            merge_head_norm(F0.c(0, 512), Gsub, l * 128, 0, None, srcres=("F0",))
            mms([(PS[2].c(0, 256), triN.c(0, 128), F5.c(0, 256), True, True),
                 (PS[2].c(256, 512), onesN.c(0, 128), F5.c(0, 256), True, True)], ["triN", "onesN", "F5a"], ["PS2"])
            cp("act", F7.c(0, 256), PS[2].c(0, 256), ["PS2"], ["F7a", "F7b"])
            tt("dve", F5.c(256, 512), PS[2].c(256, 512), F7.c(0, 256), ALU.subtract, ["PS2", "F7a", "F7b"], ["F5b"])
            actf(F5.c(256, 512), F5.c(256, 512), AF.Exp, ["F5b"], ["F5b"])
            tt("dve", B7.c(0, 256), B8.c(256, 512), F5.c(256, 512), ALU.mult, ["B8", "F5b"], ["B7"])
            mms([(PS[2].c(h * 128, h * 128 + 128, 0, 64), F5.c(h * 64, h * 64 + 64), triN.c(0, 128), True, True) for h in range(4)],
                ["F5a", "triN"], ["PS2"])
            actf(F6.c(0, 512, 0, 64), PS[2].c(0, 512, 0, 64), AF.Exp, ["PS2", "F6"], ["F6"])
            actf(F7.c(0, 512, 0, 64), PS[2].c(0, 512, 0, 64), AF.Exp, ["PS2", "F7a", "F7b"], ["F7a", "F7b"], scale=-1.0)
            trs([(PB[1].c(i * 128, i * 128 + 128, 0, 64), B8.c(i * 64, i * 64 + 64), ident_bf.c(0, 128)) for i in range(8)],
                ["B8", "ident_bf"], ["PB1"])
            stt("dve", B9.c(0, 512, 0, 64), PB[1].c(0, 512, 0, 64), 0.125, F6.c(0, 512, 0, 64), ALU.mult, ALU.mult, ["PB1", "F6"], ["B9a"])
            tt("dve", B9.c(512, 1024, 0, 64), PB[1].c(512, 1024, 0, 64), F7.c(0, 512, 0, 64), ALU.mult, ["PB1", "F7a", "F7b"], ["B9b"])
            mms([(PS[2].c(h * 128, h * 128 + 128), B9.c(512 + h * 128, 512 + h * 128 + 128, 0, 64), B9.c(h * 128, h * 128 + 128, 0, 64), True, True) for h in range(4)],
                ["B9a", "B9b"], ["PS2"])
            tt("dve", B10.c(0, 512), PS[2].c(0, 512), tri4.c(0, 512), ALU.mult, ["PS2", "tri4"], ["B10"])
            lst = []
            for h in range(4):
                lst.append((PS[3].c(h * 128, h * 128 + 128), B9.c(h * 128, h * 128 + 128, 0, 64), Sbf.c(h * 128, h * 128 + 128, 0, 64), True, False))
                lst.append((PS[3].c(h * 128, h * 128 + 128), B10.c(h * 128, h * 128 + 128), B2.c(h * 128, h * 128 + 128), False, True))
            mms(lst, ["B9a", "Sbf", "B10", "B2"], ["PS3"])
            mms([(PS[4].c(h * 128, h * 128 + 128, 0, 64), B7.c(h * 64, h * 64 + 64), B2.c(h * 128, h * 128 + 128), True, True) for h in range(4)],
                ["B7", "B2"], ["PS4"])
            for h in range(4):
                stt("dve", Sst.c(h * 128, h * 128 + 128, 0, 64), Sst.c(h * 128, h * 128 + 128, 0, 64), F6.c(h * 128 + 127, h * 128 + 128, 0, 64),
                    PS[4].c(h * 128, h * 128 + 128, 0, 64), ALU.mult, ALU.add, ["Sst", "F6", "PS4"], ["Sst"])
            cp("pool", Sbf.c(0, 512, 0, 64), Sst.c(0, 512, 0, 64), ["Sst"], ["Sbf"])
            merge_head_norm(PS[3].c(0, 512), Ggn, l * 128, 512, B11.c(0, 512))
            tile_C(t)
        dmaout("sp", gp.ap()[l].rearrange("h k v -> k h v"), Sst.v(0, [[128, 4], [1, 128]], 0, 64), ["Sst"])

        if _STOP < 3:
            continue
        tile_A_redo = True
        tile_A(ST, CSS, "CSS", True, emit_out=False)
        S0c, VBc = 0, 4096
        S0r = arfres(S0c, 4096)
        VBr = arfres(VBc, 4096)
        dmain("sp", ARFt.v(S0c, [[256, 16], [128, 2], [1, 128]]),
              bass.AP(sg, l * 16 * 32768, [[128, 128], [32768, 16], [16384, 2], [1, 128]]), S0r)
        for h2 in range(2):
            dmain("sp", ARFt.v(VBc, [[256, 16], [128, 2], [1, 128]], h2 * 64, 64),
                  bass.AP(vscr, h2 * 128, [[0, 64], [512, 16], [256, 2], [1, 128]]), VBr, reads=["vscr"])
        actf(F6.c(0, 256, 0, 16), F5.c(0, 256, 0, 16), AF.Exp, ["F5a", "F6"], ["F6"], scale=-1.0 / 16)
        cp("dve", F6.c(256, 512, 0, 16), B8.c(256, 512, 0, 16), ["B8", "F6"], ["F6"])
        ts("dve", F5.c(256, 512, 0, 16), B8.c(0, 256, 0, 16), 0.125, None, ALU.mult, None, ["B8"], ["F5b"])
        lst = []
        for qi, (src, c0) in enumerate(((F6, 0), (F6, 256), (F5, 256))):
            for hp in range(2):
                lst.append((PS[2].c((qi * 2 + hp) * 16, (qi * 2 + hp) * 16 + 16), src.c(c0 + hp * 128, c0 + hp * 128 + 128, 0, 16), identf.c(0, 16, 0, 16)))
        trs(lst, ["F6", "F5b", "identf"], ["PS2"])
        cp("act", F2.c(0, 96), PS[2].c(0, 96), ["PS2", "F2"], ["F2"])
        def s_view(c0):
            return ARFt.v(c0, [[128, 2], [256, 16], [1, 128]])

        def col_view(qi):
            return F2.v(qi * 32, [[16, 2], [1, 16], [0, 128]])
        tt("dve", s_view(S0c), s_view(S0c), col_view(0), ALU.mult, S0r + ["F2"], S0r)
        tt("pool", s_view(VBc), s_view(VBc), col_view(1), ALU.mult, VBr + ["F2"], VBr)
        tt("dve", ARFt.c(S0c, S0c + 4096), ARFt.c(S0c, S0c + 4096), ARFt.c(VBc, VBc + 4096), ALU.add, S0r + VBr, S0r)
        dmaout("sp", bass.AP(gs, l * 16 * 32768, [[128, 128], [32768, 16], [16384, 2], [1, 128]]),
               ARFt.v(S0c, [[256, 16], [128, 2], [1, 128]]), S0r)
        SBc = 16384 + 0
        SBr = arres(SBc, 4096)
        cp("pool", AR.c(SBc, SBc + 4096), ARFt.c(S0c, S0c + 4096), S0r, SBr)
        QDc = SBc + 4096
        QDr = arres(QDc, 512)
        tt("dve", AR.v(QDc, [[256, 2], [16, 16], [1, 16]]), F2.v(64, [[16, 2], [1, 16], [0, 16]]), i16.v(0, [[0, 2], [16, 16], [1, 16]]), ALU.mult,
           ["F2", "i16"], QDr)
        lst = []
        for h in range(4):
            hp, h2 = h // 2, h % 2
            for b in range(16):
                lst.append((PS[3 + h2].c(h * 128, h * 128 + 128, 0, 16),
                            AR.c(QDc + hp * 256 + b * 16, QDc + hp * 256 + b * 16 + 16, h2 * 64, 64),
                            AR.c(SBc + b * 256 + hp * 128, SBc + b * 256 + hp * 128 + 128, h2 * 64, 64), b == 0, b == 15))
        mms([x for x in lst if x[0] is not None], QDr + SBr, ["PS3", "PS4"])
        for h in range(4):
            cp("act", F0.c(h * 128, h * 128 + 128, 0, 16), PS[3 + h % 2].c(h * 128, h * 128 + 128, 0, 16), ["PS3", "PS4", "F0"], ["F0"])
        merge_head_norm(F0.c(0, 512, 0, 16), Ggn, l * 128, 512, B11.c(0, 512, 0, 16), nrows=16, srcres=("F0",))

        if _STOP < 4:
            continue
        QKV = F0
        dmain("sp", F0.c(0, 384, 0, 64), bass.AP(ag1_in, 0, [[384, 64], [1, 384]]), ["F0"], reads=["ag1_in"])
        KtO = [0, 2048]
        VtO = [4096, 6144]
        PRO = 8192
        PACC = F1
        QBt = [F7, F6]
        for i in range(64):
            bsel = i % 2
            ko, vo = KtO[bsel], VtO[bsel]
            kr, vr = arfres(ko, 2048), arfres(vo, 2048)
            qb = QBt[bsel]
            qbn = "F7a" if bsel == 0 else "F6"
            qbw = ["F7a", "F7b"] if bsel == 0 else ["F6"]
            dmain("sp", qb.c(0, 128), bass.AP(ag1_in, i * 384, [[0, 128], [1, 128]]), qbw, reads=["ag1_in"])
            P.dma("pool", (lambda i_, ko_: (lambda e: e.indirect_dma_start(
                out=ARFt.c(ko_, ko_ + 2048), out_offset=None, in_=kc.ap(),
                in_offset=bass.IndirectOffsetOnAxis(ap=idxK.c(i_, i_ + 1), axis=0))))(l * 64 + i, ko),
                reads=["idxK"], writes=kr)
            P.dma("pool", (lambda i_, vo_: (lambda e: e.indirect_dma_start(
                out=ARFt.c(vo_, vo_ + 2048), out_offset=None, in_=vc.ap(),
                in_offset=bass.IndirectOffsetOnAxis(ap=idxK.c(i_, i_ + 1), axis=0))))(l * 64 + i, vo),
                reads=["idxK"], writes=vr)
            pr = arfres(PRO, 2048)
            tt("pool" if i % 2 else "dve", ARFt.v(PRO, [[128, 16], [1, 128]]), ARFt.v(ko, [[128, 16], [1, 128]]), qb.v(0, [[0, 16], [1, 128]]), ALU.mult,
               kr + qbw, pr)
            red("dve", sc32.c(0, 32), ARFt.v(PRO, [[64, 32], [1, 64]]), pr, ["sc"])
            actf(pbuf.c(0, 32), sc32.c(0, 32), AF.Exp, ["sc"], ["pbuf"], scale=0.125)
            red("dve", PACC.c(i * 2, i * 2 + 2), pbuf.v(0, [[1, 2], [2, 16]]), ["pbuf"], ["F1"])
            mms([(PS[4].c(i * 2, i * 2 + 2), ARFt.c(vo + r * 128, vo + r * 128 + 128), pbuf.c(r * 2, r * 2 + 2), r == 0, r == 15) for r in range(16)],
                vr + ["pbuf"], ["PS4"])
        cp("act", F2.c(0, 128), PS[4].c(0, 128), ["PS4", "F2"], ["F2"])
        trs([(PS[2].c(m * 128, m * 128 + 128, 0, 64), F2.v(m, [[2, 64]]), identf.c(0, 128)) for m in range(2)], ["F2", "identf"], ["PS2"])
        mms([(PS[2].c(256 + m, 257 + m, 0, 64), PACC.v(m, [[2, 64]]), onesf.c(0, 1), True, True) for m in range(2)], ["F1", "onesf"], ["PS2"])
        tt("dve", F5.c(0, 128, 0, 64), F0.c(0, 128, 0, 64), F0.c(128, 256, 0, 64), ALU.mult, ["F0"], ["F5a"])
        red("dve", stat.c(32, 34, 0, 64), F5.v(0, [[64, 2], [1, 64]], 0, 64), ["F5a"], ["snew"])
        actf(stat.c(32, 34, 0, 64), stat.c(32, 34, 0, 64), AF.Exp, ["snew"], ["snew"], scale=0.125)
        tt("dve", stat.c(34, 36, 0, 64), PS[2].c(256, 258, 0, 64), stat.c(32, 34, 0, 64), ALU.add, ["PS2", "snew"], ["den"])
        P.dve(lambda e: e.reciprocal(out=stat.c(34, 36, 0, 64), in_=stat.c(34, 36, 0, 64)), ["den"], ["den"])
        ts("dve", stat.c(35, 36, 0, 64), stat.c(35, 36, 0, 64), lamc.c(l, l + 1, 0, 64), None, ALU.mult, None, ["den", "lamc"], ["den"])
        for m in range(2):
            stt("dve", F5.c(256 + m * 128, 384 + m * 128, 0, 64), F0.c(256, 384, 0, 64), stat.c(32 + m, 33 + m, 0, 64), PS[2].c(m * 128, m * 128 + 128, 0, 64),
                ALU.mult, ALU.add, ["F0", "snew", "PS2", "F5b"], ["F5b"])
        ts("dve", F5.c(0, 128, 0, 64), F5.c(256, 384, 0, 64), stat.c(34, 35, 0, 64), None, ALU.mult, None, ["F5b", "den", "F5a"], ["F5a"])
        stt("dve", F5.c(0, 128, 0, 64), F5.c(384, 512, 0, 64), stat.c(35, 36, 0, 64), F5.c(0, 128, 0, 64), ALU.mult, ALU.add, ["F5b", "den", "F5a"], ["F5a"])
        dmain("sp", ag2_in.ap(), F5.c(0, 128, 0, 64), ["ag2_in"], reads=["F5a"])
        dmain("sp", F0.c(0, 512, 0, 16), bass.AP(ag2_in, 0, [[512, 16], [1, 512]]), ["F0"], reads=["ag2_in"])
        merge_head_norm(F0.c(0, 512, 0, 16), Gsub, l * 128, 0, None, nrows=16, srcres=("F0",))
        tile_C(ST)

        if _STOP < 5:
            continue
        dmain("pool", AR.v(WD_O, [[D, 22], [1, D]]), w_down.ap()[l].rearrange("(j p) c -> p j c", p=128), arres(WD_O, 22528))
        WDR = arres(WD_O, 22528)
        groups = []
        t0 = 0
        per = -(-NT // 4)
        while t0 < NT:
            groups.append(list(range(t0, min(NT, t0 + per))))
            t0 += per
        wq = 0
        for grp in groups:
            ntok = len(grp) * 128
            N2R = arres(N2T_O, 8 * GT)
            for gi, t in enumerate(grp):
                tile_norm_T(t, g2T, (lambda gi_: (lambda: AR.v(N2T_O + gi_ * 128, [[GT, 8], [1, 128]])))(gi), N2R)
            HTR = arres(HT_O, 22 * GT)
            for j in range(22):
                wb = WGU_O + (wq % 3) * 2048
                wr = arres(wb, 2048)
                wq += 1
                for hf in range(2):
                    dmain("pool", AR.v(wb + hf * 128, [[256, 8], [1, 128]]),
                          bass.AP(w_gu, l * D * 2 * DFF + hf * DFF + j * 128, [[2 * DFF, 128], [128 * 2 * DFF, 8], [1, 128]]), wr)
                c0 = 0
                while c0 < ntok:
                    n = min(512, ntok - c0)
                    mms([(PS[0].c(0, n), AR.c(wb + k * 256, wb + k * 256 + 128), AR.c(N2T_O + k * GT + c0, N2T_O + k * GT + c0 + n), k == 0, k == 7) for k in range(8)],
                        wr + N2R, ["PS0"])
                    mms([(PS[1].c(0, n), AR.c(wb + k * 256 + 128, wb + k * 256 + 256), AR.c(N2T_O + k * GT + c0, N2T_O + k * GT + c0 + n), k == 0, k == 7) for k in range(8)],
                        wr + N2R, ["PS1"])
                    actf(F0.c(0, n), PS[0].c(0, n), AF.Silu, ["PS0", "F0"], ["F0"])
                    tt("dve", AR.c(HT_O + j * GT + c0, HT_O + j * GT + c0 + n), F0.c(0, n), PS[1].c(0, n), ALU.mult, ["F0", "PS1"], HTR)
                    c0 += n
            for gi, t in enumerate(grp):
                for cb in range(2):
                    ps = PS[2 + cb]
                    mms([(ps.c(0, 512), AR.c(HT_O + j * GT + gi * 128, HT_O + j * GT + gi * 128 + 128), AR.c(WD_O + j * D + cb * 512, WD_O + j * D + cb * 512 + 512), j == 0, j == 21) for j in range(22)],
                        HTR + WDR, ["PS%d" % (2 + cb)])
                    xa = X.c(t * D + cb * 512, t * D + cb * 512 + 512)
                    tt("dve", xa, xa, ps.c(0, 512), ALU.add, ["X%d" % t, "PS%d" % (2 + cb)], ["X%d" % t])
                if l == _NL - 1:
                    if t < NTP:
                        dmaout("sp", yp.ap()[t * 128:(t + 1) * 128, :], X.c(t * D, t * D + D), ["X%d" % t])
                    else:
                        dmaout("sp", ys.ap(), X.c(t * D, t * D + D, 0, 16), ["X%d" % t])

    if _STOP < 5:
        for t in range(NTP):
            dmaout("sp", yp.ap()[t * 128:(t + 1) * 128, :], X.c(t * D, t * D + D), ["X%d" % t])
        dmaout("sp", ys.ap(), X.c(ST * D, ST * D + D, 0, 16), ["X%d" % ST])
    P.finalize(sems)
    with nc.Block() as block:
        @block.tensor
        def _(e):
            P.emit("pe", e)

        @block.scalar
        def _(e):
            P.emit("act", e)

        @block.vector
        def _(e):
            P.emit("dve", e)

        @block.gpsimd
        def _(e):
            P.emit("pool", e)

        @block.sync
        def _(e):
            P.emit("sp", e)
    es.close()
    return nc


_DEBUG_HOOK = None
_USE_CC = True
_STOP = 99
_OPLIMIT = 10 ** 9
_DBG_N = 99
_NL = 2
def _consts(S, past):
    half = 32
    freqs = (10000.0 ** (-np.arange(half, dtype=np.float32) / half)).astype(np.float32)
    pos = np.arange(S, dtype=np.float32)
    ang = pos[:, None] * freqs[None, :]
    angs = np.full((128, 1), float(past), np.float32) * freqs[None, :]
    ident = np.eye(128, dtype=np.float32)
    tri = np.triu(np.ones((128, 128), np.float32))
    i16 = np.tile(np.eye(16, dtype=np.float32).reshape(1, 256), (128, 1))
    rg = (np.arange(128) % 8).astype(np.float32).reshape(128, 1)
    return dict(cosP=np.cos(ang).astype(np.float32), sinP=np.sin(ang).astype(np.float32),
                cosS=np.cos(angs).astype(np.float32), sinS=np.sin(angs).astype(np.float32),
                identd=ident, trid=tri, i16d=i16, rgcol=rg)


def kernel(x_prompt, x_sample, cache_k, cache_v, state_gla, page_table, norm1, w_in, q_norm, k_norm,
           lambda_qk, subln, w_a2, b_a, gla_norm, w_out, norm2, w_gu, w_down):
    f = lambda a: np.ascontiguousarray(np.asarray(a, dtype=np.float32))
    x_prompt, x_sample = f(x_prompt), f(x_sample)
    cache_k, cache_v, state_gla = np.asarray(cache_k), np.asarray(cache_v), f(state_gla)
    page_table = np.asarray(page_table).astype(np.int32)
    S = x_prompt.shape[1]
    NPHYS = cache_k.shape[1]
    n_cores = 8
    nc = bass.Bass("TRN2", target_bir_lowering=False)
    build(nc, S, use_cc=_USE_CC, NPHYS=NPHYS)
    cst = _consts(S, 2048)
    shared = dict(w_in=f(w_in), w_out=f(w_out), w_gu=f(w_gu), w_down=f(w_down), norm1=f(norm1), norm2=f(norm2),
                  qn=f(q_norm), kn=f(k_norm), lqk=f(lambda_qk).reshape(2, 256), subln=f(subln), wa2=f(w_a2),
                  ba=f(b_a), gnorm=f(gla_norm))
    shared.update(cst)
    kh = np.ascontiguousarray(cache_k.reshape(2, NPHYS, 128, 4, 128).transpose(0, 3, 1, 2, 4)).reshape(8 * NPHYS * 8, 2048)
    vh = np.ascontiguousarray(cache_v.transpose(0, 3, 1, 2, 4)).reshape(8 * NPHYS * 8, 2048)
    in_maps = []
    for c in range(n_cores):
        half, h = c // 4, c % 4
        m = dict(shared)
        m["xp"] = x_prompt[c]
        xs = np.zeros((128, D), np.float32)
        xs[:16] = x_sample[16 * c:16 * c + 16, 0]
        m["xs"] = xs
        m["kc"] = kh
        m["vc"] = vh
        m["sg"] = np.ascontiguousarray(state_gla[:, 16 * c:16 * c + 16])
        pt = page_table[16 * c:16 * c + 16]
        m["ptrep"] = np.ascontiguousarray(np.repeat(pt.T, 8, axis=0)).astype(np.int32)
        in_maps.append(m)
    if _DEBUG_HOOK is not None:
        return _DEBUG_HOOK(nc, in_maps)
    res = run_bass_kernel_spmd(nc, in_maps, core_ids=list(range(n_cores))).results
    yp = np.stack([r["yp"] for r in res])
    ys = np.concatenate([r["ys"] for r in res])[:, None, :]
    kp = np.stack([r["kp"] for r in res], axis=1).reshape(2, 8, S, 4, 2, 64)
    vp = np.stack([r["vp"] for r in res], axis=1).reshape(2, 8, S, 4, 128)
    gp = np.stack([r["gp"] for r in res], axis=1)
    ks = np.concatenate([r["ks"] for r in res], axis=1).reshape(2, 128, 1, 4, 2, 64)
    vs = np.concatenate([r["vs"] for r in res], axis=1).reshape(2, 128, 1, 4, 128)
    gs = np.concatenate([r["gs"] for r in res], axis=1)
    return (yp.astype(np.float32), ys.astype(np.float32), kp.astype(np.float32), vp.astype(np.float32),
            gp.astype(np.float32), ks.astype(np.float32), vs.astype(np.float32), gs.astype(np.float32))
```

```python
import contextlib
import math
import numpy as np
import concourse.bass as bass
import concourse.mybir as mb
from concourse.bass_utils import run_bass_kernel_spmd

F32 = mb.dt.float32
BF = mb.dt.bfloat16
I32 = mb.dt.int32
AF = mb.ActivationFunctionType
ALU = mb.AluOpType
AX = mb.AxisListType

D = 1024
DIN = 3088
DFF = 2816
NPHYS = 2560
EPS = 1e-6
ENGS = ("pe", "act", "dve", "pool", "sp")


class Op:
    __slots__ = ("eng", "fn", "dma", "deps", "seq", "signal", "sigidx", "dsem", "dval",
                 "waits", "gidx", "slotwait", "cc")

    def __init__(self, eng, fn, dma):
        self.eng = eng
        self.fn = fn
        self.dma = dma
        self.deps = set()
        self.signal = False
        self.sigidx = 0
        self.dsem = None
        self.dval = 0
        self.waits = []
        self.slotwait = None
        self.cc = False


class Prog:
    def __init__(self, nc):
        self.nc = nc
        self.ops = []
        self.eops = {e: [] for e in ENGS}
        self.last_writer = {}
        self.readers = {}
        self.ndma = {"sp": 8, "pool": 8, "act": 4}
        self.out_dmas = []

    def op(self, eng, fn, reads=(), writes=(), dma=False, is_output=False):
        if len(self.ops) >= _OPLIMIT and not is_output:
            return None
        o = Op(eng, fn, dma)
        deps = set()
        pr = [r for r in reads if r.startswith("PS") or r.startswith("PB")]
        if pr:
            writes = list(writes) + [r for r in pr if r not in writes]
        for r in reads:
            w = self.last_writer.get(r)
            if w is not None:
                deps.add(w)
        for r in writes:
            w = self.last_writer.get(r)
            if w is not None:
                deps.add(w)
            for rd in self.readers.get(r, ()):
                deps.add(rd)
        for r in reads:
            self.readers.setdefault(r, []).append(o)
        for r in writes:
            self.last_writer[r] = o
            self.readers[r] = []
        deps.discard(o)
        o.deps = deps
        o.gidx = len(self.ops)
        o.seq = len(self.eops[eng])
        self.ops.append(o)
        self.eops[eng].append(o)
        if is_output:
            self.out_dmas.append(o)
        return o

    def pe(self, fn, reads=(), writes=()):
        return self.op("pe", fn, reads, writes)

    def act(self, fn, reads=(), writes=()):
        return self.op("act", fn, reads, writes)

    def dve(self, fn, reads=(), writes=()):
        return self.op("dve", fn, reads, writes)

    def pool(self, fn, reads=(), writes=()):
        return self.op("pool", fn, reads, writes)

    def dma(self, q, fn, reads=(), writes=(), is_output=False):
        return self.op(q, fn, reads, writes, dma=True, is_output=is_output)

    def ccop(self, fn, reads=(), writes=()):
        o = self.op("pool", fn, reads, writes, dma=True)
        if o is not None:
            o.cc = True
        return o

    def finalize(self, sems):
        fin = Op("sp", None, False)
        fin.deps = set(self.out_dmas)
        fin.gidx = len(self.ops)
        fin.seq = len(self.eops["sp"])
        self.ops.append(fin)
        self.eops["sp"].append(fin)
        cnt = {}
        lastval = {}
        ncc = 0
        for o in self.ops:
            if o.dma and o.cc:
                o.dsem = "cc%d" % ncc
                ncc += 1
                o.dval = 1
                continue
            if o.dma:
                q = o.eng
                i = cnt.get(q, 0)
                cnt[q] = i + 1
                key = "d_%s_%d" % (q, i % self.ndma[q])
                o.dsem = key
                prev = lastval.get(key, 0)
                o.dval = prev + 16
                lastval[key] = o.dval
                if prev > 0:
                    o.slotwait = (key, prev)
        known = {e: {p: -1 for p in ENGS} for e in ENGS}
        known_dma = {e: {} for e in ENGS}
        for o in self.ops:
            e = o.eng
            best = {}
            for d in o.deps:
                if d.dma:
                    k = known_dma[e].get(d.dsem, 0)
                    if d.dval > k:
                        cur = best.get(("dma", d.dsem))
                        if cur is None or d.dval > cur.dval:
                            best[("dma", d.dsem)] = d
                else:
                    if d.seq > known[e][d.eng]:
                        cur = best.get(("eng", d.eng))
                        if cur is None or d.seq > cur.seq:
                            best[("eng", d.eng)] = d
            if o.slotwait is not None:
                key, prev = o.slotwait
                if known_dma[e].get(key, 0) >= prev:
                    o.slotwait = None
                else:
                    known_dma[e][key] = prev
            for k, d in best.items():
                if k[0] == "dma":
                    known_dma[e][d.dsem] = d.dval
                else:
                    known[e][d.eng] = d.seq
                    d.signal = True
            o.waits = [best[k] for k in sorted(best.keys())]
        for e in ENGS:
            n = 0
            for o in self.eops[e]:
                if o.signal and not o.dma:
                    n += 1
                    o.sigidx = n
        self.sems = sems

    def emit(self, eng, e):
        sems = self.sems
        for o in self.eops[eng]:
            if o.slotwait is not None:
                e.wait_ge(sems[o.slotwait[0]], o.slotwait[1])
            for d in o.waits:
                if d.dma:
                    e.wait_ge(sems[d.dsem], d.dval)
                else:
                    e.wait_ge(sems[d.eng], d.sigidx)
            if o.fn is None:
                continue
            ins = o.fn(e)
            if o.dma and o.cc:
                ins.then_inc(sems[o.dsem])
            elif o.dma:
                ins.then_inc(sems[o.dsem], 16)
            elif o.signal:
                ins.then_inc(sems[o.eng], 1)


class T:
    def __init__(self, h, F):
        self.h = h
        self.F = F

    def v(self, col=0, dims=None, p0=0, np_=128):
        return bass.AP(self.h, p0 * self.F + col, [[self.F, np_]] + [list(d) for d in dims])

    def c(self, a, b, p0=0, np_=128):
        return self.v(a, [[1, b - a]], p0, np_)


def build(nc, S, use_cc=True, NPHYS=2560):
    NTP = S // 128
    NT = NTP + 1
    ST = NTP
    es = contextlib.ExitStack()
    P = Prog(nc)

    def din(name, shape, dt=F32):
        return nc.dram_tensor(name, list(shape), dt, kind="ExternalInput")

    def dout(name, shape, dt=F32):
        return nc.dram_tensor(name, list(shape), dt, kind="ExternalOutput")

    xp = din("xp", [S, D])
    xs = din("xs", [128, D])
    w_in = din("w_in", [2, D, DIN])
    w_out = din("w_out", [2, D, D])
    w_gu = din("w_gu", [2, D, 2 * DFF])
    w_down = din("w_down", [2, DFF, D])
    norm1 = din("norm1", [2, D])
    norm2 = din("norm2", [2, D])
    qn = din("qn", [2, 64])
    kn = din("kn", [2, 64])
    lqk = din("lqk", [2, 256])
    subln = din("subln", [2, 128])
    wa2 = din("wa2", [2, 16, 256])
    ba = din("ba", [2, 256])
    gnorm = din("gnorm", [2, 128])
    kc = din("kc", [8 * NPHYS * 8, 2048])
    vc = din("vc", [8 * NPHYS * 8, 2048])
    sg = din("sg", [2, 16, 4, 64, 128])
    ptrep = din("ptrep", [128, 16], I32)
    rgcol = din("rgcol", [128, 1])
    cosP = din("cosP", [S, 32])
    sinP = din("sinP", [S, 32])
    cosS = din("cosS", [128, 32])
    sinS = din("sinS", [128, 32])
    identd = din("identd", [128, 128])
    trid = din("trid", [128, 128])
    i16d = din("i16d", [128, 256])

    yp = dout("yp", [S, D])
    ys = dout("ys", [16, D])
    kp = dout("kp", [2, S, 512])
    vp = dout("vp", [2, S, 512])
    gp = dout("gp", [2, 4, 64, 128])
    ks = dout("ks", [2, 16, 512])
    vs = dout("vs", [2, 16, 512])
    gs = dout("gs", [2, 16, 4, 64, 128])

    ag1_in = nc.dram_tensor("ag1_in", [16, 1536], F32)
    ag2_in = nc.dram_tensor("ag2_in", [64, 128], F32)
    vscr = nc.dram_tensor("vscr", [16, 512], F32)

    def sb(name, F, dt):
        return T(es.enter_context(nc.sbuf_tensor(name, [128, F], dt)), F)

    def psb(name, F, dt):
        return T(es.enter_context(nc.psum_tensor(name, [128, F], dt)), F)

    X = sb("X", NT * D, F32)
    ARC = 49664
    AR = sb("AR", ARC, BF)
    CH = 2048

    def arres(off, n):
        return ["A%d" % i for i in range(off // CH, (off + n - 1) // CH + 1)]

    WIN_O, WOUT_O, KT_O, VA_O = 0, 24704, 32896, 41088
    WIN_R = arres(WIN_O, 24704)
    WOUT_R = arres(WOUT_O, 8192)
    GT = 640
    WD_O, HT_O, WGU_O, N2T_O = 0, 22528, 36864, 43008
    ARF = AR.h.bitcast(F32)
    ARFt = T(ARF, ARC // 2)

    def arfres(offf, n):
        return arres(offf * 2, n * 2)

    ident_bf = sb("ident_bf", 128, BF)
    identf = sb("identf", 128, F32)
    triN = sb("triN", 128, F32)
    onesN = sb("onesN", 128, F32)
    onesf = sb("onesf", 128, F32)
    tri4 = sb("tri4", 512, BF)
    i16 = sb("i16", 256, F32)
    g1T = sb("g1T", 16, F32)
    g2T = sb("g2T", 16, F32)
    Gq = sb("Gq", 128, F32)
    Gk = sb("Gk", 128, F32)
    Gsub = sb("Gsub", 256, F32)
    Ggn = sb("Ggn", 256, F32)
    lamc = sb("lamc", 8, F32)
    wa2a = sb("wa2a", 512, BF)
    CS = sb("CS", 64, F32)
    CSS = sb("CSS", 64, F32)
    grT = sb("grT", 128, BF)
    stat = sb("stat", 64, F32)
    F0 = sb("F0", 512, F32)
    F1 = sb("F1", 512, F32)
    F2 = sb("F2", 512, F32)
    F5 = sb("F5", 512, F32)
    LQ = F5
    F6 = sb("F6", 512, F32)
    F7 = sb("F7", 512, F32)
    B0 = sb("B0", 1024, BF)
    B1 = sb("B1", 1024, BF)
    B2 = sb("B2", 512, BF)
    B3 = sb("B3", 512, BF)
    B5 = sb("B5", 1024, BF)
    B7 = sb("B7", 256, BF)
    B8 = sb("B8", 512, BF)
    B9 = sb("B9", 1024, BF)
    B10 = sb("B10", 512, BF)
    B11 = sb("B11", 512, BF)
    PT = sb("PT", 1024, BF)
    Sst = sb("Sst", 512, F32)
    Sbf = sb("Sbf", 512, BF)
    sc32 = sb("sc32", 32, F32)
    pbuf = sb("pbuf", 32, F32)
    idxK = sb("idxK", 128, I32)
    idxKf = sb("idxKf", 128, F32)
    ptf = sb("ptf", 16, F32)
    pti = sb("pti", 16, I32)
    PS = [psb("ps%d" % i, 512, F32) for i in range(6)]
    PB = [psb("pb%d" % i, 1024, BF) for i in range(2)]

    names = ["pe", "act", "dve", "pool"] + ["d_sp_%d" % i for i in range(8)] + \
        ["d_pool_%d" % i for i in range(8)] + ["d_act_%d" % i for i in range(4)] + ["cc%d" % i for i in range(4)]
    sems = {n: es.enter_context(nc.semaphore(n)) for n in names}

    def dmain(q, out, in_, writes, reads=()):
        P.dma(q, lambda e: e.dma_start(out=out, in_=in_, allow_slow_non_contiguous=True), reads=reads, writes=writes)

    def dmaout(q, out, in_, reads):
        P.dma(q, lambda e: e.dma_start(out=out, in_=in_, allow_slow_non_contiguous=True), reads=reads, is_output=True)

    def tt(eng, out, in0, in1, op, reads, writes):
        P.op(eng, lambda e: e.tensor_tensor(out=out, in0=in0, in1=in1, op=op), reads, writes)

    def ts(eng, out, in0, s1, s2, op0, op1, reads, writes):
        if op1 is None:
            P.op(eng, lambda e: e.tensor_scalar(out=out, in0=in0, scalar1=s1, scalar2=None, op0=op0), reads, writes)
        else:
            P.op(eng, lambda e: e.tensor_scalar(out=out, in0=in0, scalar1=s1, scalar2=s2, op0=op0, op1=op1), reads, writes)

    def stt(eng, out, in0, sc, in1, op0, op1, reads, writes):
        P.op(eng, lambda e: e.scalar_tensor_tensor(out=out, in0=in0, scalar=sc, in1=in1, op0=op0, op1=op1), reads, writes)

    def red(eng, out, in_, reads, writes):
        P.op(eng, lambda e: e.tensor_reduce(out=out, in_=in_, axis=AX.X, op=ALU.add), reads, writes)

    def actf(out, in_, func, reads, writes, scale=1.0, bias=0.0, accum=None):
        if accum is None:
            P.act(lambda e: e.activation(out=out, in_=in_, func=func, bias=bias, scale=scale), reads, writes)
        else:
            P.act(lambda e: e.activation(out=out, in_=in_, func=func, bias=bias, scale=scale, accum_out=accum), reads, writes)

    def cp(eng, out, in_, reads, writes):
        if eng == "act":
            P.act(lambda e: e.copy(out=out, in_=in_), reads, writes)
        else:
            P.op(eng, lambda e: e.tensor_copy(out=out, in_=in_), reads, writes)

    def mms(lst, reads, writes):
        def fn(e):
            ins = None
            for it in lst:
                (o, l, r, st, sp_) = it[:5]
                if len(it) > 5:
                    ins = e.matmul(o, l, r, start=st, stop=sp_, skip_group_check=True)
                else:
                    ins = e.matmul(o, l, r, start=st, stop=sp_)
            return ins
        P.pe(fn, reads, writes)

    def trs(lst, reads, writes):
        def fn(e):
            ins = None
            for (o, i, idn) in lst:
                ins = e.transpose(o, i, idn)
            return ins
        P.pe(fn, reads, writes)

    def rstd_from_ss(ssap, n, cols, name):
        actf(ssap, ssap, AF.Ln, [name], [name], scale=1.0 / n, bias=EPS)
        actf(ssap, ssap, AF.Exp, [name], [name], scale=-0.5)

    dmain("sp", identf.c(0, 128), identd.ap(), ["identf"])
    dmain("pool", ident_bf.c(0, 128), identd.ap(), ["ident_bf"])
    dmain("sp", triN.c(0, 128), trid.ap(), ["triN"])
    for r in range(4):
        dmain("pool", tri4.c(r * 128, r * 128 + 128), trid.ap(), ["tri4"])
    dmain("sp", i16.c(0, 256), i16d.ap(), ["i16"])
    P.pool(lambda e: e.memset(onesN.c(0, 128), -1.0 / 16), writes=["onesN"])
    P.pool(lambda e: e.memset(onesf.c(0, 128), 1.0), writes=["onesf"])
    P.pool(lambda e: e.memset(grT.c(0, 128), 1.0), writes=["grT"])
    P.pool(lambda e: e.memset(B5.c(0, 1024), 0.0), writes=["B5"])
    ts("dve", triN.c(0, 128), triN.c(0, 128), -1.0 / 16, None, ALU.mult, None, ["triN"], ["triN"])
    dmain("sp", g1T.v(0, [[8, 2], [1, 8]]), norm1.ap().rearrange("l (k p) -> p l k", p=128), ["g1T"])
    dmain("sp", g2T.v(0, [[8, 2], [1, 8]]), norm2.ap().rearrange("l (k p) -> p l k", p=128), ["g2T"])

    def bc(dr, n):
        return bass.AP(dr, 0, [[0, 128], [1, 2 * n]])
    dmain("sp", Gq.c(0, 128), bc(qn, 64), ["Gq"])
    dmain("sp", Gk.c(0, 128), bc(kn, 64), ["Gk"])
    dmain("sp", Gsub.c(0, 256), bc(subln, 128), ["Gsub"])
    dmain("sp", Ggn.c(0, 256), bc(gnorm, 128), ["Ggn"])
    dmain("sp", LQ.c(0, 512), bc(lqk, 256), ["F5a", "F5b"])
    dmain("sp", CSS.c(0, 32), cosS.ap(), ["CSS"])
    dmain("sp", CSS.c(32, 64), sinS.ap(), ["CSS"])
    dmain("sp", pti.c(0, 16), ptrep.ap(), ["pti"])
    dmain("sp", stat.c(60, 61), rgcol.ap(), ["rg"])
    dmain("pool", wa2a.v(0, [[256, 2], [1, 256]], 0, 16), wa2.ap().rearrange("l r c -> r l c"), ["wa2a"])
    dmain("pool", wa2a.v(0, [[1, 512]], 16, 1), bass.AP(ba, 0, [[0, 1], [1, 512]]), ["wa2a"])
    for l in range(2):
        li = 0.8 - 0.6 * math.exp(-0.3 * l)
        ts("dve", Gsub.c(l * 128, l * 128 + 128), Gsub.c(l * 128, l * 128 + 128), 1.0 - li, None, ALU.mult, None, ["Gsub"], ["Gsub"])
        b0 = l * 256
        tt("dve", F6.c(0, 64), LQ.c(b0, b0 + 64), LQ.c(b0 + 64, b0 + 128), ALU.mult, ["F5a", "F5b"], ["F6"])
        tt("dve", F6.c(64, 128), LQ.c(b0 + 128, b0 + 192), LQ.c(b0 + 192, b0 + 256), ALU.mult, ["F5a", "F5b", "F6"], ["F6"])
        red("dve", stat.c(40, 42), F6.v(0, [[64, 2], [1, 64]]), ["F6"], ["lamtmp"])
        actf(stat.c(40, 42), stat.c(40, 42), AF.Exp, ["lamtmp"], ["lamtmp"])
        tt("dve", stat.c(42, 43), stat.c(41, 42), stat.c(40, 41), ALU.subtract, ["lamtmp"], ["lamtmp2"])
        ts("dve", lamc.c(l, l + 1), stat.c(42, 43), -li, None, ALU.add, None, ["lamtmp2"], ["lamc"])
    cp("dve", ptf.c(0, 16), pti.c(0, 16), ["pti"], ["ptf"])
    ts("dve", ptf.c(0, 16), ptf.c(0, 16), 8.0, stat.c(60, 61), ALU.mult, ALU.add, ["ptf", "rg"], ["ptf"])
    for l_ in range(2):
        for h_ in range(4):
            ts("dve", idxKf.v(l_ * 64 + h_, [[4, 16]]), ptf.c(0, 16), float((l_ * 4 + h_) * NPHYS * 8), None, ALU.add, None, ["ptf", "idxKf"], ["idxKf"])
    cp("dve", idxK.c(0, 128), idxKf.c(0, 128), ["idxKf"], ["idxK"])

    for t in range(NTP):
        dmain("sp", X.c(t * D, t * D + D), xp.ap()[t * 128:(t + 1) * 128, :], ["X%d" % t])
    dmain("sp", X.c(ST * D, ST * D + D), xs.ap(), ["X%d" % ST])

    def bcast_mid(Tn, col, n_g, n_d):
        return Tn.v(col, [[0, n_g], [1, n_d]])

    for l in range(_NL if _STOP >= 1 else 0):
        for kc_ in range(8):
            dmain("pool", AR.c(WIN_O + kc_ * DIN, WIN_O + (kc_ + 1) * DIN), w_in.ap()[l, kc_ * 128:(kc_ + 1) * 128, :], arres(WIN_O + kc_ * DIN, DIN))
        dmain("pool", AR.v(WOUT_O, [[D, 8], [1, D]]), w_out.ap()[l].rearrange("(k p) c -> p k c", p=128), WOUT_R)
        P.pool(lambda e: e.memset(AR.c(VA_O, VA_O + NTP * 528), 1.0), writes=arres(VA_O, NTP * 528))
        P.pool(lambda e: e.memset(Sst.c(0, 512), 0.0), writes=["Sst"])
        P.pool(lambda e: e.memset(Sbf.c(0, 512), 0.0), writes=["Sbf"])

        def win(kc_, c0, c1):
            return AR.c(WIN_O + kc_ * DIN + c0, WIN_O + kc_ * DIN + c1)

        def tile_norm_T(t, gT, dstT_ap_fn, dst_res):
            xr = "X%d" % t
            xa = X.c(t * D, t * D + D)
            actf(B0.c(0, 1024), xa, AF.Square, [xr], ["B0", "ssn"], accum=stat.c(0, 1))
            rstd_from_ss(stat.c(0, 1), D, 1, "ssn")
            ts("dve", B0.c(0, 1024), xa, stat.c(0, 1), None, ALU.mult, None, [xr, "ssn"], ["B0"])
            trs([(PB[0].c(k * 128, k * 128 + 128), B0.c(k * 128, k * 128 + 128), ident_bf.c(0, 128)) for k in range(8)],
                ["B0", "ident_bf"], ["PB0"])
            tt("dve", dstT_ap_fn(), PB[0].v(0, [[128, 8], [1, 128]]), gT.v(l * 8, [[1, 8], [0, 128]]), ALU.mult,
               ["PB0", "g1T", "g2T"], dst_res)

        def qk_norm_rope(Fx, fx, Gx, cs, csn):
            tt("pool", F6.c(0, 512), Fx.c(0, 512), Fx.c(0, 512), ALU.mult, [fx], ["F6"])
            red("dve", stat.c(8, 16), F6.v(0, [[64, 8], [1, 64]]), ["F6"], ["ss8"])
            rstd_from_ss(stat.c(8, 16), 64, 8, "ss8")
            tt("dve", Fx.v(0, [[64, 8], [1, 64]]), Fx.v(0, [[64, 8], [1, 64]]), stat.v(8, [[1, 8], [0, 64]]), ALU.mult, [fx, "ss8"], [fx])
            tt("dve", Fx.v(0, [[64, 8], [1, 64]]), Fx.v(0, [[64, 8], [1, 64]]), bcast_mid(Gx, l * 64, 8, 64), ALU.mult, [fx, "Gq", "Gk"], [fx])
            tt("pool", F6.v(0, [[32, 16], [1, 32]]), Fx.v(0, [[32, 16], [1, 32]]), bcast_mid(cs, 0, 16, 32), ALU.mult, [fx, csn], ["F6"])
            tt("dve", F7.v(0, [[64, 8], [1, 32]]), Fx.v(32, [[64, 8], [1, 32]]), bcast_mid(cs, 32, 8, 32), ALU.mult, [fx, csn], ["F7a"])
            tt("dve", F7.v(32, [[64, 8], [1, 32]]), Fx.v(0, [[64, 8], [1, 32]]), bcast_mid(cs, 32, 8, 32), ALU.mult, [fx, csn], ["F7b"])
            tt("dve", Fx.v(0, [[64, 8], [1, 32]]), F6.v(0, [[64, 8], [1, 32]]), F7.v(0, [[64, 8], [1, 32]]), ALU.subtract, ["F6", "F7a", "F7b"], [fx])
            tt("dve", Fx.v(32, [[64, 8], [1, 32]]), F6.v(32, [[64, 8], [1, 32]]), F7.v(32, [[64, 8], [1, 32]]), ALU.add, ["F6", "F7a", "F7b", fx], [fx])

        def proj_block(c0, n, ps):
            mms([(ps.c(0, n), B1.c(k * 128, k * 128 + 128), win(k, c0, c0 + n), k == 0, k == 7) for k in range(8)],
                ["B1"] + WIN_R, [ps_name(ps)])

        def ps_name(ps):
            for i, p_ in enumerate(PS):
                if p_ is ps:
                    return "PS%d" % i
            return "PB"

        def tile_A(t, cs, csn, is_sample, emit_out=True):
            tile_norm_T(t, g1T, lambda: B1.v(0, [[128, 8], [1, 128]]), ["B1"])
            proj_block(0, 512, PS[0])
            cp("act", F0.c(0, 512), PS[0].c(0, 512), ["PS0"], ["F0"])
            proj_block(512, 512, PS[1])
            cp("act", F1.c(0, 512), PS[1].c(0, 512), ["PS1"], ["F1"])
            proj_block(1024, 512, PS[0])
            cp("act", F2.c(0, 512), PS[0].c(0, 512), ["PS0"], ["F2"])
            if not is_sample:
                cp("dve", AR.v(VA_O + t * 528, [[132, 4], [1, 128]]), PS[0].v(0, [[128, 4], [1, 128]]), ["PS0"], arres(VA_O + t * 528, 528))
                dmaout("sp", vp.ap()[l, t * 128:(t + 1) * 128, :], F2.c(0, 512), ["F2"])
            elif emit_out:
                dmaout("sp", vs.ap()[l], F2.c(0, 512, 0, 16), ["F2"])
            proj_block(1536, 512, PS[1])
            cp("act", B8.c(0, 512), PS[1].c(0, 512), ["PS1"], ["B8"])
            proj_block(2048, 512, PS[0])
            cp("act", B2.c(0, 512), PS[0].c(0, 512), ["PS0"], ["B2"])
            if is_sample:
                cp("dve", F7.c(0, 512, 0, 16), PS[0].c(0, 512, 0, 16), ["PS0"], ["F7a", "F7b"])
                dmain("sp", bass.AP(vscr, 0, [[512, 16], [1, 512]]), F7.c(0, 512, 0, 16), ["vscr"], reads=["F7a", "F7b"])
            proj_block(2560, 512, PS[1])
            actf(B11.c(0, 512), PS[1].c(0, 512), AF.Silu, ["PS1"], ["B11"])
            mms([(PS[0].c(0, 128, 0, 16), win(k, 3072, 3088), B1.c(k * 128, k * 128 + 128), k == 0, k == 7) for k in range(8)],
                ["B1"] + WIN_R, ["PS0"])
            cp("act", grT.c(0, 128, 0, 16), PS[0].c(0, 128, 0, 16), ["PS0"], ["grT"])
            mms([(PS[1].c(0, 256), grT.c(0, 128, 0, 17), wa2a.c(l * 256, l * 256 + 256, 0, 17), True, True)], ["grT", "wa2a"], ["PS1"])
            actf(F5.c(0, 256), PS[1].c(0, 256), AF.Exp, ["PS1"], ["F5a"], scale=-1.0)
            actf(F5.c(0, 256), F5.c(0, 256), AF.Ln, ["F5a"], ["F5a"], bias=1.0)
            qk_norm_rope(F0, "F0", Gq, cs, csn)
            qk_norm_rope(F1, "F1", Gk, cs, csn)
            if not is_sample:
                dmaout("sp", kp.ap()[l, t * 128:(t + 1) * 128, :], F1.c(0, 512), ["F1"])
                cp("pool", B3.c(0, 512), F1.c(0, 512), ["F1"], ["B3"])
                trs([(PB[1].c(h * 128, h * 128 + 128), B3.c(h * 128, h * 128 + 128), ident_bf.c(0, 128)) for h in range(4)],
                    ["B3", "ident_bf"], ["PB1"])
                cp("act", AR.v(KT_O + t * 128, [[S, 4], [1, 128]]), PB[1].v(0, [[128, 4], [1, 128]]), ["PB1"], arres(KT_O, 4 * S))
                cp("pool", B3.c(0, 512), F0.c(0, 512), ["F0"], ["B3"])
                trs([(PB[1].c(h * 128, h * 128 + 128), B3.c(h * 128, h * 128 + 128), ident_bf.c(0, 128)) for h in range(4)],
                    ["B3", "ident_bf"], ["PB1"])
                cp("act", B5.v(0, [[256, 4], [1, 128]], 0, 64), PB[1].v(0, [[128, 4], [1, 128]], 0, 64), ["PB1"], ["B5"])
                cp("act", B5.v(128, [[256, 4], [1, 128]], 64, 64), PB[1].v(0, [[128, 4], [1, 128]], 64, 64), ["PB1"], ["B5"])
            elif emit_out:
                dmaout("sp", ks.ap()[l], F1.c(0, 512, 0, 16), ["F1"])

        def merge_head_norm(src_ps, Gx, goff, dst_col, extra_mul, nrows=128, srcres=("PS3",)):
            r = nrows
            cp("act", F6.c(0, 512, 0, r), src_ps, list(srcres) + ["F6"], ["F6"])
            tt("pool", F7.c(0, 512, 0, r), F6.c(0, 512, 0, r), F6.c(0, 512, 0, r), ALU.mult, ["F6"], ["F7a", "F7b"])
            red("dve", stat.c(16, 20, 0, r), F7.v(0, [[128, 4], [1, 128]], 0, r), ["F7a", "F7b"], ["ss4"])
            rstd_from_ss(stat.c(16, 20, 0, r), 128, 4, "ss4")
            tt("dve", F6.v(0, [[128, 4], [1, 128]], 0, r), F6.v(0, [[128, 4], [1, 128]], 0, r), stat.v(16, [[1, 4], [0, 128]], 0, r), ALU.mult, ["F6", "ss4"], ["F6"])
            if extra_mul is None:
                tt("dve", B0.v(dst_col, [[128, 4], [1, 128]], 0, r), F6.v(0, [[128, 4], [1, 128]], 0, r), Gx.v(goff, [[0, 4], [1, 128]], 0, r), ALU.mult, ["F6", "Gsub", "Ggn"], ["B0"])
            else:
                tt("dve", F6.v(0, [[128, 4], [1, 128]], 0, r), F6.v(0, [[128, 4], [1, 128]], 0, r), Gx.v(goff, [[0, 4], [1, 128]], 0, r), ALU.mult, ["F6", "Gsub", "Ggn"], ["F6"])
                tt("dve", B0.c(dst_col, dst_col + 512, 0, r), F6.c(0, 512, 0, r), extra_mul, ALU.mult, ["F6", "B11"], ["B0"])

        def tile_C(t):
            trs([(PB[0].c(k * 128, k * 128 + 128), B0.c(k * 128, k * 128 + 128), ident_bf.c(0, 128)) for k in range(8)],
                ["B0", "ident_bf"], ["PB0"])
            cp("act", B1.c(0, 1024), PB[0].c(0, 1024), ["PB0"], ["B1"])
            for cb in range(2):
                ps = PS[cb]
                mms([(ps.c(0, 512), B1.c(k * 128, k * 128 + 128), AR.c(WOUT_O + k * D + cb * 512, WOUT_O + k * D + cb * 512 + 512), k == 0, k == 7) for k in range(8)],
                    ["B1"] + WOUT_R, ["PS%d" % cb])
                xa = X.c(t * D + cb * 512, t * D + cb * 512 + 512)
                tt("dve", xa, xa, ps.c(0, 512), ALU.add, ["X%d" % t, "PS%d" % cb], ["X%d" % t])

        tile_A(ST, CSS, "CSS", True)
        for (src, o) in ((F0, 0), (F1, 128), (F2, 256)):
            P.dma("sp", (lambda s_, o_: (lambda e: e.dma_start(out=bass.AP(ag1_in, o_, [[1536, 16], [384, 4], [1, 128]]),
                                                                in_=s_.v(0, [[128, 4], [1, 128]], 0, 16))))(src, o),
                  reads=["F0", "F1", "F2"], writes=["ag1_in"])
        SR = arfres(0, 12288)
        S0 = ARFt

        for t in range(NTP if _STOP >= 2 else 0):
            dmain("sp", CS.c(0, 32), cosP.ap()[t * 128:(t + 1) * 128, :], ["CS"])
            dmain("sp", CS.c(32, 64), sinP.ap()[t * 128:(t + 1) * 128, :], ["CS"])
            tile_A(t, CS, "CS", False)
            KTR = arres(KT_O, 4 * S)
            accs = []
            for g in range(8):
                accs.append((PS[3 + g // 3], (g % 3) * 129))
            def ptbuf(j, half):
                if j % 2 == 0:
                    return PT, "PT%d" % half
                return B9, ("B9a" if half == 0 else "B9b")

            def emit_st(j, half):
                psi = 2 if half == 0 else 1
                ps = PS[psi]
                lst = []
                for hh in range(2):
                    h = half * 2 + hh
                    lst.append((ps.c(hh * 256, hh * 256 + 256),
                                AR.c(KT_O + h * S + j * 128, KT_O + h * S + j * 128 + 128),
                                B5.c(h * 256, h * 256 + 256), True, True))
                mms(lst, KTR + ["B5"], ["PS%d" % psi])

            def emit_exp(j, half):
                psi = 2 if half == 0 else 1
                buf, bn = ptbuf(j, half)
                pt = buf.c(half * 512, half * 512 + 512)
                actf(pt, PS[psi].c(0, 512), AF.Exp, ["PS%d" % psi], [bn], scale=0.125)
                if j == t:
                    tt("pool", pt, pt, tri4.c(0, 512), ALU.mult, [bn, "tri4"], [bn])

            def emit_pv(j, half):
                buf, bn = ptbuf(j, half)
                lst = []
                for hh in range(2):
                    h = half * 2 + hh
                    for m in range(2):
                        bank, col = accs[h * 2 + m]
                        lst.append((bank.c(col, col + 129),
                                    buf.c(half * 512 + (hh * 2 + m) * 128, half * 512 + (hh * 2 + m) * 128 + 128),
                                    AR.c(VA_O + j * 528 + h * 132, VA_O + j * 528 + h * 132 + 129), (j == 0 and (h * 2 + m) in (0, 3, 6)), j == t, 1))
                mms(lst, [bn] + arres(VA_O + j * 528, 528), ["PS3", "PS4", "PS5"])

            emit_st(0, 0)
            emit_st(0, 1)
            for j in range(t + 1):
                emit_exp(j, 0)
                emit_exp(j, 1)
                if j + 1 <= t:
                    emit_st(j + 1, 0)
                    emit_st(j + 1, 1)
                emit_pv(j, 0)
                emit_pv(j, 1)
            for g in range(8):
                bank, col = accs[g]
                P.dve(lambda e, b_=bank, c_=col, g_=g: e.reciprocal(out=stat.c(24 + g_, 25 + g_), in_=b_.c(c_ + 128, c_ + 129)),
                      ["PS3", "PS4", "PS5"], ["rden"])
            for h in range(4):
                ts("dve", stat.c(24 + 2 * h + 1, 24 + 2 * h + 2), stat.c(24 + 2 * h + 1, 24 + 2 * h + 2), lamc.c(l, l + 1), None, ALU.mult, None, ["rden", "lamc"], ["rden"])
            for h in range(4):
                b1, c1 = accs[2 * h]
                b2, c2 = accs[2 * h + 1]
                ts("dve", F0.c(h * 128, h * 128 + 128), b1.c(c1, c1 + 128), stat.c(24 + 2 * h, 25 + 2 * h), None, ALU.mult, None, ["PS3", "PS4", "PS5", "rden"], ["F0"])
                stt("dve", F0.c(h * 128, h * 128 + 128), b2.c(c2, c2 + 128), stat.c(25 + 2 * h, 26 + 2 * h), F0.c(h * 128, h * 128 + 128), ALU.mult, ALU.add, ["PS3", "PS4", "PS5", "rden", "F0"], ["F0"])
            merge_head_norm(F0.c(0, 512), Gsub, l * 128, 0, None, srcres=("F0",))
            mms([(PS[2].c(0, 256), triN.c(0, 128), F5.c(0, 256), True, True),
                 (PS[2].c(256, 512), onesN.c(0, 128), F5.c(0, 256), True, True)], ["triN", "onesN", "F5a"], ["PS2"])
            cp("act", F7.c(0, 256), PS[2].c(0, 256), ["PS2"], ["F7a", "F7b"])
            tt("dve", F5.c(256, 512), PS[2].c(256, 512), F7.c(0, 256), ALU.subtract, ["PS2", "F7a", "F7b"], ["F5b"])
            actf(F5.c(256, 512), F5.c(256, 512), AF.Exp, ["F5b"], ["F5b"])
            tt("dve", B7.c(0, 256), B8.c(256, 512), F5.c(256, 512), ALU.mult, ["B8", "F5b"], ["B7"])
            mms([(PS[2].c(h * 128, h * 128 + 128, 0, 64), F5.c(h * 64, h * 64 + 64), triN.c(0, 128), True, True) for h in range(4)],
                ["F5a", "triN"], ["PS2"])
            actf(F6.c(0, 512, 0, 64), PS[2].c(0, 512, 0, 64), AF.Exp, ["PS2", "F6"], ["F6"])
            actf(F7.c(0, 512, 0, 64), PS[2].c(0, 512, 0, 64), AF.Exp, ["PS2", "F7a", "F7b"], ["F7a", "F7b"], scale=-1.0)
            trs([(PB[1].c(i * 128, i * 128 + 128, 0, 64), B8.c(i * 64, i * 64 + 64), ident_bf.c(0, 128)) for i in range(8)],
                ["B8", "ident_bf"], ["PB1"])
            stt("dve", B9.c(0, 512, 0, 64), PB[1].c(0, 512, 0, 64), 0.125, F6.c(0, 512, 0, 64), ALU.mult, ALU.mult, ["PB1", "F6"], ["B9a"])
            tt("dve", B9.c(512, 1024, 0, 64), PB[1].c(512, 1024, 0, 64), F7.c(0, 512, 0, 64), ALU.mult, ["PB1", "F7a", "F7b"], ["B9b"])
            mms([(PS[2].c(h * 128, h * 128 + 128), B9.c(512 + h * 128, 512 + h * 128 + 128, 0, 64), B9.c(h * 128, h * 128 + 128, 0, 64), True, True) for h in range(4)],
                ["B9a", "B9b"], ["PS2"])
            tt("dve", B10.c(0, 512), PS[2].c(0, 512), tri4.c(0, 512), ALU.mult, ["PS2", "tri4"], ["B10"])
            lst = []
            for h in range(4):
                lst.append((PS[3].c(h * 128, h * 128 + 128), B9.c(h * 128, h * 128 + 128, 0, 64), Sbf.c(h * 128, h * 128 + 128, 0, 64), True, False))
                lst.append((PS[3].c(h * 128, h * 128 + 128), B10.c(h * 128, h * 128 + 128), B2.c(h * 128, h * 128 + 128), False, True))
            mms(lst, ["B9a", "Sbf", "B10", "B2"], ["PS3"])
            mms([(PS[4].c(h * 128, h * 128 + 128, 0, 64), B7.c(h * 64, h * 64 + 64), B2.c(h * 128, h * 128 + 128), True, True) for h in range(4)],
                ["B7", "B2"], ["PS4"])
            for h in range(4):
                stt("dve", Sst.c(h * 128, h * 128 + 128, 0, 64), Sst.c(h * 128, h * 128 + 128, 0, 64), F6.c(h * 128 + 127, h * 128 + 128, 0, 64),
                    PS[4].c(h * 128, h * 128 + 128, 0, 64), ALU.mult, ALU.add, ["Sst", "F6", "PS4"], ["Sst"])
            cp("pool", Sbf.c(0, 512, 0, 64), Sst.c(0, 512, 0, 64), ["Sst"], ["Sbf"])
            merge_head_norm(PS[3].c(0, 512), Ggn, l * 128, 512, B11.c(0, 512))
            tile_C(t)
        dmaout("sp", gp.ap()[l].rearrange("h k v -> k h v"), Sst.v(0, [[128, 4], [1, 128]], 0, 64), ["Sst"])

        if _STOP < 3:
            continue
        tile_A_redo = True
        tile_A(ST, CSS, "CSS", True, emit_out=False)
        S0c, VBc = 0, 4096
        S0r = arfres(S0c, 4096)
        VBr = arfres(VBc, 4096)
        dmain("sp", ARFt.v(S0c, [[256, 16], [128, 2], [1, 128]]),
              bass.AP(sg, l * 16 * 32768, [[128, 128], [32768, 16], [16384, 2], [1, 128]]), S0r)
        for h2 in range(2):
            dmain("sp", ARFt.v(VBc, [[256, 16], [128, 2], [1, 128]], h2 * 64, 64),
                  bass.AP(vscr, h2 * 128, [[0, 64], [512, 16], [256, 2], [1, 128]]), VBr, reads=["vscr"])
        actf(F6.c(0, 256, 0, 16), F5.c(0, 256, 0, 16), AF.Exp, ["F5a", "F6"], ["F6"], scale=-1.0 / 16)
        cp("dve", F6.c(256, 512, 0, 16), B8.c(256, 512, 0, 16), ["B8", "F6"], ["F6"])
        ts("dve", F5.c(256, 512, 0, 16), B8.c(0, 256, 0, 16), 0.125, None, ALU.mult, None, ["B8"], ["F5b"])
        lst = []
        for qi, (src, c0) in enumerate(((F6, 0), (F6, 256), (F5, 256))):
            for hp in range(2):
                lst.append((PS[2].c((qi * 2 + hp) * 16, (qi * 2 + hp) * 16 + 16), src.c(c0 + hp * 128, c0 + hp * 128 + 128, 0, 16), identf.c(0, 16, 0, 16)))
        trs(lst, ["F6", "F5b", "identf"], ["PS2"])
        cp("act", F2.c(0, 96), PS[2].c(0, 96), ["PS2", "F2"], ["F2"])
        def s_view(c0):
            return ARFt.v(c0, [[128, 2], [256, 16], [1, 128]])

        def col_view(qi):
            return F2.v(qi * 32, [[16, 2], [1, 16], [0, 128]])
        tt("dve", s_view(S0c), s_view(S0c), col_view(0), ALU.mult, S0r + ["F2"], S0r)
        tt("pool", s_view(VBc), s_view(VBc), col_view(1), ALU.mult, VBr + ["F2"], VBr)
        tt("dve", ARFt.c(S0c, S0c + 4096), ARFt.c(S0c, S0c + 4096), ARFt.c(VBc, VBc + 4096), ALU.add, S0r + VBr, S0r)
        dmaout("sp", bass.AP(gs, l * 16 * 32768, [[128, 128], [32768, 16], [16384, 2], [1, 128]]),
               ARFt.v(S0c, [[256, 16], [128, 2], [1, 128]]), S0r)
        SBc = 16384 + 0
        SBr = arres(SBc, 4096)
        cp("pool", AR.c(SBc, SBc + 4096), ARFt.c(S0c, S0c + 4096), S0r, SBr)
        QDc = SBc + 4096
        QDr = arres(QDc, 512)
        tt("dve", AR.v(QDc, [[256, 2], [16, 16], [1, 16]]), F2.v(64, [[16, 2], [1, 16], [0, 16]]), i16.v(0, [[0, 2], [16, 16], [1, 16]]), ALU.mult,
           ["F2", "i16"], QDr)
        lst = []
        for h in range(4):
            hp, h2 = h // 2, h % 2
            for b in range(16):
                lst.append((PS[3 + h2].c(h * 128, h * 128 + 128, 0, 16),
                            AR.c(QDc + hp * 256 + b * 16, QDc + hp * 256 + b * 16 + 16, h2 * 64, 64),
                            AR.c(SBc + b * 256 + hp * 128, SBc + b * 256 + hp * 128 + 128, h2 * 64, 64), b == 0, b == 15))
        mms([x for x in lst if x[0] is not None], QDr + SBr, ["PS3", "PS4"])
        for h in range(4):
            cp("act", F0.c(h * 128, h * 128 + 128, 0, 16), PS[3 + h % 2].c(h * 128, h * 128 + 128, 0, 16), ["PS3", "PS4", "F0"], ["F0"])
        merge_head_norm(F0.c(0, 512, 0, 16), Ggn, l * 128, 512, B11.c(0, 512, 0, 16), nrows=16, srcres=("F0",))

        if _STOP < 4:
            continue
        QKV = F0
        dmain("sp", F0.c(0, 384, 0, 64), bass.AP(ag1_in, 0, [[384, 64], [1, 384]]), ["F0"], reads=["ag1_in"])
        KtO = [0, 2048]
        VtO = [4096, 6144]
        PRO = 8192
        PACC = F1
        QBt = [F7, F6]
        for i in range(64):
            bsel = i % 2
            ko, vo = KtO[bsel], VtO[bsel]
            kr, vr = arfres(ko, 2048), arfres(vo, 2048)
            qb = QBt[bsel]
            qbn = "F7a" if bsel == 0 else "F6"
            qbw = ["F7a", "F7b"] if bsel == 0 else ["F6"]
            dmain("sp", qb.c(0, 128), bass.AP(ag1_in, i * 384, [[0, 128], [1, 128]]), qbw, reads=["ag1_in"])
            P.dma("pool", (lambda i_, ko_: (lambda e: e.indirect_dma_start(
                out=ARFt.c(ko_, ko_ + 2048), out_offset=None, in_=kc.ap(),
                in_offset=bass.IndirectOffsetOnAxis(ap=idxK.c(i_, i_ + 1), axis=0))))(l * 64 + i, ko),
                reads=["idxK"], writes=kr)
            P.dma("pool", (lambda i_, vo_: (lambda e: e.indirect_dma_start(
                out=ARFt.c(vo_, vo_ + 2048), out_offset=None, in_=vc.ap(),
                in_offset=bass.IndirectOffsetOnAxis(ap=idxK.c(i_, i_ + 1), axis=0))))(l * 64 + i, vo),
                reads=["idxK"], writes=vr)
            pr = arfres(PRO, 2048)
            tt("pool" if i % 2 else "dve", ARFt.v(PRO, [[128, 16], [1, 128]]), ARFt.v(ko, [[128, 16], [1, 128]]), qb.v(0, [[0, 16], [1, 128]]), ALU.mult,
               kr + qbw, pr)
            red("dve", sc32.c(0, 32), ARFt.v(PRO, [[64, 32], [1, 64]]), pr, ["sc"])
            actf(pbuf.c(0, 32), sc32.c(0, 32), AF.Exp, ["sc"], ["pbuf"], scale=0.125)
            red("dve", PACC.c(i * 2, i * 2 + 2), pbuf.v(0, [[1, 2], [2, 16]]), ["pbuf"], ["F1"])
            mms([(PS[4].c(i * 2, i * 2 + 2), ARFt.c(vo + r * 128, vo + r * 128 + 128), pbuf.c(r * 2, r * 2 + 2), r == 0, r == 15) for r in range(16)],
                vr + ["pbuf"], ["PS4"])
        cp("act", F2.c(0, 128), PS[4].c(0, 128), ["PS4", "F2"], ["F2"])
        trs([(PS[2].c(m * 128, m * 128 + 128, 0, 64), F2.v(m, [[2, 64]]), identf.c(0, 128)) for m in range(2)], ["F2", "identf"], ["PS2"])
        mms([(PS[2].c(256 + m, 257 + m, 0, 64), PACC.v(m, [[2, 64]]), onesf.c(0, 1), True, True) for m in range(2)], ["F1", "onesf"], ["PS2"])
        tt("dve", F5.c(0, 128, 0, 64), F0.c(0, 128, 0, 64), F0.c(128, 256, 0, 64), ALU.mult, ["F0"], ["F5a"])
        red("dve", stat.c(32, 34, 0, 64), F5.v(0, [[64, 2], [1, 64]], 0, 64), ["F5a"], ["snew"])
        actf(stat.c(32, 34, 0, 64), stat.c(32, 34, 0, 64), AF.Exp, ["snew"], ["snew"], scale=0.125)
        tt("dve", stat.c(34, 36, 0, 64), PS[2].c(256, 258, 0, 64), stat.c(32, 34, 0, 64), ALU.add, ["PS2", "snew"], ["den"])
        P.dve(lambda e: e.reciprocal(out=stat.c(34, 36, 0, 64), in_=stat.c(34, 36, 0, 64)), ["den"], ["den"])
        ts("dve", stat.c(35, 36, 0, 64), stat.c(35, 36, 0, 64), lamc.c(l, l + 1, 0, 64), None, ALU.mult, None, ["den", "lamc"], ["den"])
        for m in range(2):
            stt("dve", F5.c(256 + m * 128, 384 + m * 128, 0, 64), F0.c(256, 384, 0, 64), stat.c(32 + m, 33 + m, 0, 64), PS[2].c(m * 128, m * 128 + 128, 0, 64),
                ALU.mult, ALU.add, ["F0", "snew", "PS2", "F5b"], ["F5b"])
        ts("dve", F5.c(0, 128, 0, 64), F5.c(256, 384, 0, 64), stat.c(34, 35, 0, 64), None, ALU.mult, None, ["F5b", "den", "F5a"], ["F5a"])
        stt("dve", F5.c(0, 128, 0, 64), F5.c(384, 512, 0, 64), stat.c(35, 36, 0, 64), F5.c(0, 128, 0, 64), ALU.mult, ALU.add, ["F5b", "den", "F5a"], ["F5a"])
        dmain("sp", ag2_in.ap(), F5.c(0, 128, 0, 64), ["ag2_in"], reads=["F5a"])
        dmain("sp", F0.c(0, 512, 0, 16), bass.AP(ag2_in, 0, [[512, 16], [1, 512]]), ["F0"], reads=["ag2_in"])
        merge_head_norm(F0.c(0, 512, 0, 16), Gsub, l * 128, 0, None, nrows=16, srcres=("F0",))
        tile_C(ST)

        if _STOP < 5:
            continue
        dmain("pool", AR.v(WD_O, [[D, 22], [1, D]]), w_down.ap()[l].rearrange("(j p) c -> p j c", p=128), arres(WD_O, 22528))
        WDR = arres(WD_O, 22528)
        groups = []
        t0 = 0
        per = -(-NT // 4)
        while t0 < NT:
            groups.append(list(range(t0, min(NT, t0 + per))))
            t0 += per
        wq = 0
        blk = 0
        for grp in groups:
            ntok = len(grp) * 128
            N2R = arres(N2T_O, 8 * GT)
            for gi, t in enumerate(grp):
                tile_norm_T(t, g2T, (lambda gi_: (lambda: AR.v(N2T_O + gi_ * 128, [[GT, 8], [1, 128]])))(gi), N2R)
            HTR = arres(HT_O, 22 * GT)
            for j in range(22):
                wb = WGU_O + (wq % 3) * 2048
                wr = arres(wb, 2048)
                wq += 1
                for hf in range(2):
                    dmain("pool", AR.v(wb + hf * 128, [[256, 8], [1, 128]]),
                          bass.AP(w_gu, l * D * 2 * DFF + hf * DFF + j * 128, [[2 * DFF, 128], [128 * 2 * DFF, 8], [1, 128]]), wr)
                c0 = 0
                while c0 < ntok:
                    n = min(512, ntok - c0)
                    blk += 1
                    ia, ib, Fs, fsn = (0, 1, F0, "F0") if blk % 2 == 0 else (4, 5, F1, "F1")
                    mms([(PS[ia].c(0, n), AR.c(wb + k * 256, wb + k * 256 + 128), AR.c(N2T_O + k * GT + c0, N2T_O + k * GT + c0 + n), k == 0, k == 7) for k in range(8)],
                        wr + N2R, ["PS%d" % ia])
                    mms([(PS[ib].c(0, n), AR.c(wb + k * 256 + 128, wb + k * 256 + 256), AR.c(N2T_O + k * GT + c0, N2T_O + k * GT + c0 + n), k == 0, k == 7) for k in range(8)],
                        wr + N2R, ["PS%d" % ib])
                    actf(Fs.c(0, n), PS[ia].c(0, n), AF.Silu, ["PS%d" % ia, fsn], [fsn])
                    tt("dve", AR.c(HT_O + j * GT + c0, HT_O + j * GT + c0 + n), Fs.c(0, n), PS[ib].c(0, n), ALU.mult, [fsn, "PS%d" % ib], arres(HT_O + j * GT + c0, n))
                    c0 += n
            for gi, t in enumerate(grp):
                for cb in range(2):
                    ps = PS[2 + cb]
                    mms([(ps.c(0, 512), AR.c(HT_O + j * GT + gi * 128, HT_O + j * GT + gi * 128 + 128), AR.c(WD_O + j * D + cb * 512, WD_O + j * D + cb * 512 + 512), j == 0, j == 21) for j in range(22)],
                        HTR + WDR, ["PS%d" % (2 + cb)])
                    xa = X.c(t * D + cb * 512, t * D + cb * 512 + 512)
                    tt("dve", xa, xa, ps.c(0, 512), ALU.add, ["X%d" % t, "PS%d" % (2 + cb)], ["X%d" % t])
                if l == _NL - 1:
                    if t < NTP:
                        dmaout("sp", yp.ap()[t * 128:(t + 1) * 128, :], X.c(t * D, t * D + D), ["X%d" % t])
                    else:
                        dmaout("sp", ys.ap(), X.c(t * D, t * D + D, 0, 16), ["X%d" % t])

    if _STOP < 5:
        for t in range(NTP):
            dmaout("sp", yp.ap()[t * 128:(t + 1) * 128, :], X.c(t * D, t * D + D), ["X%d" % t])
        dmaout("sp", ys.ap(), X.c(ST * D, ST * D + D, 0, 16), ["X%d" % ST])
    P.finalize(sems)
    with nc.Block() as block:
        @block.tensor
        def _(e):
            P.emit("pe", e)

        @block.scalar
        def _(e):
            P.emit("act", e)

        @block.vector
        def _(e):
            P.emit("dve", e)

        @block.gpsimd
        def _(e):
            P.emit("pool", e)

        @block.sync
        def _(e):
            P.emit("sp", e)
    es.close()
    return nc


_DEBUG_HOOK = None
_USE_CC = True
_STOP = 99
_OPLIMIT = 10 ** 9
_DBG_N = 99
_NL = 2
def _consts(S, past):
    half = 32
    freqs = (10000.0 ** (-np.arange(half, dtype=np.float32) / half)).astype(np.float32)
    pos = np.arange(S, dtype=np.float32)
    ang = pos[:, None] * freqs[None, :]
    angs = np.full((128, 1), float(past), np.float32) * freqs[None, :]
    ident = np.eye(128, dtype=np.float32)
    tri = np.triu(np.ones((128, 128), np.float32))
    i16 = np.tile(np.eye(16, dtype=np.float32).reshape(1, 256), (128, 1))
    rg = (np.arange(128) % 8).astype(np.float32).reshape(128, 1)
    return dict(cosP=np.cos(ang).astype(np.float32), sinP=np.sin(ang).astype(np.float32),
                cosS=np.cos(angs).astype(np.float32), sinS=np.sin(angs).astype(np.float32),
                identd=ident, trid=tri, i16d=i16, rgcol=rg)


def kernel(x_prompt, x_sample, cache_k, cache_v, state_gla, page_table, norm1, w_in, q_norm, k_norm,
           lambda_qk, subln, w_a2, b_a, gla_norm, w_out, norm2, w_gu, w_down):
    f = lambda a: np.ascontiguousarray(np.asarray(a, dtype=np.float32))
    x_prompt, x_sample = f(x_prompt), f(x_sample)
    cache_k, cache_v, state_gla = np.asarray(cache_k), np.asarray(cache_v), f(state_gla)
    page_table = np.asarray(page_table).astype(np.int32)
    S = x_prompt.shape[1]
    NPHYS = cache_k.shape[1]
    n_cores = 8
    nc = bass.Bass("TRN2", target_bir_lowering=False)
    build(nc, S, use_cc=_USE_CC, NPHYS=NPHYS)
    cst = _consts(S, 2048)
    shared = dict(w_in=f(w_in), w_out=f(w_out), w_gu=f(w_gu), w_down=f(w_down), norm1=f(norm1), norm2=f(norm2),
                  qn=f(q_norm), kn=f(k_norm), lqk=f(lambda_qk).reshape(2, 256), subln=f(subln), wa2=f(w_a2),
                  ba=f(b_a), gnorm=f(gla_norm))
    shared.update(cst)
    kh = np.ascontiguousarray(cache_k.reshape(2, NPHYS, 128, 4, 128).transpose(0, 3, 1, 2, 4)).reshape(8 * NPHYS * 8, 2048)
    vh = np.ascontiguousarray(cache_v.transpose(0, 3, 1, 2, 4)).reshape(8 * NPHYS * 8, 2048)
    in_maps = []
    for c in range(n_cores):
        half, h = c // 4, c % 4
        m = dict(shared)
        m["xp"] = x_prompt[c]
        xs = np.zeros((128, D), np.float32)
        xs[:16] = x_sample[16 * c:16 * c + 16, 0]
        m["xs"] = xs
        m["kc"] = kh
        m["vc"] = vh
        m["sg"] = np.ascontiguousarray(state_gla[:, 16 * c:16 * c + 16])
        pt = page_table[16 * c:16 * c + 16]
        m["ptrep"] = np.ascontiguousarray(np.repeat(pt.T, 8, axis=0)).astype(np.int32)
        in_maps.append(m)
    if _DEBUG_HOOK is not None:
        return _DEBUG_HOOK(nc, in_maps)
    res = run_bass_kernel_spmd(nc, in_maps, core_ids=list(range(n_cores))).results
    yp = np.stack([r["yp"] for r in res])
    ys = np.concatenate([r["ys"] for r in res])[:, None, :]
    kp = np.stack([r["kp"] for r in res], axis=1).reshape(2, 8, S, 4, 2, 64)
    vp = np.stack([r["vp"] for r in res], axis=1).reshape(2, 8, S, 4, 128)
    gp = np.stack([r["gp"] for r in res], axis=1)
    ks = np.concatenate([r["ks"] for r in res], axis=1).reshape(2, 128, 1, 4, 2, 64)
    vs = np.concatenate([r["vs"] for r in res], axis=1).reshape(2, 128, 1, 4, 128)
    gs = np.concatenate([r["gs"] for r in res], axis=1)
    return (yp.astype(np.float32), ys.astype(np.float32), kp.astype(np.float32), vp.astype(np.float32),
            gp.astype(np.float32), ks.astype(np.float32), vs.astype(np.float32), gs.astype(np.float32))
```

```python
import contextlib
import math
import numpy as np
import concourse.bass as bass
import concourse.mybir as mb
from concourse.bass_utils import run_bass_kernel_spmd

F32 = mb.dt.float32
BF = mb.dt.bfloat16
I32 = mb.dt.int32
AF = mb.ActivationFunctionType
ALU = mb.AluOpType
AX = mb.AxisListType

D = 1024
DIN = 3088
DFF = 2816
NPHYS = 2560
EPS = 1e-6
ENGS = ("pe", "act", "dve", "pool", "sp")


class Op:
    __slots__ = ("eng", "fn", "dma", "deps", "seq", "signal", "sigidx", "dsem", "dval",
                 "waits", "gidx", "slotwait", "cc")

    def __init__(self, eng, fn, dma):
        self.eng = eng
        self.fn = fn
        self.dma = dma
        self.deps = set()
        self.signal = False
        self.sigidx = 0
        self.dsem = None
        self.dval = 0
        self.waits = []
        self.slotwait = None
        self.cc = False


class Prog:
    def __init__(self, nc):
        self.nc = nc
        self.ops = []
        self.eops = {e: [] for e in ENGS}
        self.last_writer = {}
        self.readers = {}
        self.ndma = {"sp": 8, "pool": 8, "act": 4}
        self.out_dmas = []

    def op(self, eng, fn, reads=(), writes=(), dma=False, is_output=False):
        if len(self.ops) >= _OPLIMIT and not is_output:
            return None
        o = Op(eng, fn, dma)
        deps = set()
        pr = [r for r in reads if r.startswith("PS") or r.startswith("PB")]
        if pr:
            writes = list(writes) + [r for r in pr if r not in writes]
        for r in reads:
            w = self.last_writer.get(r)
            if w is not None:
                deps.add(w)
        for r in writes:
            w = self.last_writer.get(r)
            if w is not None:
                deps.add(w)
            for rd in self.readers.get(r, ()):
                deps.add(rd)
        for r in reads:
            self.readers.setdefault(r, []).append(o)
        for r in writes:
            self.last_writer[r] = o
            self.readers[r] = []
        deps.discard(o)
        o.deps = deps
        o.gidx = len(self.ops)
        o.seq = len(self.eops[eng])
        self.ops.append(o)
        self.eops[eng].append(o)
        if is_output:
            self.out_dmas.append(o)
        return o

    def pe(self, fn, reads=(), writes=()):
        return self.op("pe", fn, reads, writes)

    def act(self, fn, reads=(), writes=()):
        return self.op("act", fn, reads, writes)

    def dve(self, fn, reads=(), writes=()):
        return self.op("dve", fn, reads, writes)

    def pool(self, fn, reads=(), writes=()):
        return self.op("pool", fn, reads, writes)

    def dma(self, q, fn, reads=(), writes=(), is_output=False):
        return self.op(q, fn, reads, writes, dma=True, is_output=is_output)

    def ccop(self, fn, reads=(), writes=()):
        o = self.op("pool", fn, reads, writes, dma=True)
        if o is not None:
            o.cc = True
        return o

    def finalize(self, sems):
        fin = Op("sp", None, False)
        fin.deps = set(self.out_dmas)
        fin.gidx = len(self.ops)
        fin.seq = len(self.eops["sp"])
        self.ops.append(fin)
        self.eops["sp"].append(fin)
        cnt = {}
        lastval = {}
        ncc = 0
        for o in self.ops:
            if o.dma and o.cc:
                o.dsem = "cc%d" % ncc
                ncc += 1
                o.dval = 1
                continue
            if o.dma:
                q = o.eng
                i = cnt.get(q, 0)
                cnt[q] = i + 1
                key = "d_%s_%d" % (q, i % self.ndma[q])
                o.dsem = key
                prev = lastval.get(key, 0)
                o.dval = prev + 16
                lastval[key] = o.dval
                if prev > 0:
                    o.slotwait = (key, prev)
        known = {e: {p: -1 for p in ENGS} for e in ENGS}
        known_dma = {e: {} for e in ENGS}
        for o in self.ops:
            e = o.eng
            best = {}
            for d in o.deps:
                if d.dma:
                    k = known_dma[e].get(d.dsem, 0)
                    if d.dval > k:
                        cur = best.get(("dma", d.dsem))
                        if cur is None or d.dval > cur.dval:
                            best[("dma", d.dsem)] = d
                else:
                    if d.seq > known[e][d.eng]:
                        cur = best.get(("eng", d.eng))
                        if cur is None or d.seq > cur.seq:
                            best[("eng", d.eng)] = d
            if o.slotwait is not None:
                key, prev = o.slotwait
                if known_dma[e].get(key, 0) >= prev:
                    o.slotwait = None
                else:
                    known_dma[e][key] = prev
            for k, d in best.items():
                if k[0] == "dma":
                    known_dma[e][d.dsem] = d.dval
                else:
                    known[e][d.eng] = d.seq
                    d.signal = True
            o.waits = [best[k] for k in sorted(best.keys())]
        for e in ENGS:
            n = 0
            for o in self.eops[e]:
                if o.signal and not o.dma:
                    n += 1
                    o.sigidx = n
        self.sems = sems

    def emit(self, eng, e):
        sems = self.sems
        for o in self.eops[eng]:
            if o.slotwait is not None:
                e.wait_ge(sems[o.slotwait[0]], o.slotwait[1])
            for d in o.waits:
                if d.dma:
                    e.wait_ge(sems[d.dsem], d.dval)
                else:
                    e.wait_ge(sems[d.eng], d.sigidx)
            if o.fn is None:
                continue
            ins = o.fn(e)
            if o.dma and o.cc:
                ins.then_inc(sems[o.dsem])
            elif o.dma:
                ins.then_inc(sems[o.dsem], 16)
            elif o.signal:
                ins.then_inc(sems[o.eng], 1)


class T:
    def __init__(self, h, F):
        self.h = h
        self.F = F

    def v(self, col=0, dims=None, p0=0, np_=128):
        return bass.AP(self.h, p0 * self.F + col, [[self.F, np_]] + [list(d) for d in dims])

    def c(self, a, b, p0=0, np_=128):
        return self.v(a, [[1, b - a]], p0, np_)


def build(nc, S, use_cc=True, NPHYS=2560):
    NTP = S // 128
    NT = NTP + 1
    ST = NTP
    es = contextlib.ExitStack()
    P = Prog(nc)

    def din(name, shape, dt=F32):
        return nc.dram_tensor(name, list(shape), dt, kind="ExternalInput")

    def dout(name, shape, dt=F32):
        return nc.dram_tensor(name, list(shape), dt, kind="ExternalOutput")

    xp = din("xp", [S, D])
    xs = din("xs", [128, D])
    w_in = din("w_in", [2, D, DIN])
    w_out = din("w_out", [2, D, D])
    w_gu = din("w_gu", [2, D, 2 * DFF])
    w_down = din("w_down", [2, DFF, D])
    norm1 = din("norm1", [2, D])
    norm2 = din("norm2", [2, D])
    qn = din("qn", [2, 64])
    kn = din("kn", [2, 64])
    lqk = din("lqk", [2, 256])
    subln = din("subln", [2, 128])
    wa2 = din("wa2", [2, 16, 256])
    ba = din("ba", [2, 256])
    gnorm = din("gnorm", [2, 128])
    kc = din("kc", [8 * NPHYS * 8, 2048])
    vc = din("vc", [8 * NPHYS * 8, 2048])
    sg = din("sg", [2, 16, 4, 64, 128])
    ptrep = din("ptrep", [128, 16], I32)
    rgcol = din("rgcol", [128, 1])
    cosP = din("cosP", [S, 32])
    sinP = din("sinP", [S, 32])
    cosS = din("cosS", [128, 32])
    sinS = din("sinS", [128, 32])
    identd = din("identd", [128, 128])
    trid = din("trid", [128, 128])
    i16d = din("i16d", [128, 256])

    yp = dout("yp", [S, D])
    ys = dout("ys", [16, D])
    kp = dout("kp", [2, S, 512])
    vp = dout("vp", [2, S, 512])
    gp = dout("gp", [2, 4, 64, 128])
    ks = dout("ks", [2, 16, 512])
    vs = dout("vs", [2, 16, 512])
    gs = dout("gs", [2, 16, 4, 64, 128])

    ag1_in = nc.dram_tensor("ag1_in", [16, 1536], F32)
    ag2_in = nc.dram_tensor("ag2_in", [64, 128], F32)
    vscr = nc.dram_tensor("vscr", [16, 512], F32)

    def sb(name, F, dt):
        return T(es.enter_context(nc.sbuf_tensor(name, [128, F], dt)), F)

    def psb(name, F, dt):
        return T(es.enter_context(nc.psum_tensor(name, [128, F], dt)), F)

    X = sb("X", NT * D, F32)
    ARC = 49664
    AR = sb("AR", ARC, BF)
    CH = 2048

    def arres(off, n):
        return ["A%d" % i for i in range(off // CH, (off + n - 1) // CH + 1)]

    WIN_O, WOUT_O, KT_O, VA_O = 0, 24704, 32896, 41088
    WIN_R = arres(WIN_O, 24704)
    WOUT_R = arres(WOUT_O, 8192)
    GT = 640
    WD_O, HT_O, WGU_O, N2T_O = 0, 22528, 36864, 43008
    ARF = AR.h.bitcast(F32)
    ARFt = T(ARF, ARC // 2)

    def arfres(offf, n):
        return arres(offf * 2, n * 2)

    ident_bf = sb("ident_bf", 128, BF)
    identf = sb("identf", 128, F32)
    triN = sb("triN", 128, F32)
    onesN = sb("onesN", 128, F32)
    onesf = sb("onesf", 128, F32)
    tri4 = sb("tri4", 512, BF)
    i16 = sb("i16", 256, F32)
    g1T = sb("g1T", 16, F32)
    g2T = sb("g2T", 16, F32)
    Gq = sb("Gq", 128, F32)
    Gk = sb("Gk", 128, F32)
    Gsub = sb("Gsub", 256, F32)
    Ggn = sb("Ggn", 256, F32)
    lamc = sb("lamc", 8, F32)
    wa2a = sb("wa2a", 512, BF)
    CS = sb("CS", 64, F32)
    CSS = sb("CSS", 64, F32)
    grT = sb("grT", 128, BF)
    stat = sb("stat", 64, F32)
    F0 = sb("F0", 512, F32)
    F1 = sb("F1", 512, F32)
    F2 = sb("F2", 512, F32)
    F5 = sb("F5", 512, F32)
    LQ = F5
    F6 = sb("F6", 512, F32)
    F7 = sb("F7", 512, F32)
    B0 = sb("B0", 1024, BF)
    B1 = sb("B1", 1024, BF)
    B2 = sb("B2", 512, BF)
    B3 = sb("B3", 512, BF)
    B5 = sb("B5", 1024, BF)
    B7 = sb("B7", 256, BF)
    B8 = sb("B8", 512, BF)
    B9 = sb("B9", 1024, BF)
    B10 = sb("B10", 512, BF)
    B11 = sb("B11", 512, BF)
    PT = sb("PT", 1024, BF)
    Sst = sb("Sst", 512, F32)
    Sbf = sb("Sbf", 512, BF)
    sc32 = sb("sc32", 64, F32)
    pbuf = sb("pbuf", 64, F32)
    idxK = sb("idxK", 128, I32)
    idxKf = sb("idxKf", 128, F32)
    ptf = sb("ptf", 16, F32)
    pti = sb("pti", 16, I32)
    PS = [psb("ps%d" % i, 512, F32) for i in range(6)]
    PB = [psb("pb%d" % i, 1024, BF) for i in range(2)]

    names = ["pe", "act", "dve", "pool"] + ["d_sp_%d" % i for i in range(8)] + \
        ["d_pool_%d" % i for i in range(8)] + ["d_act_%d" % i for i in range(4)] + ["cc%d" % i for i in range(4)]
    sems = {n: es.enter_context(nc.semaphore(n)) for n in names}

    def dmain(q, out, in_, writes, reads=()):
        P.dma(q, lambda e: e.dma_start(out=out, in_=in_, allow_slow_non_contiguous=True), reads=reads, writes=writes)

    def dmaout(q, out, in_, reads):
        P.dma(q, lambda e: e.dma_start(out=out, in_=in_, allow_slow_non_contiguous=True), reads=reads, is_output=True)

    def tt(eng, out, in0, in1, op, reads, writes):
        P.op(eng, lambda e: e.tensor_tensor(out=out, in0=in0, in1=in1, op=op), reads, writes)

    def ts(eng, out, in0, s1, s2, op0, op1, reads, writes):
        if op1 is None:
            P.op(eng, lambda e: e.tensor_scalar(out=out, in0=in0, scalar1=s1, scalar2=None, op0=op0), reads, writes)
        else:
            P.op(eng, lambda e: e.tensor_scalar(out=out, in0=in0, scalar1=s1, scalar2=s2, op0=op0, op1=op1), reads, writes)

    def stt(eng, out, in0, sc, in1, op0, op1, reads, writes):
        P.op(eng, lambda e: e.scalar_tensor_tensor(out=out, in0=in0, scalar=sc, in1=in1, op0=op0, op1=op1), reads, writes)

    def red(eng, out, in_, reads, writes):
        P.op(eng, lambda e: e.tensor_reduce(out=out, in_=in_, axis=AX.X, op=ALU.add), reads, writes)

    def actf(out, in_, func, reads, writes, scale=1.0, bias=0.0, accum=None):
        if accum is None:
            P.act(lambda e: e.activation(out=out, in_=in_, func=func, bias=bias, scale=scale), reads, writes)
        else:
            P.act(lambda e: e.activation(out=out, in_=in_, func=func, bias=bias, scale=scale, accum_out=accum), reads, writes)

    def cp(eng, out, in_, reads, writes):
        if eng == "act":
            P.act(lambda e: e.copy(out=out, in_=in_), reads, writes)
        else:
            P.op(eng, lambda e: e.tensor_copy(out=out, in_=in_), reads, writes)

    def mms(lst, reads, writes):
        def fn(e):
            ins = None
            for it in lst:
                (o, l, r, st, sp_) = it[:5]
                if len(it) > 5:
                    ins = e.matmul(o, l, r, start=st, stop=sp_, skip_group_check=True)
                else:
                    ins = e.matmul(o, l, r, start=st, stop=sp_)
            return ins
        P.pe(fn, reads, writes)

    def trs(lst, reads, writes):
        def fn(e):
            ins = None
            for (o, i, idn) in lst:
                ins = e.transpose(o, i, idn)
            return ins
        P.pe(fn, reads, writes)

    def rstd_from_ss(ssap, n, cols, name):
        actf(ssap, ssap, AF.Ln, [name], [name], scale=1.0 / n, bias=EPS)
        actf(ssap, ssap, AF.Exp, [name], [name], scale=-0.5)

    dmain("sp", identf.c(0, 128), identd.ap(), ["identf"])
    dmain("pool", ident_bf.c(0, 128), identd.ap(), ["ident_bf"])
    dmain("sp", triN.c(0, 128), trid.ap(), ["triN"])
    for r in range(4):
        dmain("pool", tri4.c(r * 128, r * 128 + 128), trid.ap(), ["tri4"])
    dmain("sp", i16.c(0, 256), i16d.ap(), ["i16"])
    P.pool(lambda e: e.memset(onesN.c(0, 128), -1.0 / 16), writes=["onesN"])
    P.pool(lambda e: e.memset(onesf.c(0, 128), 1.0), writes=["onesf"])
    P.pool(lambda e: e.memset(grT.c(0, 128), 1.0), writes=["grT"])
    P.pool(lambda e: e.memset(B5.c(0, 1024), 0.0), writes=["B5"])
    ts("dve", triN.c(0, 128), triN.c(0, 128), -1.0 / 16, None, ALU.mult, None, ["triN"], ["triN"])
    dmain("sp", g1T.v(0, [[8, 2], [1, 8]]), norm1.ap().rearrange("l (k p) -> p l k", p=128), ["g1T"])
    dmain("sp", g2T.v(0, [[8, 2], [1, 8]]), norm2.ap().rearrange("l (k p) -> p l k", p=128), ["g2T"])

    def bc(dr, n):
        return bass.AP(dr, 0, [[0, 128], [1, 2 * n]])
    dmain("sp", Gq.c(0, 128), bc(qn, 64), ["Gq"])
    dmain("sp", Gk.c(0, 128), bc(kn, 64), ["Gk"])
    dmain("sp", Gsub.c(0, 256), bc(subln, 128), ["Gsub"])
    dmain("sp", Ggn.c(0, 256), bc(gnorm, 128), ["Ggn"])
    dmain("sp", LQ.c(0, 512), bc(lqk, 256), ["F5a", "F5b"])
    dmain("sp", CSS.c(0, 32), cosS.ap(), ["CSS"])
    dmain("sp", CSS.c(32, 64), sinS.ap(), ["CSS"])
    dmain("sp", pti.c(0, 16), ptrep.ap(), ["pti"])
    dmain("sp", stat.c(60, 61), rgcol.ap(), ["rg"])
    dmain("pool", wa2a.v(0, [[256, 2], [1, 256]], 0, 16), wa2.ap().rearrange("l r c -> r l c"), ["wa2a"])
    dmain("pool", wa2a.v(0, [[1, 512]], 16, 1), bass.AP(ba, 0, [[0, 1], [1, 512]]), ["wa2a"])
    for l in range(2):
        li = 0.8 - 0.6 * math.exp(-0.3 * l)
        ts("dve", Gsub.c(l * 128, l * 128 + 128), Gsub.c(l * 128, l * 128 + 128), 1.0 - li, None, ALU.mult, None, ["Gsub"], ["Gsub"])
        b0 = l * 256
        tt("dve", F6.c(0, 64), LQ.c(b0, b0 + 64), LQ.c(b0 + 64, b0 + 128), ALU.mult, ["F5a", "F5b"], ["F6"])
        tt("dve", F6.c(64, 128), LQ.c(b0 + 128, b0 + 192), LQ.c(b0 + 192, b0 + 256), ALU.mult, ["F5a", "F5b", "F6"], ["F6"])
        red("dve", stat.c(40, 42), F6.v(0, [[64, 2], [1, 64]]), ["F6"], ["lamtmp"])
        actf(stat.c(40, 42), stat.c(40, 42), AF.Exp, ["lamtmp"], ["lamtmp"])
        tt("dve", stat.c(42, 43), stat.c(41, 42), stat.c(40, 41), ALU.subtract, ["lamtmp"], ["lamtmp2"])
        ts("dve", lamc.c(l, l + 1), stat.c(42, 43), -li, None, ALU.add, None, ["lamtmp2"], ["lamc"])
    cp("dve", ptf.c(0, 16), pti.c(0, 16), ["pti"], ["ptf"])
    ts("dve", ptf.c(0, 16), ptf.c(0, 16), 8.0, stat.c(60, 61), ALU.mult, ALU.add, ["ptf", "rg"], ["ptf"])
    for l_ in range(2):
        for h_ in range(4):
            ts("dve", idxKf.v(l_ * 64 + h_, [[4, 16]]), ptf.c(0, 16), float((l_ * 4 + h_) * NPHYS * 8), None, ALU.add, None, ["ptf", "idxKf"], ["idxKf"])
    cp("dve", idxK.c(0, 128), idxKf.c(0, 128), ["idxKf"], ["idxK"])

    for t in range(NTP):
        dmain("sp", X.c(t * D, t * D + D), xp.ap()[t * 128:(t + 1) * 128, :], ["X%d" % t])
    dmain("sp", X.c(ST * D, ST * D + D), xs.ap(), ["X%d" % ST])

    def bcast_mid(Tn, col, n_g, n_d):
        return Tn.v(col, [[0, n_g], [1, n_d]])

    for l in range(_NL if _STOP >= 1 else 0):
        for kc_ in range(8):
            dmain("pool", AR.c(WIN_O + kc_ * DIN, WIN_O + (kc_ + 1) * DIN), w_in.ap()[l, kc_ * 128:(kc_ + 1) * 128, :], arres(WIN_O + kc_ * DIN, DIN))
        dmain("pool", AR.v(WOUT_O, [[D, 8], [1, D]]), w_out.ap()[l].rearrange("(k p) c -> p k c", p=128), WOUT_R)
        P.pool(lambda e: e.memset(AR.c(VA_O, VA_O + NTP * 528), 1.0), writes=arres(VA_O, NTP * 528))
        P.pool(lambda e: e.memset(Sst.c(0, 512), 0.0), writes=["Sst"])
        P.pool(lambda e: e.memset(Sbf.c(0, 512), 0.0), writes=["Sbf"])

        def win(kc_, c0, c1):
            return AR.c(WIN_O + kc_ * DIN + c0, WIN_O + kc_ * DIN + c1)

        def tile_norm_T(t, gT, dstT_ap_fn, dst_res):
            xr = "X%d" % t
            xa = X.c(t * D, t * D + D)
            actf(B0.c(0, 1024), xa, AF.Square, [xr], ["B0", "ssn"], accum=stat.c(0, 1))
            rstd_from_ss(stat.c(0, 1), D, 1, "ssn")
            ts("dve", B0.c(0, 1024), xa, stat.c(0, 1), None, ALU.mult, None, [xr, "ssn"], ["B0"])
            trs([(PB[0].c(k * 128, k * 128 + 128), B0.c(k * 128, k * 128 + 128), ident_bf.c(0, 128)) for k in range(8)],
                ["B0", "ident_bf"], ["PB0"])
            tt("dve", dstT_ap_fn(), PB[0].v(0, [[128, 8], [1, 128]]), gT.v(l * 8, [[1, 8], [0, 128]]), ALU.mult,
               ["PB0", "g1T", "g2T"], dst_res)

        def qk_norm_rope(Fx, fx, Gx, cs, csn):
            tt("pool", F6.c(0, 512), Fx.c(0, 512), Fx.c(0, 512), ALU.mult, [fx], ["F6"])
            red("dve", stat.c(8, 16), F6.v(0, [[64, 8], [1, 64]]), ["F6"], ["ss8"])
            rstd_from_ss(stat.c(8, 16), 64, 8, "ss8")
            tt("dve", Fx.v(0, [[64, 8], [1, 64]]), Fx.v(0, [[64, 8], [1, 64]]), stat.v(8, [[1, 8], [0, 64]]), ALU.mult, [fx, "ss8"], [fx])
            tt("dve", Fx.v(0, [[64, 8], [1, 64]]), Fx.v(0, [[64, 8], [1, 64]]), bcast_mid(Gx, l * 64, 8, 64), ALU.mult, [fx, "Gq", "Gk"], [fx])
            tt("pool", F6.v(0, [[32, 16], [1, 32]]), Fx.v(0, [[32, 16], [1, 32]]), bcast_mid(cs, 0, 16, 32), ALU.mult, [fx, csn], ["F6"])
            tt("dve", F7.v(0, [[64, 8], [1, 32]]), Fx.v(32, [[64, 8], [1, 32]]), bcast_mid(cs, 32, 8, 32), ALU.mult, [fx, csn], ["F7a"])
            tt("dve", F7.v(32, [[64, 8], [1, 32]]), Fx.v(0, [[64, 8], [1, 32]]), bcast_mid(cs, 32, 8, 32), ALU.mult, [fx, csn], ["F7b"])
            tt("dve", Fx.v(0, [[64, 8], [1, 32]]), F6.v(0, [[64, 8], [1, 32]]), F7.v(0, [[64, 8], [1, 32]]), ALU.subtract, ["F6", "F7a", "F7b"], [fx])
            tt("dve", Fx.v(32, [[64, 8], [1, 32]]), F6.v(32, [[64, 8], [1, 32]]), F7.v(32, [[64, 8], [1, 32]]), ALU.add, ["F6", "F7a", "F7b", fx], [fx])

        def proj_block(c0, n, ps):
            mms([(ps.c(0, n), B1.c(k * 128, k * 128 + 128), win(k, c0, c0 + n), k == 0, k == 7) for k in range(8)],
                ["B1"] + WIN_R, [ps_name(ps)])

        def ps_name(ps):
            for i, p_ in enumerate(PS):
                if p_ is ps:
                    return "PS%d" % i
            return "PB"

        def tile_A(t, cs, csn, is_sample, emit_out=True):
            tile_norm_T(t, g1T, lambda: B1.v(0, [[128, 8], [1, 128]]), ["B1"])
            proj_block(0, 512, PS[0])
            cp("act", F0.c(0, 512), PS[0].c(0, 512), ["PS0"], ["F0"])
            proj_block(512, 512, PS[1])
            cp("act", F1.c(0, 512), PS[1].c(0, 512), ["PS1"], ["F1"])
            proj_block(1024, 512, PS[0])
            cp("act", F2.c(0, 512), PS[0].c(0, 512), ["PS0"], ["F2"])
            if not is_sample:
                cp("dve", AR.v(VA_O + t * 528, [[132, 4], [1, 128]]), PS[0].v(0, [[128, 4], [1, 128]]), ["PS0"], arres(VA_O + t * 528, 528))
                dmaout("sp", vp.ap()[l, t * 128:(t + 1) * 128, :], F2.c(0, 512), ["F2"])
            elif emit_out:
                dmaout("sp", vs.ap()[l], F2.c(0, 512, 0, 16), ["F2"])
            proj_block(1536, 512, PS[1])
            cp("act", B8.c(0, 512), PS[1].c(0, 512), ["PS1"], ["B8"])
            proj_block(2048, 512, PS[0])
            cp("act", B2.c(0, 512), PS[0].c(0, 512), ["PS0"], ["B2"])
            if is_sample:
                cp("dve", F7.c(0, 512, 0, 16), PS[0].c(0, 512, 0, 16), ["PS0"], ["F7a", "F7b"])
                dmain("sp", bass.AP(vscr, 0, [[512, 16], [1, 512]]), F7.c(0, 512, 0, 16), ["vscr"], reads=["F7a", "F7b"])
            proj_block(2560, 512, PS[1])
            actf(B11.c(0, 512), PS[1].c(0, 512), AF.Silu, ["PS1"], ["B11"])
            mms([(PS[0].c(0, 128, 0, 16), win(k, 3072, 3088), B1.c(k * 128, k * 128 + 128), k == 0, k == 7) for k in range(8)],
                ["B1"] + WIN_R, ["PS0"])
            cp("act", grT.c(0, 128, 0, 16), PS[0].c(0, 128, 0, 16), ["PS0"], ["grT"])
            mms([(PS[1].c(0, 256), grT.c(0, 128, 0, 17), wa2a.c(l * 256, l * 256 + 256, 0, 17), True, True)], ["grT", "wa2a"], ["PS1"])
            actf(F5.c(0, 256), PS[1].c(0, 256), AF.Exp, ["PS1"], ["F5a"], scale=-1.0)
            actf(F5.c(0, 256), F5.c(0, 256), AF.Ln, ["F5a"], ["F5a"], bias=1.0)
            qk_norm_rope(F0, "F0", Gq, cs, csn)
            qk_norm_rope(F1, "F1", Gk, cs, csn)
            if not is_sample:
                dmaout("sp", kp.ap()[l, t * 128:(t + 1) * 128, :], F1.c(0, 512), ["F1"])
                cp("pool", B3.c(0, 512), F1.c(0, 512), ["F1"], ["B3"])
                trs([(PB[1].c(h * 128, h * 128 + 128), B3.c(h * 128, h * 128 + 128), ident_bf.c(0, 128)) for h in range(4)],
                    ["B3", "ident_bf"], ["PB1"])
                cp("act", AR.v(KT_O + t * 128, [[S, 4], [1, 128]]), PB[1].v(0, [[128, 4], [1, 128]]), ["PB1"], arres(KT_O, 4 * S))
                cp("pool", B3.c(0, 512), F0.c(0, 512), ["F0"], ["B3"])
                trs([(PB[1].c(h * 128, h * 128 + 128), B3.c(h * 128, h * 128 + 128), ident_bf.c(0, 128)) for h in range(4)],
                    ["B3", "ident_bf"], ["PB1"])
                cp("act", B5.v(0, [[256, 4], [1, 128]], 0, 64), PB[1].v(0, [[128, 4], [1, 128]], 0, 64), ["PB1"], ["B5"])
                cp("act", B5.v(128, [[256, 4], [1, 128]], 64, 64), PB[1].v(0, [[128, 4], [1, 128]], 64, 64), ["PB1"], ["B5"])
            elif emit_out:
                dmaout("sp", ks.ap()[l], F1.c(0, 512, 0, 16), ["F1"])

        def merge_head_norm(src_ps, Gx, goff, dst_col, extra_mul, nrows=128, srcres=("PS3",)):
            r = nrows
            cp("act", F6.c(0, 512, 0, r), src_ps, list(srcres) + ["F6"], ["F6"])
            tt("pool", F7.c(0, 512, 0, r), F6.c(0, 512, 0, r), F6.c(0, 512, 0, r), ALU.mult, ["F6"], ["F7a", "F7b"])
            red("dve", stat.c(16, 20, 0, r), F7.v(0, [[128, 4], [1, 128]], 0, r), ["F7a", "F7b"], ["ss4"])
            rstd_from_ss(stat.c(16, 20, 0, r), 128, 4, "ss4")
            tt("dve", F6.v(0, [[128, 4], [1, 128]], 0, r), F6.v(0, [[128, 4], [1, 128]], 0, r), stat.v(16, [[1, 4], [0, 128]], 0, r), ALU.mult, ["F6", "ss4"], ["F6"])
            if extra_mul is None:
                tt("dve", B0.v(dst_col, [[128, 4], [1, 128]], 0, r), F6.v(0, [[128, 4], [1, 128]], 0, r), Gx.v(goff, [[0, 4], [1, 128]], 0, r), ALU.mult, ["F6", "Gsub", "Ggn"], ["B0"])
            else:
                tt("dve", F6.v(0, [[128, 4], [1, 128]], 0, r), F6.v(0, [[128, 4], [1, 128]], 0, r), Gx.v(goff, [[0, 4], [1, 128]], 0, r), ALU.mult, ["F6", "Gsub", "Ggn"], ["F6"])
                tt("dve", B0.c(dst_col, dst_col + 512, 0, r), F6.c(0, 512, 0, r), extra_mul, ALU.mult, ["F6", "B11"], ["B0"])

        def tile_C(t):
            trs([(PB[0].c(k * 128, k * 128 + 128), B0.c(k * 128, k * 128 + 128), ident_bf.c(0, 128)) for k in range(8)],
                ["B0", "ident_bf"], ["PB0"])
            cp("act", B1.c(0, 1024), PB[0].c(0, 1024), ["PB0"], ["B1"])
            for cb in range(2):
                ps = PS[cb]
                mms([(ps.c(0, 512), B1.c(k * 128, k * 128 + 128), AR.c(WOUT_O + k * D + cb * 512, WOUT_O + k * D + cb * 512 + 512), k == 0, k == 7) for k in range(8)],
                    ["B1"] + WOUT_R, ["PS%d" % cb])
                xa = X.c(t * D + cb * 512, t * D + cb * 512 + 512)
                tt("dve", xa, xa, ps.c(0, 512), ALU.add, ["X%d" % t, "PS%d" % cb], ["X%d" % t])

        tile_A(ST, CSS, "CSS", True)
        for (src, o) in ((F0, 0), (F1, 128), (F2, 256)):
            P.dma("sp", (lambda s_, o_: (lambda e: e.dma_start(out=bass.AP(ag1_in, o_, [[1536, 16], [384, 4], [1, 128]]),
                                                                in_=s_.v(0, [[128, 4], [1, 128]], 0, 16))))(src, o),
                  reads=["F0", "F1", "F2"], writes=["ag1_in"])
        SR = arfres(0, 12288)
        S0 = ARFt

        for t in range(NTP if _STOP >= 2 else 0):
            dmain("sp", CS.c(0, 32), cosP.ap()[t * 128:(t + 1) * 128, :], ["CS"])
            dmain("sp", CS.c(32, 64), sinP.ap()[t * 128:(t + 1) * 128, :], ["CS"])
            tile_A(t, CS, "CS", False)
            KTR = arres(KT_O, 4 * S)
            accs = []
            for g in range(8):
                accs.append((PS[3 + g // 3], (g % 3) * 129))
            def ptbuf(j, half):
                if j % 2 == 0:
                    return PT, "PT%d" % half
                return B9, ("B9a" if half == 0 else "B9b")

            def emit_st(j, half):
                psi = 2 if half == 0 else 1
                ps = PS[psi]
                lst = []
                for hh in range(2):
                    h = half * 2 + hh
                    lst.append((ps.c(hh * 256, hh * 256 + 256),
                                AR.c(KT_O + h * S + j * 128, KT_O + h * S + j * 128 + 128),
                                B5.c(h * 256, h * 256 + 256), True, True))
                mms(lst, KTR + ["B5"], ["PS%d" % psi])

            def emit_exp(j, half):
                psi = 2 if half == 0 else 1
                buf, bn = ptbuf(j, half)
                pt = buf.c(half * 512, half * 512 + 512)
                actf(pt, PS[psi].c(0, 512), AF.Exp, ["PS%d" % psi], [bn], scale=0.125)
                if j == t:
                    tt("pool", pt, pt, tri4.c(0, 512), ALU.mult, [bn, "tri4"], [bn])

            def emit_pv(j, half):
                buf, bn = ptbuf(j, half)
                lst = []
                for hh in range(2):
                    h = half * 2 + hh
                    for m in range(2):
                        bank, col = accs[h * 2 + m]
                        lst.append((bank.c(col, col + 129),
                                    buf.c(half * 512 + (hh * 2 + m) * 128, half * 512 + (hh * 2 + m) * 128 + 128),
                                    AR.c(VA_O + j * 528 + h * 132, VA_O + j * 528 + h * 132 + 129), (j == 0 and (h * 2 + m) in (0, 3, 6)), j == t, 1))
                mms(lst, [bn] + arres(VA_O + j * 528, 528), ["PS3", "PS4", "PS5"])

            emit_st(0, 0)
            emit_st(0, 1)
            for j in range(t + 1):
                emit_exp(j, 0)
                emit_exp(j, 1)
                if j + 1 <= t:
                    emit_st(j + 1, 0)
                    emit_st(j + 1, 1)
                emit_pv(j, 0)
                emit_pv(j, 1)
            for g in range(8):
                bank, col = accs[g]
                P.dve(lambda e, b_=bank, c_=col, g_=g: e.reciprocal(out=stat.c(24 + g_, 25 + g_), in_=b_.c(c_ + 128, c_ + 129)),
                      ["PS3", "PS4", "PS5"], ["rden"])
            for h in range(4):
                ts("dve", stat.c(24 + 2 * h + 1, 24 + 2 * h + 2), stat.c(24 + 2 * h + 1, 24 + 2 * h + 2), lamc.c(l, l + 1), None, ALU.mult, None, ["rden", "lamc"], ["rden"])
            for h in range(4):
                b1, c1 = accs[2 * h]
                b2, c2 = accs[2 * h + 1]
                ts("dve", F0.c(h * 128, h * 128 + 128), b1.c(c1, c1 + 128), stat.c(24 + 2 * h, 25 + 2 * h), None, ALU.mult, None, ["PS3", "PS4", "PS5", "rden"], ["F0"])
                stt("dve", F0.c(h * 128, h * 128 + 128), b2.c(c2, c2 + 128), stat.c(25 + 2 * h, 26 + 2 * h), F0.c(h * 128, h * 128 + 128), ALU.mult, ALU.add, ["PS3", "PS4", "PS5", "rden", "F0"], ["F0"])
            merge_head_norm(F0.c(0, 512), Gsub, l * 128, 0, None, srcres=("F0",))
            mms([(PS[2].c(0, 256), triN.c(0, 128), F5.c(0, 256), True, True),
                 (PS[2].c(256, 512), onesN.c(0, 128), F5.c(0, 256), True, True)], ["triN", "onesN", "F5a"], ["PS2"])
            cp("act", F7.c(0, 256), PS[2].c(0, 256), ["PS2"], ["F7a", "F7b"])
            tt("dve", F5.c(256, 512), PS[2].c(256, 512), F7.c(0, 256), ALU.subtract, ["PS2", "F7a", "F7b"], ["F5b"])
            actf(F5.c(256, 512), F5.c(256, 512), AF.Exp, ["F5b"], ["F5b"])
            tt("dve", B7.c(0, 256), B8.c(256, 512), F5.c(256, 512), ALU.mult, ["B8", "F5b"], ["B7"])
            mms([(PS[2].c(h * 128, h * 128 + 128, 0, 64), F5.c(h * 64, h * 64 + 64), triN.c(0, 128), True, True) for h in range(4)],
                ["F5a", "triN"], ["PS2"])
            actf(F6.c(0, 512, 0, 64), PS[2].c(0, 512, 0, 64), AF.Exp, ["PS2", "F6"], ["F6"])
            actf(F7.c(0, 512, 0, 64), PS[2].c(0, 512, 0, 64), AF.Exp, ["PS2", "F7a", "F7b"], ["F7a", "F7b"], scale=-1.0)
            trs([(PB[1].c(i * 128, i * 128 + 128, 0, 64), B8.c(i * 64, i * 64 + 64), ident_bf.c(0, 128)) for i in range(8)],
                ["B8", "ident_bf"], ["PB1"])
            stt("dve", B9.c(0, 512, 0, 64), PB[1].c(0, 512, 0, 64), 0.125, F6.c(0, 512, 0, 64), ALU.mult, ALU.mult, ["PB1", "F6"], ["B9a"])
            tt("dve", B9.c(512, 1024, 0, 64), PB[1].c(512, 1024, 0, 64), F7.c(0, 512, 0, 64), ALU.mult, ["PB1", "F7a", "F7b"], ["B9b"])
            mms([(PS[2].c(h * 128, h * 128 + 128), B9.c(512 + h * 128, 512 + h * 128 + 128, 0, 64), B9.c(h * 128, h * 128 + 128, 0, 64), True, True) for h in range(4)],
                ["B9a", "B9b"], ["PS2"])
            tt("dve", B10.c(0, 512), PS[2].c(0, 512), tri4.c(0, 512), ALU.mult, ["PS2", "tri4"], ["B10"])
            lst = []
            for h in range(4):
                lst.append((PS[3].c(h * 128, h * 128 + 128), B9.c(h * 128, h * 128 + 128, 0, 64), Sbf.c(h * 128, h * 128 + 128, 0, 64), True, False))
                lst.append((PS[3].c(h * 128, h * 128 + 128), B10.c(h * 128, h * 128 + 128), B2.c(h * 128, h * 128 + 128), False, True))
            mms(lst, ["B9a", "Sbf", "B10", "B2"], ["PS3"])
            mms([(PS[4].c(h * 128, h * 128 + 128, 0, 64), B7.c(h * 64, h * 64 + 64), B2.c(h * 128, h * 128 + 128), True, True) for h in range(4)],
                ["B7", "B2"], ["PS4"])
            for h in range(4):
                stt("dve", Sst.c(h * 128, h * 128 + 128, 0, 64), Sst.c(h * 128, h * 128 + 128, 0, 64), F6.c(h * 128 + 127, h * 128 + 128, 0, 64),
                    PS[4].c(h * 128, h * 128 + 128, 0, 64), ALU.mult, ALU.add, ["Sst", "F6", "PS4"], ["Sst"])
            cp("pool", Sbf.c(0, 512, 0, 64), Sst.c(0, 512, 0, 64), ["Sst"], ["Sbf"])
            merge_head_norm(PS[3].c(0, 512), Ggn, l * 128, 512, B11.c(0, 512))
            tile_C(t)
        dmaout("sp", gp.ap()[l].rearrange("h k v -> k h v"), Sst.v(0, [[128, 4], [1, 128]], 0, 64), ["Sst"])

        if _STOP < 3:
            continue
        tile_A_redo = True
        tile_A(ST, CSS, "CSS", True, emit_out=False)
        S0c, VBc = 0, 4096
        S0r = arfres(S0c, 4096)
        VBr = arfres(VBc, 4096)
        dmain("sp", ARFt.v(S0c, [[256, 16], [128, 2], [1, 128]]),
              bass.AP(sg, l * 16 * 32768, [[128, 128], [32768, 16], [16384, 2], [1, 128]]), S0r)
        for h2 in range(2):
            dmain("sp", ARFt.v(VBc, [[256, 16], [128, 2], [1, 128]], h2 * 64, 64),
                  bass.AP(vscr, h2 * 128, [[0, 64], [512, 16], [256, 2], [1, 128]]), VBr, reads=["vscr"])
        actf(F6.c(0, 256, 0, 16), F5.c(0, 256, 0, 16), AF.Exp, ["F5a", "F6"], ["F6"], scale=-1.0 / 16)
        cp("dve", F6.c(256, 512, 0, 16), B8.c(256, 512, 0, 16), ["B8", "F6"], ["F6"])
        ts("dve", F5.c(256, 512, 0, 16), B8.c(0, 256, 0, 16), 0.125, None, ALU.mult, None, ["B8"], ["F5b"])
        lst = []
        for qi, (src, c0) in enumerate(((F6, 0), (F6, 256), (F5, 256))):
            for hp in range(2):
                lst.append((PS[2].c((qi * 2 + hp) * 16, (qi * 2 + hp) * 16 + 16), src.c(c0 + hp * 128, c0 + hp * 128 + 128, 0, 16), identf.c(0, 16, 0, 16)))
        trs(lst, ["F6", "F5b", "identf"], ["PS2"])
        cp("act", F2.c(0, 96), PS[2].c(0, 96), ["PS2", "F2"], ["F2"])
        def s_view(c0):
            return ARFt.v(c0, [[128, 2], [256, 16], [1, 128]])

        def col_view(qi):
            return F2.v(qi * 32, [[16, 2], [1, 16], [0, 128]])
        tt("dve", s_view(S0c), s_view(S0c), col_view(0), ALU.mult, S0r + ["F2"], S0r)
        tt("pool", s_view(VBc), s_view(VBc), col_view(1), ALU.mult, VBr + ["F2"], VBr)
        tt("dve", ARFt.c(S0c, S0c + 4096), ARFt.c(S0c, S0c + 4096), ARFt.c(VBc, VBc + 4096), ALU.add, S0r + VBr, S0r)
        dmaout("sp", bass.AP(gs, l * 16 * 32768, [[128, 128], [32768, 16], [16384, 2], [1, 128]]),
               ARFt.v(S0c, [[256, 16], [128, 2], [1, 128]]), S0r)
        SBc = 16384 + 0
        SBr = arres(SBc, 4096)
        cp("pool", AR.c(SBc, SBc + 4096), ARFt.c(S0c, S0c + 4096), S0r, SBr)
        QDc = SBc + 4096
        QDr = arres(QDc, 512)
        tt("dve", AR.v(QDc, [[256, 2], [16, 16], [1, 16]]), F2.v(64, [[16, 2], [1, 16], [0, 16]]), i16.v(0, [[0, 2], [16, 16], [1, 16]]), ALU.mult,
           ["F2", "i16"], QDr)
        lst = []
        for h in range(4):
            hp, h2 = h // 2, h % 2
            for b in range(16):
                lst.append((PS[3 + h2].c(h * 128, h * 128 + 128, 0, 16),
                            AR.c(QDc + hp * 256 + b * 16, QDc + hp * 256 + b * 16 + 16, h2 * 64, 64),
                            AR.c(SBc + b * 256 + hp * 128, SBc + b * 256 + hp * 128 + 128, h2 * 64, 64), b == 0, b == 15))
        mms([x for x in lst if x[0] is not None], QDr + SBr, ["PS3", "PS4"])
        for h in range(4):
            cp("act", F0.c(h * 128, h * 128 + 128, 0, 16), PS[3 + h % 2].c(h * 128, h * 128 + 128, 0, 16), ["PS3", "PS4", "F0"], ["F0"])
        merge_head_norm(F0.c(0, 512, 0, 16), Ggn, l * 128, 512, B11.c(0, 512, 0, 16), nrows=16, srcres=("F0",))

        if _STOP < 4:
            continue
        QKV = F0
        dmain("sp", F0.c(0, 384, 0, 64), bass.AP(ag1_in, 0, [[384, 64], [1, 384]]), ["F0"], reads=["ag1_in"])
        KtO = [0, 2048]
        VtO = [4096, 6144]
        PRO = 8192
        PACC = F1
        QBt = [F7, F6]
        for i in range(64):
            bsel = i % 2
            ko, vo = KtO[bsel], VtO[bsel]
            kr, vr = arfres(ko, 2048), arfres(vo, 2048)
            qb = QBt[bsel]
            qbn = "F7a" if bsel == 0 else "F6"
            qbw = ["F7a", "F7b"] if bsel == 0 else ["F6"]
            dmain("sp", qb.c(0, 128), bass.AP(ag1_in, i * 384, [[0, 128], [1, 128]]), qbw, reads=["ag1_in"])
            P.dma("pool", (lambda i_, ko_: (lambda e: e.indirect_dma_start(
                out=ARFt.c(ko_, ko_ + 2048), out_offset=None, in_=kc.ap(),
                in_offset=bass.IndirectOffsetOnAxis(ap=idxK.c(i_, i_ + 1), axis=0))))(l * 64 + i, ko),
                reads=["idxK"], writes=kr)
            P.dma("pool", (lambda i_, vo_: (lambda e: e.indirect_dma_start(
                out=ARFt.c(vo_, vo_ + 2048), out_offset=None, in_=vc.ap(),
                in_offset=bass.IndirectOffsetOnAxis(ap=idxK.c(i_, i_ + 1), axis=0))))(l * 64 + i, vo),
                reads=["idxK"], writes=vr)
            pr = arfres(PRO, 2048)
            tt("dve", ARFt.v(PRO, [[128, 16], [1, 128]]), ARFt.v(ko, [[128, 16], [1, 128]]), qb.v(0, [[0, 16], [1, 128]]), ALU.mult,
               kr + qbw, pr)
            so = bsel * 32
            scn, pbn = "sc%d" % bsel, "pbuf%d" % bsel
            red("dve", sc32.c(so, so + 32), ARFt.v(PRO, [[64, 32], [1, 64]]), pr, [scn])
            actf(pbuf.c(so, so + 32), sc32.c(so, so + 32), AF.Exp, [scn], [pbn], scale=0.125)
            red("dve", PACC.c(i * 2, i * 2 + 2), pbuf.v(so, [[1, 2], [2, 16]]), [pbn], ["F1"])
            mms([(PS[4].c(i * 2, i * 2 + 2), ARFt.c(vo + r * 128, vo + r * 128 + 128), pbuf.c(so + r * 2, so + r * 2 + 2), r == 0, r == 15) for r in range(16)],
                vr + [pbn], ["PS4"])
        cp("act", F2.c(0, 128), PS[4].c(0, 128), ["PS4", "F2"], ["F2"])
        trs([(PS[2].c(m * 128, m * 128 + 128, 0, 64), F2.v(m, [[2, 64]]), identf.c(0, 128)) for m in range(2)], ["F2", "identf"], ["PS2"])
        mms([(PS[2].c(256 + m, 257 + m, 0, 64), PACC.v(m, [[2, 64]]), onesf.c(0, 1), True, True) for m in range(2)], ["F1", "onesf"], ["PS2"])
        tt("dve", F5.c(0, 128, 0, 64), F0.c(0, 128, 0, 64), F0.c(128, 256, 0, 64), ALU.mult, ["F0"], ["F5a"])
        red("dve", stat.c(32, 34, 0, 64), F5.v(0, [[64, 2], [1, 64]], 0, 64), ["F5a"], ["snew"])
        actf(stat.c(32, 34, 0, 64), stat.c(32, 34, 0, 64), AF.Exp, ["snew"], ["snew"], scale=0.125)
        tt("dve", stat.c(34, 36, 0, 64), PS[2].c(256, 258, 0, 64), stat.c(32, 34, 0, 64), ALU.add, ["PS2", "snew"], ["den"])
        P.dve(lambda e: e.reciprocal(out=stat.c(34, 36, 0, 64), in_=stat.c(34, 36, 0, 64)), ["den"], ["den"])
        ts("dve", stat.c(35, 36, 0, 64), stat.c(35, 36, 0, 64), lamc.c(l, l + 1, 0, 64), None, ALU.mult, None, ["den", "lamc"], ["den"])
        for m in range(2):
            stt("dve", F5.c(256 + m * 128, 384 + m * 128, 0, 64), F0.c(256, 384, 0, 64), stat.c(32 + m, 33 + m, 0, 64), PS[2].c(m * 128, m * 128 + 128, 0, 64),
                ALU.mult, ALU.add, ["F0", "snew", "PS2", "F5b"], ["F5b"])
        ts("dve", F5.c(0, 128, 0, 64), F5.c(256, 384, 0, 64), stat.c(34, 35, 0, 64), None, ALU.mult, None, ["F5b", "den", "F5a"], ["F5a"])
        stt("dve", F5.c(0, 128, 0, 64), F5.c(384, 512, 0, 64), stat.c(35, 36, 0, 64), F5.c(0, 128, 0, 64), ALU.mult, ALU.add, ["F5b", "den", "F5a"], ["F5a"])
        dmain("sp", ag2_in.ap(), F5.c(0, 128, 0, 64), ["ag2_in"], reads=["F5a"])
        dmain("sp", F0.c(0, 512, 0, 16), bass.AP(ag2_in, 0, [[512, 16], [1, 512]]), ["F0"], reads=["ag2_in"])
        merge_head_norm(F0.c(0, 512, 0, 16), Gsub, l * 128, 0, None, nrows=16, srcres=("F0",))
        tile_C(ST)

        if _STOP < 5:
            continue
        dmain("pool", AR.v(WD_O, [[D, 22], [1, D]]), w_down.ap()[l].rearrange("(j p) c -> p j c", p=128), arres(WD_O, 22528))
        WDR = arres(WD_O, 22528)
        groups = []
        t0 = 0
        per = -(-NT // 4)
        while t0 < NT:
            groups.append(list(range(t0, min(NT, t0 + per))))
            t0 += per
        wq = 0
        blk = 0
        for grp in groups:
            ntok = len(grp) * 128
            N2R = arres(N2T_O, 8 * GT)
            for gi, t in enumerate(grp):
                tile_norm_T(t, g2T, (lambda gi_: (lambda: AR.v(N2T_O + gi_ * 128, [[GT, 8], [1, 128]])))(gi), N2R)
            HTR = arres(HT_O, 22 * GT)
            for j in range(22):
                wb = WGU_O + (wq % 3) * 2048
                wr = arres(wb, 2048)
                wq += 1
                for hf in range(2):
                    dmain("pool", AR.v(wb + hf * 128, [[256, 8], [1, 128]]),
                          bass.AP(w_gu, l * D * 2 * DFF + hf * DFF + j * 128, [[2 * DFF, 128], [128 * 2 * DFF, 8], [1, 128]]), wr)
                c0 = 0
                while c0 < ntok:
                    n = min(512, ntok - c0)
                    blk += 1
                    ia, ib, Fs, fsn = (0, 1, F0, "F0") if blk % 2 == 0 else (4, 5, F1, "F1")
                    mms([(PS[ia].c(0, n), AR.c(wb + k * 256, wb + k * 256 + 128), AR.c(N2T_O + k * GT + c0, N2T_O + k * GT + c0 + n), k == 0, k == 7) for k in range(8)],
                        wr + N2R, ["PS%d" % ia])
                    mms([(PS[ib].c(0, n), AR.c(wb + k * 256 + 128, wb + k * 256 + 256), AR.c(N2T_O + k * GT + c0, N2T_O + k * GT + c0 + n), k == 0, k == 7) for k in range(8)],
                        wr + N2R, ["PS%d" % ib])
                    actf(Fs.c(0, n), PS[ia].c(0, n), AF.Silu, ["PS%d" % ia, fsn], [fsn])
                    tt("dve", AR.c(HT_O + j * GT + c0, HT_O + j * GT + c0 + n), Fs.c(0, n), PS[ib].c(0, n), ALU.mult, [fsn, "PS%d" % ib], arres(HT_O + j * GT + c0, n))
                    c0 += n
            for gi, t in enumerate(grp):
                for cb in range(2):
                    ps = PS[2 + cb]
                    mms([(ps.c(0, 512), AR.c(HT_O + j * GT + gi * 128, HT_O + j * GT + gi * 128 + 128), AR.c(WD_O + j * D + cb * 512, WD_O + j * D + cb * 512 + 512), j == 0, j == 21) for j in range(22)],
                        HTR + WDR, ["PS%d" % (2 + cb)])
                    xa = X.c(t * D + cb * 512, t * D + cb * 512 + 512)
                    tt("dve", xa, xa, ps.c(0, 512), ALU.add, ["X%d" % t, "PS%d" % (2 + cb)], ["X%d" % t])
                if l == _NL - 1:
                    if t < NTP:
                        dmaout("sp", yp.ap()[t * 128:(t + 1) * 128, :], X.c(t * D, t * D + D), ["X%d" % t])
                    else:
                        dmaout("sp", ys.ap(), X.c(t * D, t * D + D, 0, 16), ["X%d" % t])

    if _STOP < 5:
        for t in range(NTP):
            dmaout("sp", yp.ap()[t * 128:(t + 1) * 128, :], X.c(t * D, t * D + D), ["X%d" % t])
        dmaout("sp", ys.ap(), X.c(ST * D, ST * D + D, 0, 16), ["X%d" % ST])
    P.finalize(sems)
    with nc.Block() as block:
        @block.tensor
        def _(e):
            P.emit("pe", e)

        @block.scalar
        def _(e):
            P.emit("act", e)

        @block.vector
        def _(e):
            P.emit("dve", e)

        @block.gpsimd
        def _(e):
            P.emit("pool", e)

        @block.sync
        def _(e):
            P.emit("sp", e)
    es.close()
    return nc


_DEBUG_HOOK = None
_USE_CC = True
_STOP = 99
_OPLIMIT = 10 ** 9
_DBG_N = 99
_NL = 2
def _consts(S, past):
    half = 32
    freqs = (10000.0 ** (-np.arange(half, dtype=np.float32) / half)).astype(np.float32)
    pos = np.arange(S, dtype=np.float32)
    ang = pos[:, None] * freqs[None, :]
    angs = np.full((128, 1), float(past), np.float32) * freqs[None, :]
    ident = np.eye(128, dtype=np.float32)
    tri = np.triu(np.ones((128, 128), np.float32))
    i16 = np.tile(np.eye(16, dtype=np.float32).reshape(1, 256), (128, 1))
    rg = (np.arange(128) % 8).astype(np.float32).reshape(128, 1)
    return dict(cosP=np.cos(ang).astype(np.float32), sinP=np.sin(ang).astype(np.float32),
                cosS=np.cos(angs).astype(np.float32), sinS=np.sin(angs).astype(np.float32),
                identd=ident, trid=tri, i16d=i16, rgcol=rg)


def kernel(x_prompt, x_sample, cache_k, cache_v, state_gla, page_table, norm1, w_in, q_norm, k_norm,
           lambda_qk, subln, w_a2, b_a, gla_norm, w_out, norm2, w_gu, w_down):
    f = lambda a: np.ascontiguousarray(np.asarray(a, dtype=np.float32))
    x_prompt, x_sample = f(x_prompt), f(x_sample)
    cache_k, cache_v, state_gla = np.asarray(cache_k), np.asarray(cache_v), f(state_gla)
    page_table = np.asarray(page_table).astype(np.int32)
    S = x_prompt.shape[1]
    NPHYS = cache_k.shape[1]
    n_cores = 8
    nc = bass.Bass("TRN2", target_bir_lowering=False)
    build(nc, S, use_cc=_USE_CC, NPHYS=NPHYS)
    cst = _consts(S, 2048)
    shared = dict(w_in=f(w_in), w_out=f(w_out), w_gu=f(w_gu), w_down=f(w_down), norm1=f(norm1), norm2=f(norm2),
                  qn=f(q_norm), kn=f(k_norm), lqk=f(lambda_qk).reshape(2, 256), subln=f(subln), wa2=f(w_a2),
                  ba=f(b_a), gnorm=f(gla_norm))
    shared.update(cst)
    kh = np.ascontiguousarray(cache_k.reshape(2, NPHYS, 128, 4, 128).transpose(0, 3, 1, 2, 4)).reshape(8 * NPHYS * 8, 2048)
    vh = np.ascontiguousarray(cache_v.transpose(0, 3, 1, 2, 4)).reshape(8 * NPHYS * 8, 2048)
    in_maps = []
    for c in range(n_cores):
        half, h = c // 4, c % 4
        m = dict(shared)
        m["xp"] = x_prompt[c]
        xs = np.zeros((128, D), np.float32)
        xs[:16] = x_sample[16 * c:16 * c + 16, 0]
        m["xs"] = xs
        m["kc"] = kh
        m["vc"] = vh
        m["sg"] = np.ascontiguousarray(state_gla[:, 16 * c:16 * c + 16])
        pt = page_table[16 * c:16 * c + 16]
        m["ptrep"] = np.ascontiguousarray(np.repeat(pt.T, 8, axis=0)).astype(np.int32)
        in_maps.append(m)
    if _DEBUG_HOOK is not None:
        return _DEBUG_HOOK(nc, in_maps)
    res = run_bass_kernel_spmd(nc, in_maps, core_ids=list(range(n_cores))).results
    yp = np.stack([r["yp"] for r in res])
    ys = np.concatenate([r["ys"] for r in res])[:, None, :]
    kp = np.stack([r["kp"] for r in res], axis=1).reshape(2, 8, S, 4, 2, 64)
    vp = np.stack([r["vp"] for r in res], axis=1).reshape(2, 8, S, 4, 128)
    gp = np.stack([r["gp"] for r in res], axis=1)
    ks = np.concatenate([r["ks"] for r in res], axis=1).reshape(2, 128, 1, 4, 2, 64)
    vs = np.concatenate([r["vs"] for r in res], axis=1).reshape(2, 128, 1, 4, 128)
    gs = np.concatenate([r["gs"] for r in res], axis=1)
    return (yp.astype(np.float32), ys.astype(np.float32), kp.astype(np.float32), vp.astype(np.float32),
            gp.astype(np.float32), ks.astype(np.float32), vs.astype(np.float32), gs.astype(np.float32))
```

```python
import contextlib
import math
import numpy as np
import concourse.bass as bass
import concourse.mybir as mb
from concourse.bass_utils import run_bass_kernel_spmd

F32 = mb.dt.float32
BF = mb.dt.bfloat16
I32 = mb.dt.int32
AF = mb.ActivationFunctionType
ALU = mb.AluOpType
AX = mb.AxisListType

D = 1024
DIN = 3088
DFF = 2816
NPHYS = 2560
EPS = 1e-6
ENGS = ("pe", "act", "dve", "pool", "sp")


class Op:
    __slots__ = ("eng", "fn", "dma", "deps", "seq", "signal", "sigidx", "dsem", "dval",
                 "waits", "gidx", "slotwait", "cc")

    def __init__(self, eng, fn, dma):
        self.eng = eng
        self.fn = fn
        self.dma = dma
        self.deps = set()
        self.signal = False
        self.sigidx = 0
        self.dsem = None
        self.dval = 0
        self.waits = []
        self.slotwait = None
        self.cc = False


class Prog:
    def __init__(self, nc):
        self.nc = nc
        self.ops = []
        self.eops = {e: [] for e in ENGS}
        self.last_writer = {}
        self.readers = {}
        self.ndma = {"sp": 8, "pool": 8, "act": 4}
        self.out_dmas = []

    def op(self, eng, fn, reads=(), writes=(), dma=False, is_output=False):
        if len(self.ops) >= _OPLIMIT and not is_output:
            return None
        o = Op(eng, fn, dma)
        deps = set()
        pr = [r for r in reads if r.startswith("PS") or r.startswith("PB")]
        if pr:
            writes = list(writes) + [r for r in pr if r not in writes]
        for r in reads:
            w = self.last_writer.get(r)
            if w is not None:
                deps.add(w)
        for r in writes:
            w = self.last_writer.get(r)
            if w is not None:
                deps.add(w)
            for rd in self.readers.get(r, ()):
                deps.add(rd)
        for r in reads:
            self.readers.setdefault(r, []).append(o)
        for r in writes:
            self.last_writer[r] = o
            self.readers[r] = []
        deps.discard(o)
        o.deps = deps
        o.gidx = len(self.ops)
        o.seq = len(self.eops[eng])
        self.ops.append(o)
        self.eops[eng].append(o)
        if is_output:
            self.out_dmas.append(o)
        return o

    def pe(self, fn, reads=(), writes=()):
        return self.op("pe", fn, reads, writes)

    def act(self, fn, reads=(), writes=()):
        return self.op("act", fn, reads, writes)

    def dve(self, fn, reads=(), writes=()):
        return self.op("dve", fn, reads, writes)

    def pool(self, fn, reads=(), writes=()):
        return self.op("pool", fn, reads, writes)

    def dma(self, q, fn, reads=(), writes=(), is_output=False):
        return self.op(q, fn, reads, writes, dma=True, is_output=is_output)

    def ccop(self, fn, reads=(), writes=()):
        o = self.op("pool", fn, reads, writes, dma=True)
        if o is not None:
            o.cc = True
        return o

    def finalize(self, sems):
        fin = Op("sp", None, False)
        fin.deps = set(self.out_dmas)
        fin.gidx = len(self.ops)
        fin.seq = len(self.eops["sp"])
        self.ops.append(fin)
        self.eops["sp"].append(fin)
        cnt = {}
        lastval = {}
        ncc = 0
        for o in self.ops:
            if o.dma and o.cc:
                o.dsem = "cc%d" % ncc
                ncc += 1
                o.dval = 1
                continue
            if o.dma:
                q = o.eng
                i = cnt.get(q, 0)
                cnt[q] = i + 1
                key = "d_%s_%d" % (q, i % self.ndma[q])
                o.dsem = key
                prev = lastval.get(key, 0)
                o.dval = prev + 16
                lastval[key] = o.dval
                if prev > 0:
                    o.slotwait = (key, prev)
        known = {e: {p: -1 for p in ENGS} for e in ENGS}
        known_dma = {e: {} for e in ENGS}
        for o in self.ops:
            e = o.eng
            best = {}
            for d in o.deps:
                if d.dma:
                    k = known_dma[e].get(d.dsem, 0)
                    if d.dval > k:
                        cur = best.get(("dma", d.dsem))
                        if cur is None or d.dval > cur.dval:
                            best[("dma", d.dsem)] = d
                else:
                    if d.seq > known[e][d.eng]:
                        cur = best.get(("eng", d.eng))
                        if cur is None or d.seq > cur.seq:
                            best[("eng", d.eng)] = d
            if o.slotwait is not None:
                key, prev = o.slotwait
                if known_dma[e].get(key, 0) >= prev:
                    o.slotwait = None
                else:
                    known_dma[e][key] = prev
            for k, d in best.items():
                if k[0] == "dma":
                    known_dma[e][d.dsem] = d.dval
                else:
                    known[e][d.eng] = d.seq
                    d.signal = True
            o.waits = [best[k] for k in sorted(best.keys())]
        for e in ENGS:
            n = 0
            for o in self.eops[e]:
                if o.signal and not o.dma:
                    n += 1
                    o.sigidx = n
        self.sems = sems

    def emit(self, eng, e):
        sems = self.sems
        for o in self.eops[eng]:
            if o.slotwait is not None:
                e.wait_ge(sems[o.slotwait[0]], o.slotwait[1])
            for d in o.waits:
                if d.dma:
                    e.wait_ge(sems[d.dsem], d.dval)
                else:
                    e.wait_ge(sems[d.eng], d.sigidx)
            if o.fn is None:
                continue
            ins = o.fn(e)
            if o.dma and o.cc:
                ins.then_inc(sems[o.dsem])
            elif o.dma:
                ins.then_inc(sems[o.dsem], 16)
            elif o.signal:
                ins.then_inc(sems[o.eng], 1)


class T:
    def __init__(self, h, F):
        self.h = h
        self.F = F

    def v(self, col=0, dims=None, p0=0, np_=128):
        return bass.AP(self.h, p0 * self.F + col, [[self.F, np_]] + [list(d) for d in dims])

    def c(self, a, b, p0=0, np_=128):
        return self.v(a, [[1, b - a]], p0, np_)


def build(nc, S, use_cc=True, NPHYS=2560):
    NTP = S // 128
    NT = NTP + 1
    ST = NTP
    es = contextlib.ExitStack()
    P = Prog(nc)

    def din(name, shape, dt=F32):
        return nc.dram_tensor(name, list(shape), dt, kind="ExternalInput")

    def dout(name, shape, dt=F32):
        return nc.dram_tensor(name, list(shape), dt, kind="ExternalOutput")

    xp = din("xp", [S, D])
    xs = din("xs", [128, D])
    w_in = din("w_in", [2, D, DIN])
    w_out = din("w_out", [2, D, D])
    w_gu = din("w_gu", [2, D, 2 * DFF])
    w_down = din("w_down", [2, DFF, D])
    norm1 = din("norm1", [2, D])
    norm2 = din("norm2", [2, D])
    qn = din("qn", [2, 64])
    kn = din("kn", [2, 64])
    lqk = din("lqk", [2, 256])
    subln = din("subln", [2, 128])
    wa2 = din("wa2", [2, 16, 256])
    ba = din("ba", [2, 256])
    gnorm = din("gnorm", [2, 128])
    kc = din("kc", [8 * NPHYS * 8, 2048])
    vc = din("vc", [8 * NPHYS * 8, 2048])
    sg = din("sg", [2, 16, 4, 64, 128])
    ptrep = din("ptrep", [128, 16], I32)
    rgcol = din("rgcol", [128, 1])
    cosP = din("cosP", [S, 32])
    sinP = din("sinP", [S, 32])
    cosS = din("cosS", [128, 32])
    sinS = din("sinS", [128, 32])
    identd = din("identd", [128, 128])
    trid = din("trid", [128, 128])
    i16d = din("i16d", [128, 256])

    yp = dout("yp", [S, D])
    ys = dout("ys", [16, D])
    kp = dout("kp", [2, S, 512])
    vp = dout("vp", [2, S, 512])
    gp = dout("gp", [2, 4, 64, 128])
    ks = dout("ks", [2, 16, 512])
    vs = dout("vs", [2, 16, 512])
    gs = dout("gs", [2, 16, 4, 64, 128])

    ag1_in = nc.dram_tensor("ag1_in", [16, 1536], F32)
    ag2_in = nc.dram_tensor("ag2_in", [64, 128], F32)
    vscr = nc.dram_tensor("vscr", [16, 512], F32)

    def sb(name, F, dt):
        return T(es.enter_context(nc.sbuf_tensor(name, [128, F], dt)), F)

    def psb(name, F, dt):
        return T(es.enter_context(nc.psum_tensor(name, [128, F], dt)), F)

    X = sb("X", NT * D, F32)
    ARC = 49664
    AR = sb("AR", ARC, BF)
    CH = 2048

    def arres(off, n):
        return ["A%d" % i for i in range(off // CH, (off + n - 1) // CH + 1)]

    WIN_O, WOUT_O, KT_O, VA_O = 0, 24704, 32896, 41088
    WIN_R = arres(WIN_O, 24704)
    WOUT_R = arres(WOUT_O, 8192)
    GT = 640
    WD_O, HT_O, WGU_O, N2T_O = 0, 22528, 36864, 43008
    ARF = AR.h.bitcast(F32)
    ARFt = T(ARF, ARC // 2)

    def arfres(offf, n):
        return arres(offf * 2, n * 2)

    ident_bf = sb("ident_bf", 128, BF)
    identf = sb("identf", 128, F32)
    triN = sb("triN", 128, F32)
    onesN = sb("onesN", 128, F32)
    onesf = sb("onesf", 128, F32)
    tri4 = sb("tri4", 512, BF)
    i16 = sb("i16", 256, F32)
    g1T = sb("g1T", 16, F32)
    g2T = sb("g2T", 16, F32)
    Gq = sb("Gq", 128, F32)
    Gk = sb("Gk", 128, F32)
    Gsub = sb("Gsub", 256, F32)
    Ggn = sb("Ggn", 256, F32)
    lamc = sb("lamc", 8, F32)
    wa2a = sb("wa2a", 512, BF)
    CS = sb("CS", 64, F32)
    CSS = sb("CSS", 64, F32)
    grT = sb("grT", 128, BF)
    stat = sb("stat", 64, F32)
    F0 = sb("F0", 512, F32)
    F1 = sb("F1", 512, F32)
    F2 = sb("F2", 512, F32)
    F5 = sb("F5", 512, F32)
    LQ = F5
    F6 = sb("F6", 512, F32)
    F7 = sb("F7", 512, F32)
    B0 = sb("B0", 1024, BF)
    B1 = sb("B1", 1024, BF)
    B2 = sb("B2", 512, BF)
    B3 = sb("B3", 512, BF)
    B5 = sb("B5", 1024, BF)
    B7 = sb("B7", 256, BF)
    B8 = sb("B8", 512, BF)
    B9 = sb("B9", 1024, BF)
    B10 = sb("B10", 512, BF)
    B11 = sb("B11", 512, BF)
    PT = sb("PT", 1024, BF)
    Sst = sb("Sst", 512, F32)
    Sbf = sb("Sbf", 512, BF)
    sc32 = sb("sc32", 64, F32)
    pbuf = sb("pbuf", 64, F32)
    idxK = sb("idxK", 128, I32)
    idxKf = sb("idxKf", 128, F32)
    ptf = sb("ptf", 16, F32)
    pti = sb("pti", 16, I32)
    PS = [psb("ps%d" % i, 512, F32) for i in range(6)]
    PB = [psb("pb%d" % i, 1024, BF) for i in range(2)]

    names = ["pe", "act", "dve", "pool"] + ["d_sp_%d" % i for i in range(8)] + \
        ["d_pool_%d" % i for i in range(8)] + ["d_act_%d" % i for i in range(4)] + ["cc%d" % i for i in range(4)]
    sems = {n: es.enter_context(nc.semaphore(n)) for n in names}

    def dmain(q, out, in_, writes, reads=()):
        P.dma(q, lambda e: e.dma_start(out=out, in_=in_, allow_slow_non_contiguous=True), reads=reads, writes=writes)

    def dmaout(q, out, in_, reads):
        P.dma(q, lambda e: e.dma_start(out=out, in_=in_, allow_slow_non_contiguous=True), reads=reads, is_output=True)

    def tt(eng, out, in0, in1, op, reads, writes):
        P.op(eng, lambda e: e.tensor_tensor(out=out, in0=in0, in1=in1, op=op), reads, writes)

    def ts(eng, out, in0, s1, s2, op0, op1, reads, writes):
        if op1 is None:
            P.op(eng, lambda e: e.tensor_scalar(out=out, in0=in0, scalar1=s1, scalar2=None, op0=op0), reads, writes)
        else:
            P.op(eng, lambda e: e.tensor_scalar(out=out, in0=in0, scalar1=s1, scalar2=s2, op0=op0, op1=op1), reads, writes)

    def stt(eng, out, in0, sc, in1, op0, op1, reads, writes):
        P.op(eng, lambda e: e.scalar_tensor_tensor(out=out, in0=in0, scalar=sc, in1=in1, op0=op0, op1=op1), reads, writes)

    def red(eng, out, in_, reads, writes):
        P.op(eng, lambda e: e.tensor_reduce(out=out, in_=in_, axis=AX.X, op=ALU.add), reads, writes)

    def actf(out, in_, func, reads, writes, scale=1.0, bias=0.0, accum=None):
        if accum is None:
            P.act(lambda e: e.activation(out=out, in_=in_, func=func, bias=bias, scale=scale), reads, writes)
        else:
            P.act(lambda e: e.activation(out=out, in_=in_, func=func, bias=bias, scale=scale, accum_out=accum), reads, writes)

    def cp(eng, out, in_, reads, writes):
        if eng == "act":
            P.act(lambda e: e.copy(out=out, in_=in_), reads, writes)
        else:
            P.op(eng, lambda e: e.tensor_copy(out=out, in_=in_), reads, writes)

    def mms(lst, reads, writes):
        def fn(e):
            ins = None
            for it in lst:
                (o, l, r, st, sp_) = it[:5]
                if len(it) > 5:
                    ins = e.matmul(o, l, r, start=st, stop=sp_, skip_group_check=True)
                else:
                    ins = e.matmul(o, l, r, start=st, stop=sp_)
            return ins
        P.pe(fn, reads, writes)

    def trs(lst, reads, writes):
        def fn(e):
            ins = None
            for (o, i, idn) in lst:
                ins = e.transpose(o, i, idn)
            return ins
        P.pe(fn, reads, writes)

    def rstd_from_ss(ssap, n, cols, name):
        actf(ssap, ssap, AF.Ln, [name], [name], scale=1.0 / n, bias=EPS)
        actf(ssap, ssap, AF.Exp, [name], [name], scale=-0.5)

    dmain("sp", identf.c(0, 128), identd.ap(), ["identf"])
    dmain("pool", ident_bf.c(0, 128), identd.ap(), ["ident_bf"])
    dmain("sp", triN.c(0, 128), trid.ap(), ["triN"])
    for r in range(4):
        dmain("pool", tri4.c(r * 128, r * 128 + 128), trid.ap(), ["tri4"])
    dmain("sp", i16.c(0, 256), i16d.ap(), ["i16"])
    P.pool(lambda e: e.memset(onesN.c(0, 128), -1.0 / 16), writes=["onesN"])
    P.pool(lambda e: e.memset(onesf.c(0, 128), 1.0), writes=["onesf"])
    P.pool(lambda e: e.memset(grT.c(0, 128), 1.0), writes=["grT"])
    P.pool(lambda e: e.memset(B5.c(0, 1024), 0.0), writes=["B5"])
    ts("dve", triN.c(0, 128), triN.c(0, 128), -1.0 / 16, None, ALU.mult, None, ["triN"], ["triN"])
    dmain("sp", g1T.v(0, [[8, 2], [1, 8]]), norm1.ap().rearrange("l (k p) -> p l k", p=128), ["g1T"])
    dmain("sp", g2T.v(0, [[8, 2], [1, 8]]), norm2.ap().rearrange("l (k p) -> p l k", p=128), ["g2T"])

    def bc(dr, n):
        return bass.AP(dr, 0, [[0, 128], [1, 2 * n]])
    dmain("sp", Gq.c(0, 128), bc(qn, 64), ["Gq"])
    dmain("sp", Gk.c(0, 128), bc(kn, 64), ["Gk"])
    dmain("sp", Gsub.c(0, 256), bc(subln, 128), ["Gsub"])
    dmain("sp", Ggn.c(0, 256), bc(gnorm, 128), ["Ggn"])
    dmain("sp", LQ.c(0, 512), bc(lqk, 256), ["F5a", "F5b"])
    dmain("sp", CSS.c(0, 32), cosS.ap(), ["CSS"])
    dmain("sp", CSS.c(32, 64), sinS.ap(), ["CSS"])
    dmain("sp", pti.c(0, 16), ptrep.ap(), ["pti"])
    dmain("sp", stat.c(60, 61), rgcol.ap(), ["rg"])
    dmain("pool", wa2a.v(0, [[256, 2], [1, 256]], 0, 16), wa2.ap().rearrange("l r c -> r l c"), ["wa2a"])
    dmain("pool", wa2a.v(0, [[1, 512]], 16, 1), bass.AP(ba, 0, [[0, 1], [1, 512]]), ["wa2a"])
    for l in range(2):
        li = 0.8 - 0.6 * math.exp(-0.3 * l)
        ts("dve", Gsub.c(l * 128, l * 128 + 128), Gsub.c(l * 128, l * 128 + 128), 1.0 - li, None, ALU.mult, None, ["Gsub"], ["Gsub"])
        b0 = l * 256
        tt("dve", F6.c(0, 64), LQ.c(b0, b0 + 64), LQ.c(b0 + 64, b0 + 128), ALU.mult, ["F5a", "F5b"], ["F6"])
        tt("dve", F6.c(64, 128), LQ.c(b0 + 128, b0 + 192), LQ.c(b0 + 192, b0 + 256), ALU.mult, ["F5a", "F5b", "F6"], ["F6"])
        red("dve", stat.c(40, 42), F6.v(0, [[64, 2], [1, 64]]), ["F6"], ["lamtmp"])
        actf(stat.c(40, 42), stat.c(40, 42), AF.Exp, ["lamtmp"], ["lamtmp"])
        tt("dve", stat.c(42, 43), stat.c(41, 42), stat.c(40, 41), ALU.subtract, ["lamtmp"], ["lamtmp2"])
        ts("dve", lamc.c(l, l + 1), stat.c(42, 43), -li, None, ALU.add, None, ["lamtmp2"], ["lamc"])
    cp("dve", ptf.c(0, 16), pti.c(0, 16), ["pti"], ["ptf"])
    ts("dve", ptf.c(0, 16), ptf.c(0, 16), 8.0, stat.c(60, 61), ALU.mult, ALU.add, ["ptf", "rg"], ["ptf"])
    for l_ in range(2):
        for h_ in range(4):
            ts("dve", idxKf.v(l_ * 64 + h_, [[4, 16]]), ptf.c(0, 16), float((l_ * 4 + h_) * NPHYS * 8), None, ALU.add, None, ["ptf", "idxKf"], ["idxKf"])
    cp("dve", idxK.c(0, 128), idxKf.c(0, 128), ["idxKf"], ["idxK"])

    for t in range(NTP):
        dmain("sp", X.c(t * D, t * D + D), xp.ap()[t * 128:(t + 1) * 128, :], ["X%d" % t])
    dmain("sp", X.c(ST * D, ST * D + D), xs.ap(), ["X%d" % ST])

    def bcast_mid(Tn, col, n_g, n_d):
        return Tn.v(col, [[0, n_g], [1, n_d]])

    for l in range(_NL if _STOP >= 1 else 0):
        for kc_ in range(8):
            dmain("pool", AR.c(WIN_O + kc_ * DIN, WIN_O + (kc_ + 1) * DIN), w_in.ap()[l, kc_ * 128:(kc_ + 1) * 128, :], arres(WIN_O + kc_ * DIN, DIN))
        dmain("pool", AR.v(WOUT_O, [[D, 8], [1, D]]), w_out.ap()[l].rearrange("(k p) c -> p k c", p=128), WOUT_R)
        P.pool(lambda e: e.memset(AR.c(VA_O, VA_O + NTP * 528), 1.0), writes=arres(VA_O, NTP * 528))
        P.pool(lambda e: e.memset(Sst.c(0, 512), 0.0), writes=["Sst"])
        P.pool(lambda e: e.memset(Sbf.c(0, 512), 0.0), writes=["Sbf"])

        def win(kc_, c0, c1):
            return AR.c(WIN_O + kc_ * DIN + c0, WIN_O + kc_ * DIN + c1)

        def tile_norm_T(t, gT, dstT_ap_fn, dst_res):
            xr = "X%d" % t
            xa = X.c(t * D, t * D + D)
            actf(B0.c(0, 1024), xa, AF.Square, [xr], ["B0", "ssn"], accum=stat.c(0, 1))
            rstd_from_ss(stat.c(0, 1), D, 1, "ssn")
            ts("dve", B0.c(0, 1024), xa, stat.c(0, 1), None, ALU.mult, None, [xr, "ssn"], ["B0"])
            trs([(PB[0].c(k * 128, k * 128 + 128), B0.c(k * 128, k * 128 + 128), ident_bf.c(0, 128)) for k in range(8)],
                ["B0", "ident_bf"], ["PB0"])
            tt("dve", dstT_ap_fn(), PB[0].v(0, [[128, 8], [1, 128]]), gT.v(l * 8, [[1, 8], [0, 128]]), ALU.mult,
               ["PB0", "g1T", "g2T"], dst_res)

        def qk_norm_rope(Fx, fx, Gx, cs, csn):
            tt("pool", F6.c(0, 512), Fx.c(0, 512), Fx.c(0, 512), ALU.mult, [fx], ["F6"])
            red("dve", stat.c(8, 16), F6.v(0, [[64, 8], [1, 64]]), ["F6"], ["ss8"])
            rstd_from_ss(stat.c(8, 16), 64, 8, "ss8")
            tt("dve", Fx.v(0, [[64, 8], [1, 64]]), Fx.v(0, [[64, 8], [1, 64]]), stat.v(8, [[1, 8], [0, 64]]), ALU.mult, [fx, "ss8"], [fx])
            tt("dve", Fx.v(0, [[64, 8], [1, 64]]), Fx.v(0, [[64, 8], [1, 64]]), bcast_mid(Gx, l * 64, 8, 64), ALU.mult, [fx, "Gq", "Gk"], [fx])
            tt("pool", F6.v(0, [[32, 16], [1, 32]]), Fx.v(0, [[32, 16], [1, 32]]), bcast_mid(cs, 0, 16, 32), ALU.mult, [fx, csn], ["F6"])
            tt("dve", F7.v(0, [[64, 8], [1, 32]]), Fx.v(32, [[64, 8], [1, 32]]), bcast_mid(cs, 32, 8, 32), ALU.mult, [fx, csn], ["F7a"])
            tt("dve", F7.v(32, [[64, 8], [1, 32]]), Fx.v(0, [[64, 8], [1, 32]]), bcast_mid(cs, 32, 8, 32), ALU.mult, [fx, csn], ["F7b"])
            tt("dve", Fx.v(0, [[64, 8], [1, 32]]), F6.v(0, [[64, 8], [1, 32]]), F7.v(0, [[64, 8], [1, 32]]), ALU.subtract, ["F6", "F7a", "F7b"], [fx])
            tt("dve", Fx.v(32, [[64, 8], [1, 32]]), F6.v(32, [[64, 8], [1, 32]]), F7.v(32, [[64, 8], [1, 32]]), ALU.add, ["F6", "F7a", "F7b", fx], [fx])

        def proj_block(c0, n, ps):
            mms([(ps.c(0, n), B1.c(k * 128, k * 128 + 128), win(k, c0, c0 + n), k == 0, k == 7) for k in range(8)],
                ["B1"] + WIN_R, [ps_name(ps)])

        def ps_name(ps):
            for i, p_ in enumerate(PS):
                if p_ is ps:
                    return "PS%d" % i
            return "PB"

        def tile_A(t, cs, csn, is_sample, emit_out=True):
            tile_norm_T(t, g1T, lambda: B1.v(0, [[128, 8], [1, 128]]), ["B1"])
            proj_block(0, 512, PS[0])
            cp("act", F0.c(0, 512), PS[0].c(0, 512), ["PS0"], ["F0"])
            proj_block(512, 512, PS[1])
            cp("act", F1.c(0, 512), PS[1].c(0, 512), ["PS1"], ["F1"])
            proj_block(1024, 512, PS[0])
            cp("act", F2.c(0, 512), PS[0].c(0, 512), ["PS0"], ["F2"])
            if not is_sample:
                cp("dve", AR.v(VA_O + t * 528, [[132, 4], [1, 128]]), PS[0].v(0, [[128, 4], [1, 128]]), ["PS0"], arres(VA_O + t * 528, 528))
                dmaout("sp", vp.ap()[l, t * 128:(t + 1) * 128, :], F2.c(0, 512), ["F2"])
            elif emit_out:
                dmaout("sp", vs.ap()[l], F2.c(0, 512, 0, 16), ["F2"])
            if not (is_sample and emit_out):
                proj_block(1536, 512, PS[1])
                cp("act", B8.c(0, 512), PS[1].c(0, 512), ["PS1"], ["B8"])
                proj_block(2048, 512, PS[0])
                cp("act", B2.c(0, 512), PS[0].c(0, 512), ["PS0"], ["B2"])
                if is_sample:
                    cp("dve", F7.c(0, 512, 0, 16), PS[0].c(0, 512, 0, 16), ["PS0"], ["F7a", "F7b"])
                    dmain("sp", bass.AP(vscr, 0, [[512, 16], [1, 512]]), F7.c(0, 512, 0, 16), ["vscr"], reads=["F7a", "F7b"])
                proj_block(2560, 512, PS[1])
                actf(B11.c(0, 512), PS[1].c(0, 512), AF.Silu, ["PS1"], ["B11"])
                mms([(PS[0].c(0, 128, 0, 16), win(k, 3072, 3088), B1.c(k * 128, k * 128 + 128), k == 0, k == 7) for k in range(8)],
                    ["B1"] + WIN_R, ["PS0"])
                cp("act", grT.c(0, 128, 0, 16), PS[0].c(0, 128, 0, 16), ["PS0"], ["grT"])
                mms([(PS[1].c(0, 256), grT.c(0, 128, 0, 17), wa2a.c(l * 256, l * 256 + 256, 0, 17), True, True)], ["grT", "wa2a"], ["PS1"])
                actf(F5.c(0, 256), PS[1].c(0, 256), AF.Exp, ["PS1"], ["F5a"], scale=-1.0)
                actf(F5.c(0, 256), F5.c(0, 256), AF.Ln, ["F5a"], ["F5a"], bias=1.0)
            qk_norm_rope(F0, "F0", Gq, cs, csn)
            qk_norm_rope(F1, "F1", Gk, cs, csn)
            if not is_sample:
                dmaout("sp", kp.ap()[l, t * 128:(t + 1) * 128, :], F1.c(0, 512), ["F1"])
                cp("pool", B3.c(0, 512), F1.c(0, 512), ["F1"], ["B3"])
                trs([(PB[1].c(h * 128, h * 128 + 128), B3.c(h * 128, h * 128 + 128), ident_bf.c(0, 128)) for h in range(4)],
                    ["B3", "ident_bf"], ["PB1"])
                cp("act", AR.v(KT_O + t * 128, [[S, 4], [1, 128]]), PB[1].v(0, [[128, 4], [1, 128]]), ["PB1"], arres(KT_O, 4 * S))
                cp("pool", B3.c(0, 512), F0.c(0, 512), ["F0"], ["B3"])
                trs([(PB[1].c(h * 128, h * 128 + 128), B3.c(h * 128, h * 128 + 128), ident_bf.c(0, 128)) for h in range(4)],
                    ["B3", "ident_bf"], ["PB1"])
                cp("act", B5.v(0, [[256, 4], [1, 128]], 0, 64), PB[1].v(0, [[128, 4], [1, 128]], 0, 64), ["PB1"], ["B5"])
                cp("act", B5.v(128, [[256, 4], [1, 128]], 64, 64), PB[1].v(0, [[128, 4], [1, 128]], 64, 64), ["PB1"], ["B5"])
            elif emit_out:
                dmaout("sp", ks.ap()[l], F1.c(0, 512, 0, 16), ["F1"])

        def merge_head_norm(src_ps, Gx, goff, dst_col, extra_mul, nrows=128, srcres=("PS3",)):
            r = nrows
            cp("act", F6.c(0, 512, 0, r), src_ps, list(srcres) + ["F6"], ["F6"])
            tt("pool", F7.c(0, 512, 0, r), F6.c(0, 512, 0, r), F6.c(0, 512, 0, r), ALU.mult, ["F6"], ["F7a", "F7b"])
            red("dve", stat.c(16, 20, 0, r), F7.v(0, [[128, 4], [1, 128]], 0, r), ["F7a", "F7b"], ["ss4"])
            rstd_from_ss(stat.c(16, 20, 0, r), 128, 4, "ss4")
            tt("dve", F6.v(0, [[128, 4], [1, 128]], 0, r), F6.v(0, [[128, 4], [1, 128]], 0, r), stat.v(16, [[1, 4], [0, 128]], 0, r), ALU.mult, ["F6", "ss4"], ["F6"])
            if extra_mul is None:
                tt("dve", B0.v(dst_col, [[128, 4], [1, 128]], 0, r), F6.v(0, [[128, 4], [1, 128]], 0, r), Gx.v(goff, [[0, 4], [1, 128]], 0, r), ALU.mult, ["F6", "Gsub", "Ggn"], ["B0"])
            else:
                tt("dve", F6.v(0, [[128, 4], [1, 128]], 0, r), F6.v(0, [[128, 4], [1, 128]], 0, r), Gx.v(goff, [[0, 4], [1, 128]], 0, r), ALU.mult, ["F6", "Gsub", "Ggn"], ["F6"])
                tt("dve", B0.c(dst_col, dst_col + 512, 0, r), F6.c(0, 512, 0, r), extra_mul, ALU.mult, ["F6", "B11"], ["B0"])

        def tile_C(t):
            trs([(PB[0].c(k * 128, k * 128 + 128), B0.c(k * 128, k * 128 + 128), ident_bf.c(0, 128)) for k in range(8)],
                ["B0", "ident_bf"], ["PB0"])
            cp("act", B1.c(0, 1024), PB[0].c(0, 1024), ["PB0"], ["B1"])
            for cb in range(2):
                ps = PS[cb]
                mms([(ps.c(0, 512), B1.c(k * 128, k * 128 + 128), AR.c(WOUT_O + k * D + cb * 512, WOUT_O + k * D + cb * 512 + 512), k == 0, k == 7) for k in range(8)],
                    ["B1"] + WOUT_R, ["PS%d" % cb])
                xa = X.c(t * D + cb * 512, t * D + cb * 512 + 512)
                tt("dve", xa, xa, ps.c(0, 512), ALU.add, ["X%d" % t, "PS%d" % cb], ["X%d" % t])

        tile_A(ST, CSS, "CSS", True)
        for (src, o) in ((F0, 0), (F1, 128), (F2, 256)):
            P.dma("sp", (lambda s_, o_: (lambda e: e.dma_start(out=bass.AP(ag1_in, o_, [[1536, 16], [384, 4], [1, 128]]),
                                                                in_=s_.v(0, [[128, 4], [1, 128]], 0, 16))))(src, o),
                  reads=["F0", "F1", "F2"], writes=["ag1_in"])
        SR = arfres(0, 12288)
        S0 = ARFt

        for t in range(NTP if _STOP >= 2 else 0):
            dmain("sp", CS.c(0, 32), cosP.ap()[t * 128:(t + 1) * 128, :], ["CS"])
            dmain("sp", CS.c(32, 64), sinP.ap()[t * 128:(t + 1) * 128, :], ["CS"])
            tile_A(t, CS, "CS", False)
            KTR = arres(KT_O, 4 * S)
            accs = []
            for g in range(8):
                accs.append((PS[3 + g // 3], (g % 3) * 129))
            def ptbuf(j, half):
                if j % 2 == 0:
                    return PT, "PT%d" % half
                return B9, ("B9a" if half == 0 else "B9b")

            def emit_st(j, half):
                psi = 2 if half == 0 else 1
                ps = PS[psi]
                lst = []
                for hh in range(2):
                    h = half * 2 + hh
                    lst.append((ps.c(hh * 256, hh * 256 + 256),
                                AR.c(KT_O + h * S + j * 128, KT_O + h * S + j * 128 + 128),
                                B5.c(h * 256, h * 256 + 256), True, True))
                mms(lst, KTR + ["B5"], ["PS%d" % psi])

            def emit_exp(j, half):
                psi = 2 if half == 0 else 1
                buf, bn = ptbuf(j, half)
                pt = buf.c(half * 512, half * 512 + 512)
                actf(pt, PS[psi].c(0, 512), AF.Exp, ["PS%d" % psi], [bn], scale=0.125)
                if j == t:
                    tt("dve", pt, pt, tri4.c(0, 512), ALU.mult, [bn, "tri4"], [bn])

            def emit_pv(j, half):
                buf, bn = ptbuf(j, half)
                lst = []
                for hh in range(2):
                    h = half * 2 + hh
                    for m in range(2):
                        bank, col = accs[h * 2 + m]
                        lst.append((bank.c(col, col + 129),
                                    buf.c(half * 512 + (hh * 2 + m) * 128, half * 512 + (hh * 2 + m) * 128 + 128),
                                    AR.c(VA_O + j * 528 + h * 132, VA_O + j * 528 + h * 132 + 129), (j == 0 and (h * 2 + m) in (0, 3, 6)), j == t, 1))
                mms(lst, [bn] + arres(VA_O + j * 528, 528), ["PS3", "PS4", "PS5"])

            emit_st(0, 0)
            emit_st(0, 1)
            for j in range(t + 1):
                emit_exp(j, 0)
                emit_exp(j, 1)
                if j + 1 <= t:
                    emit_st(j + 1, 0)
                    emit_st(j + 1, 1)
                emit_pv(j, 0)
                emit_pv(j, 1)
            for g in range(8):
                bank, col = accs[g]
                P.dve(lambda e, b_=bank, c_=col, g_=g: e.reciprocal(out=stat.c(24 + g_, 25 + g_), in_=b_.c(c_ + 128, c_ + 129)),
                      ["PS3", "PS4", "PS5"], ["rden"])
            for h in range(4):
                ts("dve", stat.c(24 + 2 * h + 1, 24 + 2 * h + 2), stat.c(24 + 2 * h + 1, 24 + 2 * h + 2), lamc.c(l, l + 1), None, ALU.mult, None, ["rden", "lamc"], ["rden"])
            for h in range(4):
                b1, c1 = accs[2 * h]
                b2, c2 = accs[2 * h + 1]
                ts("dve", F0.c(h * 128, h * 128 + 128), b1.c(c1, c1 + 128), stat.c(24 + 2 * h, 25 + 2 * h), None, ALU.mult, None, ["PS3", "PS4", "PS5", "rden"], ["F0"])
                stt("dve", F0.c(h * 128, h * 128 + 128), b2.c(c2, c2 + 128), stat.c(25 + 2 * h, 26 + 2 * h), F0.c(h * 128, h * 128 + 128), ALU.mult, ALU.add, ["PS3", "PS4", "PS5", "rden", "F0"], ["F0"])
            merge_head_norm(F0.c(0, 512), Gsub, l * 128, 0, None, srcres=("F0",))
            mms([(PS[2].c(0, 256), triN.c(0, 128), F5.c(0, 256), True, True),
                 (PS[2].c(256, 512), onesN.c(0, 128), F5.c(0, 256), True, True)], ["triN", "onesN", "F5a"], ["PS2"])
            cp("act", F7.c(0, 256), PS[2].c(0, 256), ["PS2"], ["F7a", "F7b"])
            tt("dve", F5.c(256, 512), PS[2].c(256, 512), F7.c(0, 256), ALU.subtract, ["PS2", "F7a", "F7b"], ["F5b"])
            actf(F5.c(256, 512), F5.c(256, 512), AF.Exp, ["F5b"], ["F5b"])
            tt("dve", B7.c(0, 256), B8.c(256, 512), F5.c(256, 512), ALU.mult, ["B8", "F5b"], ["B7"])
            mms([(PS[2].c(h * 128, h * 128 + 128, 0, 64), F5.c(h * 64, h * 64 + 64), triN.c(0, 128), True, True) for h in range(4)],
                ["F5a", "triN"], ["PS2"])
            actf(F6.c(0, 512, 0, 64), PS[2].c(0, 512, 0, 64), AF.Exp, ["PS2", "F6"], ["F6"])
            actf(F7.c(0, 512, 0, 64), PS[2].c(0, 512, 0, 64), AF.Exp, ["PS2", "F7a", "F7b"], ["F7a", "F7b"], scale=-1.0)
            trs([(PB[1].c(i * 128, i * 128 + 128, 0, 64), B8.c(i * 64, i * 64 + 64), ident_bf.c(0, 128)) for i in range(8)],
                ["B8", "ident_bf"], ["PB1"])
            stt("dve", B9.c(0, 512, 0, 64), PB[1].c(0, 512, 0, 64), 0.125, F6.c(0, 512, 0, 64), ALU.mult, ALU.mult, ["PB1", "F6"], ["B9a"])
            tt("dve", B9.c(512, 1024, 0, 64), PB[1].c(512, 1024, 0, 64), F7.c(0, 512, 0, 64), ALU.mult, ["PB1", "F7a", "F7b"], ["B9b"])
            mms([(PS[2].c(h * 128, h * 128 + 128), B9.c(512 + h * 128, 512 + h * 128 + 128, 0, 64), B9.c(h * 128, h * 128 + 128, 0, 64), True, True) for h in range(4)],
                ["B9a", "B9b"], ["PS2"])
            tt("dve", B10.c(0, 512), PS[2].c(0, 512), tri4.c(0, 512), ALU.mult, ["PS2", "tri4"], ["B10"])
            lst = []
            for h in range(4):
                lst.append((PS[3].c(h * 128, h * 128 + 128), B9.c(h * 128, h * 128 + 128, 0, 64), Sbf.c(h * 128, h * 128 + 128, 0, 64), True, False))
                lst.append((PS[3].c(h * 128, h * 128 + 128), B10.c(h * 128, h * 128 + 128), B2.c(h * 128, h * 128 + 128), False, True))
            mms(lst, ["B9a", "Sbf", "B10", "B2"], ["PS3"])
            mms([(PS[4].c(h * 128, h * 128 + 128, 0, 64), B7.c(h * 64, h * 64 + 64), B2.c(h * 128, h * 128 + 128), True, True) for h in range(4)],
                ["B7", "B2"], ["PS4"])
            for h in range(4):
                stt("dve", Sst.c(h * 128, h * 128 + 128, 0, 64), Sst.c(h * 128, h * 128 + 128, 0, 64), F6.c(h * 128 + 127, h * 128 + 128, 0, 64),
                    PS[4].c(h * 128, h * 128 + 128, 0, 64), ALU.mult, ALU.add, ["Sst", "F6", "PS4"], ["Sst"])
            cp("pool", Sbf.c(0, 512, 0, 64), Sst.c(0, 512, 0, 64), ["Sst"], ["Sbf"])
            merge_head_norm(PS[3].c(0, 512), Ggn, l * 128, 512, B11.c(0, 512))
            tile_C(t)
        dmaout("sp", gp.ap()[l].rearrange("h k v -> k h v"), Sst.v(0, [[128, 4], [1, 128]], 0, 64), ["Sst"])

        if _STOP < 3:
            continue
        tile_A_redo = True
        tile_A(ST, CSS, "CSS", True, emit_out=False)
        S0c, VBc = 0, 4096
        S0r = arfres(S0c, 4096)
        VBr = arfres(VBc, 4096)
        dmain("sp", ARFt.v(S0c, [[256, 16], [128, 2], [1, 128]]),
              bass.AP(sg, l * 16 * 32768, [[128, 128], [32768, 16], [16384, 2], [1, 128]]), S0r)
        for h2 in range(2):
            dmain("sp", ARFt.v(VBc, [[256, 16], [128, 2], [1, 128]], h2 * 64, 64),
                  bass.AP(vscr, h2 * 128, [[0, 64], [512, 16], [256, 2], [1, 128]]), VBr, reads=["vscr"])
        actf(F6.c(0, 256, 0, 16), F5.c(0, 256, 0, 16), AF.Exp, ["F5a", "F6"], ["F6"], scale=-1.0 / 16)
        cp("dve", F6.c(256, 512, 0, 16), B8.c(256, 512, 0, 16), ["B8", "F6"], ["F6"])
        ts("dve", F5.c(256, 512, 0, 16), B8.c(0, 256, 0, 16), 0.125, None, ALU.mult, None, ["B8"], ["F5b"])
        lst = []
        for qi, (src, c0) in enumerate(((F6, 0), (F6, 256), (F5, 256))):
            for hp in range(2):
                lst.append((PS[2].c((qi * 2 + hp) * 16, (qi * 2 + hp) * 16 + 16), src.c(c0 + hp * 128, c0 + hp * 128 + 128, 0, 16), identf.c(0, 16, 0, 16)))
        trs(lst, ["F6", "F5b", "identf"], ["PS2"])
        cp("act", F2.c(0, 96), PS[2].c(0, 96), ["PS2", "F2"], ["F2"])
        def s_view(c0):
            return ARFt.v(c0, [[128, 2], [256, 16], [1, 128]])

        def col_view(qi):
            return F2.v(qi * 32, [[16, 2], [1, 16], [0, 128]])
        tt("dve", s_view(S0c), s_view(S0c), col_view(0), ALU.mult, S0r + ["F2"], S0r)
        tt("pool", s_view(VBc), s_view(VBc), col_view(1), ALU.mult, VBr + ["F2"], VBr)
        tt("dve", ARFt.c(S0c, S0c + 4096), ARFt.c(S0c, S0c + 4096), ARFt.c(VBc, VBc + 4096), ALU.add, S0r + VBr, S0r)
        dmaout("sp", bass.AP(gs, l * 16 * 32768, [[128, 128], [32768, 16], [16384, 2], [1, 128]]),
               ARFt.v(S0c, [[256, 16], [128, 2], [1, 128]]), S0r)
        SBc = 16384 + 0
        SBr = arres(SBc, 4096)
        cp("pool", AR.c(SBc, SBc + 4096), ARFt.c(S0c, S0c + 4096), S0r, SBr)
        QDc = SBc + 4096
        QDr = arres(QDc, 512)
        tt("dve", AR.v(QDc, [[256, 2], [16, 16], [1, 16]]), F2.v(64, [[16, 2], [1, 16], [0, 16]]), i16.v(0, [[0, 2], [16, 16], [1, 16]]), ALU.mult,
           ["F2", "i16"], QDr)
        lst = []
        for h in range(4):
            hp, h2 = h // 2, h % 2
            for b in range(16):
                lst.append((PS[3 + h2].c(h * 128, h * 128 + 128, 0, 16),
                            AR.c(QDc + hp * 256 + b * 16, QDc + hp * 256 + b * 16 + 16, h2 * 64, 64),
                            AR.c(SBc + b * 256 + hp * 128, SBc + b * 256 + hp * 128 + 128, h2 * 64, 64), b == 0, b == 15))
        mms([x for x in lst if x[0] is not None], QDr + SBr, ["PS3", "PS4"])
        for h in range(4):
            cp("act", F0.c(h * 128, h * 128 + 128, 0, 16), PS[3 + h % 2].c(h * 128, h * 128 + 128, 0, 16), ["PS3", "PS4", "F0"], ["F0"])
        merge_head_norm(F0.c(0, 512, 0, 16), Ggn, l * 128, 512, B11.c(0, 512, 0, 16), nrows=16, srcres=("F0",))

        if _STOP < 4:
            continue
        QKV = F0
        dmain("sp", F0.c(0, 384, 0, 64), bass.AP(ag1_in, 0, [[384, 64], [1, 384]]), ["F0"], reads=["ag1_in"])
        KtO = [0, 2048]
        VtO = [4096, 6144]
        PRO = 8192
        PACC = F1
        QBt = [F7, F6]
        for i in range(64):
            bsel = i % 2
            ko, vo = KtO[bsel], VtO[bsel]
            kr, vr = arfres(ko, 2048), arfres(vo, 2048)
            qb = QBt[bsel]
            qbn = "F7a" if bsel == 0 else "F6"
            qbw = ["F7a", "F7b"] if bsel == 0 else ["F6"]
            dmain("sp", qb.c(0, 128), bass.AP(ag1_in, i * 384, [[0, 128], [1, 128]]), qbw, reads=["ag1_in"])
            P.dma("pool", (lambda i_, ko_: (lambda e: e.indirect_dma_start(
                out=ARFt.c(ko_, ko_ + 2048), out_offset=None, in_=kc.ap(),
                in_offset=bass.IndirectOffsetOnAxis(ap=idxK.c(i_, i_ + 1), axis=0))))(l * 64 + i, ko),
                reads=["idxK"], writes=kr)
            P.dma("pool", (lambda i_, vo_: (lambda e: e.indirect_dma_start(
                out=ARFt.c(vo_, vo_ + 2048), out_offset=None, in_=vc.ap(),
                in_offset=bass.IndirectOffsetOnAxis(ap=idxK.c(i_, i_ + 1), axis=0))))(l * 64 + i, vo),
                reads=["idxK"], writes=vr)
            pr = arfres(PRO, 2048)
            tt("dve", ARFt.v(PRO, [[128, 16], [1, 128]]), ARFt.v(ko, [[128, 16], [1, 128]]), qb.v(0, [[0, 16], [1, 128]]), ALU.mult,
               kr + qbw, pr)
            so = bsel * 32
            scn, pbn = "sc%d" % bsel, "pbuf%d" % bsel
            red("dve", sc32.c(so, so + 32), ARFt.v(PRO, [[64, 32], [1, 64]]), pr, [scn])
            actf(pbuf.c(so, so + 32), sc32.c(so, so + 32), AF.Exp, [scn], [pbn], scale=0.125)
            red("dve", PACC.c(i * 2, i * 2 + 2), pbuf.v(so, [[1, 2], [2, 16]]), [pbn], ["F1"])
            mms([(PS[4].c(i * 2, i * 2 + 2), ARFt.c(vo + r * 128, vo + r * 128 + 128), pbuf.c(so + r * 2, so + r * 2 + 2), r == 0, r == 15) for r in range(16)],
                vr + [pbn], ["PS4"])
        cp("act", F2.c(0, 128), PS[4].c(0, 128), ["PS4", "F2"], ["F2"])
        trs([(PS[2].c(m * 128, m * 128 + 128, 0, 64), F2.v(m, [[2, 64]]), identf.c(0, 128)) for m in range(2)], ["F2", "identf"], ["PS2"])
        mms([(PS[2].c(256 + m, 257 + m, 0, 64), PACC.v(m, [[2, 64]]), onesf.c(0, 1), True, True) for m in range(2)], ["F1", "onesf"], ["PS2"])
        tt("dve", F5.c(0, 128, 0, 64), F0.c(0, 128, 0, 64), F0.c(128, 256, 0, 64), ALU.mult, ["F0"], ["F5a"])
        red("dve", stat.c(32, 34, 0, 64), F5.v(0, [[64, 2], [1, 64]], 0, 64), ["F5a"], ["snew"])
        actf(stat.c(32, 34, 0, 64), stat.c(32, 34, 0, 64), AF.Exp, ["snew"], ["snew"], scale=0.125)
        tt("dve", stat.c(34, 36, 0, 64), PS[2].c(256, 258, 0, 64), stat.c(32, 34, 0, 64), ALU.add, ["PS2", "snew"], ["den"])
        P.dve(lambda e: e.reciprocal(out=stat.c(34, 36, 0, 64), in_=stat.c(34, 36, 0, 64)), ["den"], ["den"])
        ts("dve", stat.c(35, 36, 0, 64), stat.c(35, 36, 0, 64), lamc.c(l, l + 1, 0, 64), None, ALU.mult, None, ["den", "lamc"], ["den"])
        for m in range(2):
            stt("dve", F5.c(256 + m * 128, 384 + m * 128, 0, 64), F0.c(256, 384, 0, 64), stat.c(32 + m, 33 + m, 0, 64), PS[2].c(m * 128, m * 128 + 128, 0, 64),
                ALU.mult, ALU.add, ["F0", "snew", "PS2", "F5b"], ["F5b"])
        ts("dve", F5.c(0, 128, 0, 64), F5.c(256, 384, 0, 64), stat.c(34, 35, 0, 64), None, ALU.mult, None, ["F5b", "den", "F5a"], ["F5a"])
        stt("dve", F5.c(0, 128, 0, 64), F5.c(384, 512, 0, 64), stat.c(35, 36, 0, 64), F5.c(0, 128, 0, 64), ALU.mult, ALU.add, ["F5b", "den", "F5a"], ["F5a"])
        dmain("sp", ag2_in.ap(), F5.c(0, 128, 0, 64), ["ag2_in"], reads=["F5a"])
        dmain("sp", F0.c(0, 512, 0, 16), bass.AP(ag2_in, 0, [[512, 16], [1, 512]]), ["F0"], reads=["ag2_in"])
        merge_head_norm(F0.c(0, 512, 0, 16), Gsub, l * 128, 0, None, nrows=16, srcres=("F0",))
        tile_C(ST)

        if _STOP < 5:
            continue
        dmain("pool", AR.v(WD_O, [[D, 22], [1, D]]), w_down.ap()[l].rearrange("(j p) c -> p j c", p=128), arres(WD_O, 22528))
        WDR = arres(WD_O, 22528)
        groups = []
        t0 = 0
        per = -(-NT // 4)
        while t0 < NT:
            groups.append(list(range(t0, min(NT, t0 + per))))
            t0 += per
        wq = 0
        blk = 0
        for grp in groups:
            ntok = len(grp) * 128
            N2R = arres(N2T_O, 8 * GT)
            for gi, t in enumerate(grp):
                tile_norm_T(t, g2T, (lambda gi_: (lambda: AR.v(N2T_O + gi_ * 128, [[GT, 8], [1, 128]])))(gi), N2R)
            HTR = arres(HT_O, 22 * GT)
            for j in range(22):
                wb = WGU_O + (wq % 3) * 2048
                wr = arres(wb, 2048)
                wq += 1
                for hf in range(2):
                    dmain("pool", AR.v(wb + hf * 128, [[256, 8], [1, 128]]),
                          bass.AP(w_gu, l * D * 2 * DFF + hf * DFF + j * 128, [[2 * DFF, 128], [128 * 2 * DFF, 8], [1, 128]]), wr)
                c0 = 0
                while c0 < ntok:
                    n = min(512, ntok - c0)
                    blk += 1
                    ia, ib, Fs, fsn = (0, 1, F0, "F0") if blk % 2 == 0 else (4, 5, F1, "F1")
                    mms([(PS[ia].c(0, n), AR.c(wb + k * 256, wb + k * 256 + 128), AR.c(N2T_O + k * GT + c0, N2T_O + k * GT + c0 + n), k == 0, k == 7) for k in range(8)],
                        wr + N2R, ["PS%d" % ia])
                    mms([(PS[ib].c(0, n), AR.c(wb + k * 256 + 128, wb + k * 256 + 256), AR.c(N2T_O + k * GT + c0, N2T_O + k * GT + c0 + n), k == 0, k == 7) for k in range(8)],
                        wr + N2R, ["PS%d" % ib])
                    actf(Fs.c(0, n), PS[ia].c(0, n), AF.Silu, ["PS%d" % ia, fsn], [fsn])
                    tt("dve", AR.c(HT_O + j * GT + c0, HT_O + j * GT + c0 + n), Fs.c(0, n), PS[ib].c(0, n), ALU.mult, [fsn, "PS%d" % ib], arres(HT_O + j * GT + c0, n))
                    c0 += n
            for gi, t in enumerate(grp):
                for cb in range(2):
                    ps = PS[2 + cb]
                    mms([(ps.c(0, 512), AR.c(HT_O + j * GT + gi * 128, HT_O + j * GT + gi * 128 + 128), AR.c(WD_O + j * D + cb * 512, WD_O + j * D + cb * 512 + 512), j == 0, j == 21) for j in range(22)],
                        HTR + WDR, ["PS%d" % (2 + cb)])
                    xa = X.c(t * D + cb * 512, t * D + cb * 512 + 512)
                    tt("dve", xa, xa, ps.c(0, 512), ALU.add, ["X%d" % t, "PS%d" % (2 + cb)], ["X%d" % t])
                if l == _NL - 1:
                    if t < NTP:
                        dmaout("sp", yp.ap()[t * 128:(t + 1) * 128, :], X.c(t * D, t * D + D), ["X%d" % t])
                    else:
                        dmaout("sp", ys.ap(), X.c(t * D, t * D + D, 0, 16), ["X%d" % t])

    if _STOP < 5:
        for t in range(NTP):
            dmaout("sp", yp.ap()[t * 128:(t + 1) * 128, :], X.c(t * D, t * D + D), ["X%d" % t])
        dmaout("sp", ys.ap(), X.c(ST * D, ST * D + D, 0, 16), ["X%d" % ST])
    P.finalize(sems)
    with nc.Block() as block:
        @block.tensor
        def _(e):
            P.emit("pe", e)

        @block.scalar
        def _(e):
            P.emit("act", e)

        @block.vector
        def _(e):
            P.emit("dve", e)

        @block.gpsimd
        def _(e):
            P.emit("pool", e)

        @block.sync
        def _(e):
            P.emit("sp", e)
    es.close()
    return nc


_DEBUG_HOOK = None
_USE_CC = True
_STOP = 99
_OPLIMIT = 10 ** 9
_DBG_N = 99
_NL = 2
def _consts(S, past):
    half = 32
    freqs = (10000.0 ** (-np.arange(half, dtype=np.float32) / half)).astype(np.float32)
    pos = np.arange(S, dtype=np.float32)
    ang = pos[:, None] * freqs[None, :]
    angs = np.full((128, 1), float(past), np.float32) * freqs[None, :]
    ident = np.eye(128, dtype=np.float32)
    tri = np.triu(np.ones((128, 128), np.float32))
    i16 = np.tile(np.eye(16, dtype=np.float32).reshape(1, 256), (128, 1))
    rg = (np.arange(128) % 8).astype(np.float32).reshape(128, 1)
    return dict(cosP=np.cos(ang).astype(np.float32), sinP=np.sin(ang).astype(np.float32),
                cosS=np.cos(angs).astype(np.float32), sinS=np.sin(angs).astype(np.float32),
                identd=ident, trid=tri, i16d=i16, rgcol=rg)


def kernel(x_prompt, x_sample, cache_k, cache_v, state_gla, page_table, norm1, w_in, q_norm, k_norm,
           lambda_qk, subln, w_a2, b_a, gla_norm, w_out, norm2, w_gu, w_down):
    f = lambda a: np.ascontiguousarray(np.asarray(a, dtype=np.float32))
    x_prompt, x_sample = f(x_prompt), f(x_sample)
    cache_k, cache_v, state_gla = np.asarray(cache_k), np.asarray(cache_v), f(state_gla)
    page_table = np.asarray(page_table).astype(np.int32)
    S = x_prompt.shape[1]
    NPHYS = cache_k.shape[1]
    n_cores = 8
    nc = bass.Bass("TRN2", target_bir_lowering=False)
    build(nc, S, use_cc=_USE_CC, NPHYS=NPHYS)
    cst = _consts(S, 2048)
    shared = dict(w_in=f(w_in), w_out=f(w_out), w_gu=f(w_gu), w_down=f(w_down), norm1=f(norm1), norm2=f(norm2),
                  qn=f(q_norm), kn=f(k_norm), lqk=f(lambda_qk).reshape(2, 256), subln=f(subln), wa2=f(w_a2),
                  ba=f(b_a), gnorm=f(gla_norm))
    shared.update(cst)
    kh = np.ascontiguousarray(cache_k.reshape(2, NPHYS, 128, 4, 128).transpose(0, 3, 1, 2, 4)).reshape(8 * NPHYS * 8, 2048)
    vh = np.ascontiguousarray(cache_v.transpose(0, 3, 1, 2, 4)).reshape(8 * NPHYS * 8, 2048)
    in_maps = []
    for c in range(n_cores):
        half, h = c // 4, c % 4
        m = dict(shared)
        m["xp"] = x_prompt[c]
        xs = np.zeros((128, D), np.float32)
        xs[:16] = x_sample[16 * c:16 * c + 16, 0]
        m["xs"] = xs
        m["kc"] = kh
        m["vc"] = vh
        m["sg"] = np.ascontiguousarray(state_gla[:, 16 * c:16 * c + 16])
        pt = page_table[16 * c:16 * c + 16]
        m["ptrep"] = np.ascontiguousarray(np.repeat(pt.T, 8, axis=0)).astype(np.int32)
        in_maps.append(m)
    if _DEBUG_HOOK is not None:
        return _DEBUG_HOOK(nc, in_maps)
    res = run_bass_kernel_spmd(nc, in_maps, core_ids=list(range(n_cores))).results
    yp = np.stack([r["yp"] for r in res])
    ys = np.concatenate([r["ys"] for r in res])[:, None, :]
    kp = np.stack([r["kp"] for r in res], axis=1).reshape(2, 8, S, 4, 2, 64)
    vp = np.stack([r["vp"] for r in res], axis=1).reshape(2, 8, S, 4, 128)
    gp = np.stack([r["gp"] for r in res], axis=1)
    ks = np.concatenate([r["ks"] for r in res], axis=1).reshape(2, 128, 1, 4, 2, 64)
    vs = np.concatenate([r["vs"] for r in res], axis=1).reshape(2, 128, 1, 4, 128)
    gs = np.concatenate([r["gs"] for r in res], axis=1)
    return (yp.astype(np.float32), ys.astype(np.float32), kp.astype(np.float32), vp.astype(np.float32),
            gp.astype(np.float32), ks.astype(np.float32), vs.astype(np.float32), gs.astype(np.float32))
```
